# Optimizing a Trainium2 kernel written in Bass

```python
import jax, jax.numpy as jnp
from jax import lax
import numpy as np

D_MODEL = 2048
BATCH = 1
SEQ = 8192
DEPTH = 1
DEC_BATCH = 16
DEC_SEQ = 16
PAST_LEN = 4096

CHUNK = 64
QBLOCK = 128
HEAD_DIM = 128
N_HEADS_SB = 8
N_HEADS_FOX = 8
W_SB = N_HEADS_SB * HEAD_DIM
W_FOX = N_HEADS_FOX * HEAD_DIM
RMS_EPS = 1e-6
FORGET_BIAS_INIT = 3.0
SPLITS = [W_SB, W_SB, W_SB, W_SB, W_FOX, W_FOX, W_FOX, W_FOX, N_HEADS_FOX, D_MODEL, D_MODEL]
D_IN = 4 * W_SB + 4 * W_FOX + N_HEADS_FOX + 2 * D_MODEL

kernel_name = "stickbreak_fox_gated_hybrid_step"


def _split_points():
    pts, acc = [], 0
    for w in SPLITS[:-1]:
        acc += w
        pts.append(acc)
    return pts


def rms_norm(x, w):
    xf = x.astype(jnp.float32)
    y = xf * lax.rsqrt(jnp.mean(xf * xf, axis=-1, keepdims=True) + RMS_EPS)
    return (y * w.astype(jnp.float32)).astype(x.dtype)


def _sweep(block_fn, q_pos, *q_arrays):
    tq = q_pos.shape[0]
    qb = tq if tq <= QBLOCK else QBLOCK
    nb = tq // qb

    def to_blocks(a):
        a = a.reshape(a.shape[0], nb, qb, *a.shape[2:])
        return jnp.moveaxis(a, 1, 0)

    blocks = (q_pos.reshape(nb, qb),) + tuple(to_blocks(a) for a in q_arrays)
    out = lax.map(lambda args: block_fn(*args), blocks)
    out = jnp.moveaxis(out, 0, 1)
    return out.reshape(out.shape[0], tq, *out.shape[3:])


def stick_breaking_block(qpos, q, k, v, kpos):
    z = jnp.einsum("bqhd,bkhd->bhqk", q.astype(jnp.float32), k.astype(jnp.float32)) * (HEAD_DIM ** -0.5)
    mask = (kpos[None, :] < qpos[:, None])[None, None]
    log_one_minus = jnp.where(mask, jax.nn.log_sigmoid(-z), 0.0)
    later = lax.cumsum(log_one_minus, axis=3, reverse=True) - log_one_minus
    w = jnp.where(mask, jnp.exp(jax.nn.log_sigmoid(z) + later), 0.0)
    return jnp.einsum("bhqk,bkhd->bqhd", w, v.astype(jnp.float32)).astype(v.dtype)


def forgetting_block(qpos, q, f_q, k, v, f_k, kpos):
    z = jnp.einsum("bqhd,bkhd->bhqk", q.astype(jnp.float32), k.astype(jnp.float32)) * (HEAD_DIM ** -0.5)
    decay = jnp.transpose(f_q, (0, 2, 1))[..., :, None] - jnp.transpose(f_k, (0, 2, 1))[..., None, :]
    mask = (kpos[None, :] <= qpos[:, None])[None, None]
    p = jax.nn.softmax(jnp.where(mask, z + decay, -jnp.inf), axis=-1)
    return jnp.einsum("bhqk,bkhd->bqhd", p, v.astype(jnp.float32)).astype(v.dtype)


def hybrid_layer(x, past_sb_k, past_sb_v, past_fox_k, past_fox_v, past_fox_logf,
                 norm_w, w_in, b_forget, q_norm_w, k_norm_w, w_branch_sb, w_branch_fox, w_out):
    bsz, t_new, _ = x.shape
    p_len = 0 if past_sb_k is None else past_sb_k.shape[1]
    h = rms_norm(x, norm_w)
    proj = jnp.einsum("btd,de->bte", h, w_in)
    (q_sb, k_sb, v_sb, g_sb, q_fx, k_fx, v_fx, g_fx,
     f_logit, m_sb, m_fx) = jnp.split(proj, _split_points(), axis=-1)

    heads_sb = lambda a: a.reshape(bsz, t_new, N_HEADS_SB, HEAD_DIM)
    heads_fx = lambda a: a.reshape(bsz, t_new, N_HEADS_FOX, HEAD_DIM)
    q_sb, k_sb, v_sb = heads_sb(q_sb), heads_sb(k_sb), heads_sb(v_sb)
    q_fx = rms_norm(heads_fx(q_fx), q_norm_w)
    k_fx = rms_norm(heads_fx(k_fx), k_norm_w)
    v_fx = heads_fx(v_fx)
    log_f = jax.nn.log_sigmoid(f_logit.astype(jnp.float32) + b_forget.astype(jnp.float32))

    q_pos = p_len + jnp.arange(t_new)
    k_pos = jnp.arange(p_len + t_new)
    if past_sb_k is None:
        ks_all, vs_all, kf_all, vf_all, lf_all = k_sb, v_sb, k_fx, v_fx, log_f
    else:
        ks_all = jnp.concatenate([past_sb_k.astype(k_sb.dtype), k_sb], axis=1)
        vs_all = jnp.concatenate([past_sb_v.astype(v_sb.dtype), v_sb], axis=1)
        kf_all = jnp.concatenate([past_fox_k.astype(k_fx.dtype), k_fx], axis=1)
        vf_all = jnp.concatenate([past_fox_v.astype(v_fx.dtype), v_fx], axis=1)
        lf_all = jnp.concatenate([past_fox_logf.astype(jnp.float32), log_f], axis=1)
    f_cum = lax.cumsum(lf_all, axis=1)
    f_q = f_cum[:, p_len:]

    o_sb = _sweep(lambda qp, qq: stick_breaking_block(qp, qq, ks_all, vs_all, k_pos), q_pos, q_sb)
    o_fx = _sweep(lambda qp, qq, fq: forgetting_block(qp, qq, fq, kf_all, vf_all, f_cum, k_pos),
                  q_pos, q_fx, f_q)

    u_sb = jnp.einsum("btc,cd->btd", o_sb.reshape(bsz, t_new, W_SB) * jax.nn.silu(g_sb), w_branch_sb)
    u_fx = jnp.einsum("btc,cd->btd", o_fx.reshape(bsz, t_new, W_FOX) * jax.nn.silu(g_fx), w_branch_fox)
    merged = jax.nn.sigmoid(m_sb) * u_sb + jax.nn.sigmoid(m_fx) * u_fx
    y = x + jnp.einsum("btd,de->bte", merged, w_out)
    return y, k_sb, v_sb, k_fx, v_fx, log_f


def setup_inputs(seed: int = 0) -> dict:
    key = jax.random.key(seed)
    ks = jax.random.split(key, 16)
    f32 = jnp.float32
    n = lambda k, shape, scale: jax.random.normal(k, shape, f32) * scale
    return {
        "x_prompt": n(ks[0], (BATCH, SEQ, D_MODEL), 1.0),
        "x_sample": n(ks[1], (DEC_BATCH, DEC_SEQ, D_MODEL), 1.0),
        "cache_sb_k": n(ks[2], (DEPTH, DEC_BATCH, PAST_LEN, N_HEADS_SB, HEAD_DIM), 1.0),
        "cache_sb_v": n(ks[3], (DEPTH, DEC_BATCH, PAST_LEN, N_HEADS_SB, HEAD_DIM), 1.0),
        "cache_fox_k": n(ks[4], (DEPTH, DEC_BATCH, PAST_LEN, N_HEADS_FOX, HEAD_DIM), 1.0),
        "cache_fox_v": n(ks[5], (DEPTH, DEC_BATCH, PAST_LEN, N_HEADS_FOX, HEAD_DIM), 1.0),
        "cache_fox_logf": jax.nn.log_sigmoid(FORGET_BIAS_INIT + n(ks[6], (DEPTH, DEC_BATCH, PAST_LEN, N_HEADS_FOX), 1.0)),
        "norm_w": 1.0 + n(ks[7], (DEPTH, D_MODEL), 0.02),
        "w_in": n(ks[8], (DEPTH, D_MODEL, D_IN), D_MODEL ** -0.5),
        "b_forget": FORGET_BIAS_INIT + n(ks[9], (DEPTH, N_HEADS_FOX), 0.1),
        "q_norm_w": 1.0 + n(ks[10], (DEPTH, HEAD_DIM), 0.02),
        "k_norm_w": 1.0 + n(ks[11], (DEPTH, HEAD_DIM), 0.02),
        "w_branch_sb": n(ks[12], (DEPTH, W_SB, D_MODEL), W_SB ** -0.5),
        "w_branch_fox": n(ks[13], (DEPTH, W_FOX, D_MODEL), W_FOX ** -0.5),
        "w_out": n(ks[14], (DEPTH, D_MODEL, D_MODEL), D_MODEL ** -0.5),
    }


def reference(x_prompt, x_sample, cache_sb_k, cache_sb_v, cache_fox_k, cache_fox_v, cache_fox_logf,
              norm_w, w_in, b_forget, q_norm_w, k_norm_w, w_branch_sb, w_branch_fox, w_out):
    y_prompt, y_sample = x_prompt, x_sample
    p_sbk, p_sbv, p_fxk, p_fxv, p_lf = [], [], [], [], []
    s_sbk, s_sbv, s_fxk, s_fxv, s_lf = [], [], [], [], []
    for l in range(DEPTH):
        wl = (norm_w[l], w_in[l], b_forget[l], q_norm_w[l], k_norm_w[l],
              w_branch_sb[l], w_branch_fox[l], w_out[l])
        y_prompt, a, b, c, d, e = hybrid_layer(y_prompt, None, None, None, None, None, *wl)
        p_sbk.append(a); p_sbv.append(b); p_fxk.append(c); p_fxv.append(d); p_lf.append(e)
        y_sample, a, b, c, d, e = hybrid_layer(y_sample, cache_sb_k[l], cache_sb_v[l], cache_fox_k[l],
                                               cache_fox_v[l], cache_fox_logf[l], *wl)
        s_sbk.append(a); s_sbv.append(b); s_fxk.append(c); s_fxv.append(d); s_lf.append(e)
    return (y_prompt, y_sample,
            jnp.stack(p_sbk), jnp.stack(p_sbv), jnp.stack(p_fxk), jnp.stack(p_fxv), jnp.stack(p_lf),
            jnp.stack(s_sbk), jnp.stack(s_sbv), jnp.stack(s_fxk), jnp.stack(s_fxv), jnp.stack(s_lf))
```

```python
import contextlib
import numpy as np
import concourse.bass as bass
import concourse.mybir as mybir
from concourse.bass_utils import run_bass_kernel_spmd

F32 = mybir.dt.float32
BF16 = mybir.dt.bfloat16
U32 = mybir.dt.uint32
AF = mybir.ActivationFunctionType
ALU = mybir.AluOpType

NCORES = 8
D = 2048
DC = 16
HD = 128
B_S = 16
T_S = 16
NSAMP = B_S * T_S
EPS = 1e-6


class Prog:
    ENGS = ("pe", "act", "dve", "pool", "sp")

    def __init__(self, nc, name="p"):
        self.nc = nc
        self.name = name
        self.ops = []

    def add(self, eng, fn, reads=(), writes=(), semkey=None, sem_inc=16):
        assert eng in self.ENGS
        self.ops.append(
            dict(eng=eng, fn=fn, reads=list(reads), writes=list(writes), semkey=semkey,
                 sem_inc=sem_inc)
        )

    def pe(self, fn, reads=(), writes=()):
        self.add("pe", fn, reads, writes)

    def act(self, fn, reads=(), writes=()):
        self.add("act", fn, reads, writes)

    def dve(self, fn, reads=(), writes=()):
        self.add("dve", fn, reads, writes)

    def pool(self, fn, reads=(), writes=()):
        self.add("pool", fn, reads, writes)

    def dma(self, q, out, in_, reads=(), writes=(), semkey=None, **kw):
        assert semkey is not None
        self.add(
            q,
            lambda e, out=out, in_=in_, kw=kw: e.dma_start(out=out, in_=in_, **kw),
            reads, writes, semkey,
        )

    def emit(self):
        nc = self.nc
        import os as _os
        mx = _os.environ.get("MAXOPS_" + self.name)
        if mx is not None:
            self.ops = self.ops[:int(mx)]
        ops = self.ops
        print("emit", self.name, "nops", len(ops), flush=True)
        if not ops:
            return
        last_writer = {}
        readers = {}
        for i, op in enumerate(ops):
            deps = set()
            for k in op["reads"]:
                if k in last_writer:
                    deps.add(last_writer[k])
                if _is_psum_key(k):
                    for r_ in readers.get(k, ()):
                        if ops[r_]["eng"] != op["eng"]:
                            deps.add(r_)
            for k in op["writes"]:
                if k in last_writer:
                    deps.add(last_writer[k])
                deps.update(readers.get(k, ()))
            deps.discard(i)
            if op["eng"] == "pe" and op["semkey"] is None:
                deps = {d for d in deps
                        if not (ops[d]["eng"] == "pe" and ops[d]["semkey"] is None)}
            op["deps"] = deps
            for k in op["reads"]:
                readers.setdefault(k, []).append(i)
            for k in op["writes"]:
                last_writer[k] = i
                readers[k] = []
        has_dep = [False] * len(ops)
        for op in ops:
            for d in op["deps"]:
                has_dep[d] = True
        eng_cnt = {e: 0 for e in self.ENGS}
        dma_cnt = {}
        for i, op in enumerate(ops):
            if op["semkey"] is not None:
                k = op["semkey"]
                dma_cnt[k] = dma_cnt.get(k, 0) + op["sem_inc"]
                op["sig"] = (("dma", k), dma_cnt[k])
            elif has_dep[i]:
                eng_cnt[op["eng"]] += 1
                op["sig"] = (("eng", op["eng"]), eng_cnt[op["eng"]])
            else:
                op["sig"] = None
        semnames = [("eng", e) for e in self.ENGS] + [("dma", k) for k in dma_cnt]
        with contextlib.ExitStack() as es:
            sems = {}
            for j, sn in enumerate(semnames):
                sems[sn] = es.enter_context(nc.semaphore(f"{self.name}_s{j}"))
            block = es.enter_context(nc.Block())
            per_eng = {e: [op for op in ops if op["eng"] == e] for e in self.ENGS}
            dma_issuer = {}
            for op in ops:
                if op["semkey"] is not None:
                    dma_issuer[op["semkey"]] = op["eng"]

            def run_engine(engname, e):
                sat = {}
                for op in per_eng[engname]:
                    need = {}
                    for d in op["deps"]:
                        sn, c = ops[d]["sig"]
                        if c > need.get(sn, 0):
                            need[sn] = c
                    for sn, c in need.items():
                        if sat.get(sn, 0) < c:
                            e.wait_ge(sems[sn], c)
                            sat[sn] = c
                    ins = op["fn"](e)
                    if op["sig"] is not None:
                        sn, c = op["sig"]
                        if sn[0] == "dma":
                            if op["sem_inc"] == 1:
                                ins.then_inc(sems[sn])
                            else:
                                ins.then_inc(sems[sn], op["sem_inc"])
                        else:
                            ins.then_inc(sems[sn], 1)
                for k, tot in dma_cnt.items():
                    if dma_issuer[k] == engname:
                        sn = ("dma", k)
                        if sat.get(sn, 0) < tot:
                            e.wait_ge(sems[sn], tot)

            @block.tensor
            def _(e):
                run_engine("pe", e)

            @block.scalar
            def _(e):
                run_engine("act", e)

            @block.vector
            def _(e):
                run_engine("dve", e)

            @block.gpsimd
            def _(e):
                run_engine("pool", e)

            @block.sync
            def _(e):
                run_engine("sp", e)


_PSUM_NAMES = {"misc_ps", "ss_ps", "psf", "psv", "ssq_ps", "z_ps", "c_ps", "o_ps", "den_ps", "pA", "pU", "pM", "pY"}


def _is_psum_key(k):
    n = k[0] if isinstance(k, tuple) else k
    return n in _PSUM_NAMES


def _bc_mid(ap, n):
    p, f = ap.shape
    return ap.unsqueeze(1).broadcast_to([p, n, f])


def build_A(nc, T_P, PAST, og_dst):
    NTOK = T_P + NSAMP
    NT = NTOK // 256
    NB = T_P // 128
    NPB = PAST // 128
    NQT = T_P // 512
    SC = float(HD) ** -0.5

    def din(name, shape, dt=F32):
        return nc.dram_tensor(name, shape, dt, kind="ExternalInput").ap()

    def dout(name, shape, dt=F32):
        return nc.dram_tensor(name, shape, dt, kind="ExternalOutput").ap()

    xT = din("xT", [NT, 128, DC * 256])
    wa = din("wa", [128, DC, 769])
    nwd = din("nw", [128, DC])
    vecd = din("vec", [128, 3])
    cstd = din("cst", [128, 4, 128])
    cskT = din("cskT", [B_S, 128, PAST])
    cfkT = din("cfkT", [B_S, 128, PAST])
    csv = din("csv", [B_S, 128, NPB * 128])
    cfv = din("cfv", [B_S, 128, NPB * 128])
    clfd = din("clf", [128, B_S, NPB])

    o_kT = dout("o_kT", [2, 128, NTOK])
    o_v = dout("o_v", [NTOK, 256])
    o_lf = dout("o_lf", [128, NB])
    o_lfs = dout("o_lfs", [T_S, B_S])

    es = contextlib.ExitStack()
    with es:
        def sb(name, shape, dt):
            return es.enter_context(nc.sbuf_tensor("sA_" + name, shape, dt))

        def ps(name, shape, dt=F32):
            return es.enter_context(nc.psum_tensor("pA_" + name, shape, dt))

        qsT = sb("qsT", [128, NTOK], BF16)
        ksT = sb("ksT", [128, NTOK], BF16)
        qfT = sb("qfT", [128, NTOK], BF16)
        kfT = sb("kfT", [128, NTOK], BF16)
        vv = sb("vv", [128, NB, 256], BF16)
        vnb = sb("vnb", [T_S, B_S, 256], BF16)
        cst32 = sb("cst32", [128, 4, 128], F32)
        ones_bf = sb("ones_bf", [128, 128], BF16)
        tge_bf = sb("tge_bf", [128, 128], BF16)
        vec = sb("vec", [128, 3], F32)
        negb = sb("negb", [128, 1], F32)
        kcol = sb("kcol", [128, 1], F32)
        lfcol = sb("lfcol", [128, NB], F32)
        lfs = sb("lfs", [T_S, B_S], F32)
        fcum = sb("fcum", [128, NB], F32)
        incl = sb("incl", [128, NB], F32)
        biasp = sb("biasp", [128, B_S, NPB], F32)
        biasn = sb("biasn", [T_S, B_S], F32)
        ones32 = cst32[:, 0, :]
        mle32 = cst32[:, 1, :]
        mlt32 = cst32[:, 3, :]

        es1 = contextlib.ExitStack()
        with es1:
            def sb1(name, shape, dt):
                return es1.enter_context(nc.sbuf_tensor("sA1_" + name, shape, dt))

            def ps1(name, shape, dt=F32):
                return es1.enter_context(nc.psum_tensor("pA1_" + name, shape, dt))

            wbf = sb1("wbf", [128, DC, 784], BF16)
            wst = [sb1(f"wst{i}", [128, 769], F32) for i in range(2)]
            nw = sb1("nw", [128, DC], F32)
            nws = sb1("nws", [128, DC], F32)
            xt = [sb1(f"xt{i}", [128, DC, 256], BF16) for i in range(2)]
            sq = sb1("sq", [128, DC, 256], BF16)
            hT = [sb1(f"hT{i}", [128, DC, 256], BF16) for i in range(2)]
            rr = sb1("rr", [128, 256], F32)
            sqq = sb1("sqq", [128, 512], BF16)
            rq = sb1("rq", [128, 512], F32)
            kst32 = [sb1(f"kst32_{i}", [128, 256], F32) for i in range(2)]
            kf32 = [sb1(f"kf32_{i}", [128, 256], F32) for i in range(2)]
            v32 = [sb1(f"v32_{i}", [128, 256], F32) for i in range(2)]
            vn32 = [sb1(f"vn32_{i}", [T_S, 256], F32) for i in range(2)]
            fraw = sb1("fraw", [128, NB], F32)
            fraws = sb1("fraws", [T_S, B_S], F32)
            ftmp = sb1("ftmp", [128, NB], F32)
            ftmps = sb1("ftmps", [T_S, B_S], F32)
            ss_ps = ps1("ss_ps", [128, 512])
            psf = [ps1(f"psf{i}", [128, 512]) for i in range(2)]
            psv = [ps1(f"psv{i}", [128, 512]) for i in range(2)]
            ssq_ps = ps1("ssq_ps", [128, 512])

            P = Prog(nc, "a1")
            P.dma("sp", cst32[:], cstd, writes=["cst32"], semkey="cst32")
            P.dma("sp", vec[:], vecd, writes=["vec"], semkey="vec")
            P.dma("sp", nw[:], nwd, writes=["nw"], semkey="nw")
            P.act(lambda e: e.activation(ones_bf[:], cst32[:, 0, :], AF.Copy),
                  reads=["cst32"], writes=["ones_bf"])
            P.act(lambda e: e.activation(tge_bf[:], cst32[:, 2, :], AF.Copy),
                  reads=["cst32"], writes=["tge_bf"])
            P.dve(lambda e: e.tensor_scalar(negb[:], vec[:, 0:1], -1.0, None, ALU.mult),
                  reads=["vec"], writes=["negb"])
            P.dve(lambda e: e.tensor_scalar(kcol[:], vec[:, 2:3], float(HD) ** 0.5, None, ALU.mult),
                  reads=["vec"], writes=["kcol"])
            P.dve(lambda e: e.tensor_scalar(nws[:], nw[:], float(D) ** 0.5, None, ALU.mult),
                  reads=["nw"], writes=["nws"])
            for ch in range(DC):
                s = ch % 2
                P.dma("sp", wst[s][:], wa[:, ch, :], writes=[("wst", s)], semkey=("wst", s))
                if ch % 2 == 0:
                    P.dve(lambda e, ch=ch, s=s: e.tensor_scalar(
                        wbf[:, ch, 0:769], wst[s][:], nws[:, ch:ch + 1], None, ALU.mult),
                        reads=[("wst", s), "nws"], writes=[("wbf", ch)])
                else:
                    P.act(lambda e, ch=ch, s=s: e.activation(
                        wbf[:, ch, 0:769], wst[s][:], AF.Copy, scale=nws[:, ch:ch + 1]),
                        reads=[("wst", s), "nws"], writes=[("wbf", ch)])
            WB = [("wbf", ch) for ch in range(DC)]

            for ti in range(NT):
                t0 = ti * 256
                s = ti % 2
                is_samp = ti == NT - 1
                for hh in range(0, DC, 8):
                    P.dma("pool", xt[s][:, hh:hh + 8, :].rearrange("p c t -> p (c t)"),
                          xT[ti, :, hh * 256:(hh + 8) * 256],
                          writes=[("xt", s)], semkey=("xt", s))
                P.act(lambda e, s=s: e.activation(sq[:], xt[s][:], AF.Square),
                      reads=[("xt", s)], writes=["sq"])

                def mm_ss(e):
                    for ch in range(DC):
                        ins = e.matmul(ss_ps[:, 0:256], ones_bf[:], sq[:, ch, :],
                                       start=(ch == 0), stop=(ch == DC - 1))
                    return ins
                P.pe(mm_ss, reads=["sq", "ones_bf"], writes=["ss_ps"])
                P.act(lambda e: e.activation(rr[:], ss_ps[:, 0:256], AF.Ln, bias=float(D) * EPS),
                      reads=["ss_ps"], writes=["rr"])
                P.act(lambda e: e.activation(rr[:], rr[:], AF.Exp, scale=-0.5),
                      reads=["rr"], writes=["rr"])
                P.dve(lambda e, s=s: e.tensor_tensor(hT[s][:], xt[s][:], _bc_mid(rr[:], DC), ALU.mult),
                      reads=[("xt", s), "rr"], writes=[("hT", s)])

                for bk in range(2):
                    def mm_f(e, bk=bk, s=s):
                        for half in range(2):
                            cb = bk * 2 + half
                            for ch in range(DC):
                                ins = e.matmul(psf[bk][:, half * 256:(half + 1) * 256],
                                               wbf[:, ch, cb * 128:(cb + 1) * 128],
                                               hT[s][:, ch, :],
                                               start=(ch == 0), stop=(ch == DC - 1))
                        return ins
                    P.pe(mm_f, reads=[("hT", s)] + WB, writes=[("psf", bk)])
                P.act(lambda e, t0=t0: e.activation(qsT[:, t0:t0 + 256], psf[0][:, 0:256], AF.Copy, scale=SC),
                      reads=[("psf", 0)], writes=[("qsT", ti)])
                P.dve(lambda e, s=s: e.tensor_copy(kst32[s][:], psf[0][:, 256:512]),
                      reads=[("psf", 0)], writes=[("kst32", s)])
                P.act(lambda e, t0=t0: e.activation(ksT[:, t0:t0 + 256], psf[0][:, 256:512], AF.Copy),
                      reads=[("psf", 0)], writes=[("ksT", ti)])
                P.dma("sp", o_kT[0, :, t0:t0 + 256], kst32[s][:], reads=[("kst32", s)],
                      semkey=("kst32", s))
                P.act(lambda e: e.activation(sqq[:], psf[1][:], AF.Square),
                      reads=[("psf", 1)], writes=["sqq"])

                def mm_q(e):
                    e.matmul(ssq_ps[:, 0:256], ones_bf[:], sqq[:, 0:256], start=True, stop=True)
                    return e.matmul(ssq_ps[:, 256:512], ones_bf[:], sqq[:, 256:512], start=True, stop=True)
                P.pe(mm_q, reads=["sqq", "ones_bf"], writes=["ssq_ps"])
                P.act(lambda e: e.activation(rq[:], ssq_ps[:], AF.Ln, bias=float(HD) * EPS),
                      reads=["ssq_ps"], writes=["rq"])
                P.act(lambda e: e.activation(rq[:], rq[:], AF.Exp, scale=-0.5),
                      reads=["rq"], writes=["rq"])
                P.dve(lambda e, t0=t0: e.scalar_tensor_tensor(
                    qfT[:, t0:t0 + 256], psf[1][:, 0:256], vec[:, 1:2], rq[:, 0:256], ALU.mult, ALU.mult),
                    reads=[("psf", 1), "rq", "vec"], writes=[("qfT", ti)])
                P.dve(lambda e, s=s: e.scalar_tensor_tensor(
                    kf32[s][:], psf[1][:, 256:512], kcol[:, 0:1], rq[:, 256:512], ALU.mult, ALU.mult),
                    reads=[("psf", 1), "rq", "kcol"], writes=[("kf32", s)])
                P.act(lambda e, t0=t0, s=s: e.activation(kfT[:, t0:t0 + 256], kf32[s][:], AF.Copy),
                      reads=[("kf32", s)], writes=[("kfT", ti)])
                P.dma("sp", o_kT[1, :, t0:t0 + 256], kf32[s][:], reads=[("kf32", s)],
                      semkey=("kf32", s))

                if not is_samp:
                    for tb in range(2):
                        blk = ti * 2 + tb

                        def mm_v(e, tb=tb, s=s):
                            for ch in range(DC):
                                ins = e.matmul(psv[tb][:, 0:257], hT[s][:, ch, tb * 128:(tb + 1) * 128],
                                               wbf[:, ch, 512:769], start=(ch == 0), stop=(ch == DC - 1))
                            return ins
                        P.pe(mm_v, reads=[("hT", s)] + WB, writes=[("psv", tb)])
                        P.dve(lambda e, tb=tb: e.tensor_copy(v32[tb][:], psv[tb][:, 0:256]),
                              reads=[("psv", tb)], writes=[("v32", tb)])
                        P.act(lambda e, tb=tb, blk=blk: e.activation(vv[:, blk, :], psv[tb][:, 0:256], AF.Copy),
                              reads=[("psv", tb)], writes=[("vv", blk)])
                        P.dve(lambda e, tb=tb, blk=blk: e.tensor_copy(fraw[:, blk:blk + 1], psv[tb][:, 256:257]),
                              reads=[("psv", tb)], writes=[("fraw", blk)])
                        P.dma("sp", o_v[t0 + tb * 128:t0 + (tb + 1) * 128, :], v32[tb][:],
                              reads=[("v32", tb)], semkey=("v32", tb))
                else:
                    for b in range(B_S):
                        pb = b % 2

                        def mm_vs(e, b=b, pb=pb, s=s):
                            for ch in range(DC):
                                ins = e.matmul(psv[pb][0:T_S, 0:257], hT[s][:, ch, b * T_S:(b + 1) * T_S],
                                               wbf[:, ch, 512:769], start=(ch == 0), stop=(ch == DC - 1))
                            return ins
                        P.pe(mm_vs, reads=[("hT", s)] + WB, writes=[("psv", pb)])
                        P.dve(lambda e, b=b, pb=pb: e.tensor_copy(vn32[pb][:], psv[pb][0:T_S, 0:256]),
                              reads=[("psv", pb)], writes=[("vn32", pb)])
                        P.act(lambda e, b=b, pb=pb: e.activation(vnb[:, b, :], psv[pb][0:T_S, 0:256], AF.Copy),
                              reads=[("psv", pb)], writes=[("vnb", b)])
                        P.dma("sp", o_v[T_P + b * T_S:T_P + (b + 1) * T_S, :], vn32[pb][:],
                              reads=[("vn32", pb)], semkey=("vn32", pb))
                        P.dve(lambda e, b=b, pb=pb: e.tensor_copy(fraws[:, b:b + 1], psv[pb][0:T_S, 256:257]),
                              reads=[("psv", pb)], writes=[("fraws", b)])

            FR = [("fraw", blk) for blk in range(NB)]
            P.act(lambda e: e.activation(ftmp[:], fraw[:], AF.Exp, bias=negb[:, 0:1], scale=-1.0),
                  reads=FR + ["negb"], writes=["ftmp"])
            P.act(lambda e: e.activation(ftmp[:], ftmp[:], AF.Ln, bias=1.0), reads=["ftmp"], writes=["ftmp"])
            P.dve(lambda e: e.tensor_scalar(lfcol[:], ftmp[:], -1.0, None, ALU.mult),
                  reads=["ftmp"], writes=["lfcol"])
            P.dma("sp", o_lf, lfcol[:], reads=["lfcol"], semkey="o_lf")
            FRS = [("fraws", b) for b in range(B_S)]
            P.act(lambda e: e.activation(ftmps[:], fraws[:], AF.Exp, bias=negb[0:T_S, 0:1], scale=-1.0),
                  reads=FRS + ["negb"], writes=["ftmps"])
            P.act(lambda e: e.activation(ftmps[:], ftmps[:], AF.Ln, bias=1.0), reads=["ftmps"], writes=["ftmps"])
            P.dve(lambda e: e.tensor_scalar(lfs[:], ftmps[:], -1.0, None, ALU.mult),
                  reads=["ftmps"], writes=["lfs"])
            P.dma("sp", o_lfs, lfs[:], reads=["lfs"], semkey="o_lfs")
            P.emit()

        import os as _os
        if _os.environ.get("STOP_A1"):
            return
        es2 = contextlib.ExitStack()
        with es2:
            def sb2(name, shape, dt):
                return es2.enter_context(nc.sbuf_tensor("sA2_" + name, shape, dt))

            def ps2(name, shape, dt=F32):
                return es2.enter_context(nc.psum_tensor("pA2_" + name, shape, dt))

            tot = sb2("tot", [128, NB], F32)
            onesr = sb2("onesr", [128, NB], F32)
            tmpf = sb2("tmpf", [128, NB], F32)
            clf = sb2("clf", [128, B_S, NPB], F32)
            tots = sb2("tots", [128, B_S * NPB], F32)
            segm = sb2("segm", [128, B_S, NPB], F32)
            incls = sb2("incls", [128, B_S * NPB], F32)
            fcums = sb2("fcums", [128, B_S, NPB], F32)
            fref = sb2("fref", [128, B_S], F32)
            cumn = sb2("cumn", [T_S, B_S], F32)
            biasq = [sb2(f"biasq{i}", [128, NB], F32) for i in range(2)]
            e_sb = [sb2(f"e_sb{i}", [128, 512], BF16) for i in range(2)]
            sp_sb = [sb2(f"sp_sb{i}", [128, 512], BF16) for i in range(2)]
            S_sb = [sb2(f"S_sb{i}", [128, 512], BF16) for i in range(2)]
            X_sb = [sb2(f"X_sb{i}", [128, 512], BF16) for i in range(2)]
            W_sb = [sb2(f"W_sb{i}", [128, 512], BF16) for i in range(2)]
            P_fx = [sb2(f"P_fx{i}", [128, 512], BF16) for i in range(2)]
            rden = sb2("rden", [128, 512], F32)
            ogs1 = sb2("ogs0", [128, 512], BF16)
            ogf1 = sb2("ogf0", [128, 512], BF16)
            ogs = [ogs1, ogs1]
            ogf = [ogf1, ogf1]
            Pacc = sb2("Pacc", [128, 512], F32)
            ogsamp = [sb2(f"ogsamp{i}", [128, NSAMP], BF16) for i in range(2)]
            ckT = [[sb2(f"ckT{br}_{i}", [128, PAST], BF16) for i in range(2)] for br in range(2)]
            cvv = [[sb2(f"cvv{br}_{i}", [128, NPB, 128], BF16) for i in range(2)] for br in range(2)]
            z_ps = [ps2(f"z_ps{i}", [128, 512]) for i in range(2)]
            c_ps = [ps2(f"c_ps{i}", [128, 512]) for i in range(2)]
            o_ps = [ps2(f"o_ps{i}", [128, 512]) for i in range(2)]
            den_ps = ps2("den_ps", [128, 512])
            misc_ps = ps2("misc_ps", [128, 512])
            en_s = sb2("en_s", [T_S, T_S], BF16)
            spn_s = sb2("spn_s", [T_S, T_S], BF16)
            xn_s = sb2("xn_s", [T_S, T_S], BF16)
            wn_s = sb2("wn_s", [T_S, T_S], BF16)
            pn_s = sb2("pn_s", [T_S, T_S], BF16)
            spnb = sb2("spnb", [T_S, 512], BF16)
            tz = rden

            P = Prog(nc, "a2")
            NBS = B_S * NPB
            P.dve(lambda e: e.memset(onesr[:], 1.0), writes=["onesr"])

            def mm_cum(e):
                e.matmul(c_ps[0][:, 0:NB], mle32, lfcol[:], start=True, stop=True)
                return e.matmul(c_ps[1][:, 0:NB], ones32, lfcol[:], start=True, stop=True)
            P.pe(mm_cum, reads=["lfcol", "cst32"], writes=[("c_ps", 0), ("c_ps", 1)])
            P.dve(lambda e: e.tensor_copy(tot[:], c_ps[1][:, 0:NB]), reads=[("c_ps", 1)], writes=["tot"])
            P.dve(lambda e: e.tensor_tensor_scan(incl[:], onesr[:, 0:NB], tot[:], 0.0, ALU.mult, ALU.add),
                  reads=["onesr", "tot"], writes=["incl"])
            P.dve(lambda e: e.tensor_tensor(tmpf[:], incl[:], tot[:], ALU.subtract),
                  reads=["incl", "tot"], writes=["tmpf"])
            P.dve(lambda e: e.tensor_tensor(fcum[:], c_ps[0][:, 0:NB], tmpf[:], ALU.add),
                  reads=[("c_ps", 0), "tmpf"], writes=["fcum"])
            P.dma("sp", clf[:], clfd, writes=["clf"], semkey="clf")
            P.dve(lambda e: e.memset(segm[:], 1.0), writes=["segm"])
            P.dve(lambda e: e.memset(segm[:, :, 0:1], 0.0), reads=["segm"], writes=["segm"])
            clf2 = clf[:].rearrange("p b k -> p (b k)")

            def mm_cums(e):
                e.matmul(c_ps[0][:, 0:NBS], mle32, clf2, start=True, stop=True)
                return e.matmul(c_ps[1][:, 0:NBS], ones32, clf2, start=True, stop=True)
            P.pe(mm_cums, reads=["clf", "cst32"], writes=[("c_ps", 0), ("c_ps", 1)])
            P.dve(lambda e: e.tensor_copy(tots[:], c_ps[1][:, 0:NBS]), reads=[("c_ps", 1)], writes=["tots"])
            P.dve(lambda e: e.tensor_tensor_scan(incls[:], segm[:].rearrange("p b k -> p (b k)"), tots[:], 0.0,
                                                 ALU.mult, ALU.add),
                  reads=["segm", "tots"], writes=["incls"])
            P.dve(lambda e: e.tensor_tensor(tots[:], incls[:], tots[:], ALU.subtract),
                  reads=["incls", "tots"], writes=["tots"])
            P.dve(lambda e: e.tensor_tensor(fcums[:].rearrange("p b k -> p (b k)"), c_ps[0][:, 0:NBS], tots[:],
                                            ALU.add),
                  reads=[("c_ps", 0), "tots"], writes=["fcums"])
            def mm_new(e):
                e.matmul(z_ps[0][0:T_S, 0:B_S], cst32[0:T_S, 1, 0:T_S], lfs[:], start=True, stop=True)
                return e.matmul(z_ps[1][:, 0:B_S], cst32[0:T_S, 0, :], lfs[:], start=True, stop=True)
            P.pe(mm_new, reads=["lfs", "cst32"], writes=[("z_ps", 0), ("z_ps", 1)])
            P.dve(lambda e: e.tensor_copy(cumn[:], z_ps[0][0:T_S, 0:B_S]), reads=[("z_ps", 0)], writes=["cumn"])
            incls3 = incls[:].rearrange("p (b k) -> p b k", b=B_S)
            P.dve(lambda e: e.tensor_tensor(fref[:], z_ps[1][:, 0:B_S], incls3[:, :, NPB - 1], ALU.add),
                  reads=[("z_ps", 1), "incls"], writes=["fref"])
            P.dve(lambda e: e.tensor_tensor(biasp[:], fref[:].unsqueeze(2).broadcast_to([128, B_S, NPB]),
                                            fcums[:], ALU.subtract),
                  reads=["fref", "fcums"], writes=["biasp"])
            P.dve(lambda e: e.tensor_tensor(biasn[:], z_ps[1][0:T_S, 0:B_S], cumn[:], ALU.subtract),
                  reads=[("z_ps", 1), "cumn"], writes=["biasn"])


            ogd = [og_dst(0), og_dst(1)]

            def tkeys(name, lo, hi):
                return [(name, t) for t in range(lo // 256, (hi + 255) // 256)]

            def prompt_items(qt):
                q0 = qt * 512
                nkb = 4 * qt + 4
                its = []
                for j, kb in enumerate(range(nkb - 1, -1, -1)):
                    i = kb - 4 * qt
                    its.append(dict(qt=qt, q0=q0, kb=kb, k0=kb * 128, c0=(128 * i if i > 0 else 0),
                                    diag=(i >= 0), first=(j == 0), last=(kb == 0), idx=j))
                return its

            gcnt = {"sb": 0, "fx": 0}

            def sb_s1(it):
                si = it["g"] % 2
                c0, q0, k0 = it["c0"], it["q0"], it["k0"]
                P.pe(lambda e: e.matmul(z_ps[0][:, c0:512], ksT[:, k0:k0 + 128], qsT[:, q0 + c0:q0 + 512],
                                        start=True, stop=True),
                     reads=tkeys("ksT", k0, k0 + 128) + tkeys("qsT", q0, q0 + 512), writes=[("z_ps", 0)])
                P.act(lambda e: e.activation(e_sb[si][:, c0:512], z_ps[0][:, c0:512], AF.Exp),
                      reads=[("z_ps", 0)], writes=[("e_sb", si)])
                if it["diag"]:
                    P.dve(lambda e: e.tensor_tensor(e_sb[si][:, c0:c0 + 128], e_sb[si][:, c0:c0 + 128],
                                                    cst32[:, 3, :], ALU.mult),
                          reads=[("e_sb", si), "cst32"], writes=[("e_sb", si)])
                P.act(lambda e: e.activation(sp_sb[si][:, c0:512], e_sb[si][:, c0:512], AF.Ln, bias=1.0),
                      reads=[("e_sb", si)], writes=[("sp_sb", si)])

            def sb_s2(it):
                si = it["g"] % 2
                sprev = 1 - si
                c0 = it["c0"]
                first, last = it["first"], it["last"]

                def mm_c(e):
                    ins = e.matmul(c_ps[si][:, c0:512], tge_bf[:], sp_sb[si][:, c0:512], start=True, stop=first)
                    if not first:
                        ins = e.matmul(c_ps[si][:, c0:512], ones_bf[:], S_sb[sprev][:, c0:512],
                                       start=False, stop=True)
                    return ins
                P.pe(mm_c, reads=[("sp_sb", si), "tge_bf", "ones_bf"] + ([] if first else [("S_sb", sprev)]),
                     writes=[("c_ps", si)])
                P.act(lambda e: e.activation(X_sb[si][:, c0:512], c_ps[si][:, c0:512], AF.Exp, scale=-1.0),
                      reads=[("c_ps", si)], writes=[("X_sb", si)])
                P.dve(lambda e: e.tensor_tensor(W_sb[si][:, c0:512], e_sb[si][:, c0:512], X_sb[si][:, c0:512],
                                                ALU.mult),
                      reads=[("e_sb", si), ("X_sb", si)], writes=[("W_sb", si)])
                if not last:
                    if first:
                        P.dve(lambda e: e.memset(S_sb[si][:, 0:c0], 0.0), writes=[("S_sb", si)])
                        P.dve(lambda e: e.tensor_copy(S_sb[si][:, c0:512], sp_sb[si][:, c0:512]),
                              reads=[("sp_sb", si), ("S_sb", si)], writes=[("S_sb", si)])
                    elif c0 > 0:
                        P.dve(lambda e: e.tensor_copy(S_sb[si][:, 0:c0], S_sb[sprev][:, 0:c0]),
                              reads=[("S_sb", sprev)], writes=[("S_sb", si)])
                        P.dve(lambda e: e.tensor_tensor(S_sb[si][:, c0:512], S_sb[sprev][:, c0:512],
                                                        sp_sb[si][:, c0:512], ALU.add),
                              reads=[("S_sb", sprev), ("sp_sb", si), ("S_sb", si)], writes=[("S_sb", si)])
                    else:
                        P.dve(lambda e: e.tensor_tensor(S_sb[si][:], S_sb[sprev][:], sp_sb[si][:], ALU.add),
                              reads=[("S_sb", sprev), ("sp_sb", si)], writes=[("S_sb", si)])

            def sb_s3(it):
                si = it["g"] % 2
                c0, kb, qt = it["c0"], it["kb"], it["qt"]
                P.pe(lambda e: e.matmul(o_ps[0][:, c0:512], vv[:, kb, 0:128], W_sb[si][:, c0:512],
                                        start=it["first"], stop=it["last"]),
                     reads=[("vv", kb), ("W_sb", si)], writes=[("o_ps", 0)])
                if it["last"]:
                    osl = qt % 2
                    q0 = it["q0"]
                    P.act(lambda e: e.activation(ogs[0][:], o_ps[0][:], AF.Copy),
                          reads=[("o_ps", 0)], writes=[("ogs", 0)])
                    P.dma("sp", ogd[0][:, q0 // 2:(q0 + 512) // 2], ogs[0][:].bitcast(F32),
                          reads=[("ogs", 0)], semkey=("ogs", 0))

            def fx_s1(it):
                pi = it["g"] % 2
                zi = 1
                c0, q0, k0, kb = it["c0"], it["q0"], it["k0"], it["kb"]
                osl = it["qt"] % 2
                P.pe(lambda e: e.matmul(z_ps[zi][:, c0:512], kfT[:, k0:k0 + 128], qfT[:, q0 + c0:q0 + 512],
                                        start=True, stop=True),
                     reads=tkeys("kfT", k0, k0 + 128) + tkeys("qfT", q0, q0 + 512), writes=[("z_ps", zi)])
                P.act(lambda e: e.activation(P_fx[pi][:, c0:512], z_ps[zi][:, c0:512], AF.Exp,
                                             bias=biasq[osl][:, kb:kb + 1]),
                      reads=[("z_ps", zi), ("biasq", osl)], writes=[("P_fx", pi)])
                if it["diag"]:
                    P.dve(lambda e: e.tensor_tensor(P_fx[pi][:, c0:c0 + 128], P_fx[pi][:, c0:c0 + 128],
                                                    cst32[:, 1, :], ALU.mult),
                          reads=[("P_fx", pi), "cst32"], writes=[("P_fx", pi)])

            def fx_s2(it):
                pi = it["g"] % 2
                c0, kb, qt = it["c0"], it["kb"], it["qt"]

                P.pe(lambda e: e.matmul(o_ps[1][:, c0:512], vv[:, kb, 128:256], P_fx[pi][:, c0:512],
                                        start=it["first"], stop=it["last"]),
                     reads=[("vv", kb), ("P_fx", pi)], writes=[("o_ps", 1)])
                if it["first"]:
                    P.pool(lambda e: e.memset(Pacc[:], 0.0), writes=["Pacc"])
                P.pool(lambda e: e.tensor_tensor(Pacc[:, c0:512], Pacc[:, c0:512], P_fx[pi][:, c0:512], ALU.add),
                       reads=["Pacc", ("P_fx", pi)], writes=["Pacc"])
                if it["last"]:
                    q0 = it["q0"]
                    P.pe(lambda e: e.matmul(den_ps[:], ones32, Pacc[:], start=True, stop=True),
                         reads=["Pacc", "cst32"], writes=["den_ps"])
                    P.dve(lambda e: e.reciprocal(rden[:], den_ps[:]), reads=["den_ps"], writes=["rden"])
                    P.dve(lambda e: e.tensor_tensor(ogf[0][:], o_ps[1][:], rden[:], ALU.mult),
                          reads=[("o_ps", 1), "rden"], writes=[("ogf", 0)])
                    P.dma("sp", ogd[1][:, q0 // 2:(q0 + 512) // 2], ogf[0][:].bitcast(F32),
                          reads=[("ogf", 0)], semkey=("ogf", 0))

            def run_pipeline(items, stages, tag):
                for it in items:
                    it["g"] = gcnt[tag]
                    gcnt[tag] += 1
                ns = len(stages)
                n = len(items)
                for step in range(n + ns - 1):
                    for si_, st in enumerate(stages):
                        k = step - si_
                        if 0 <= k < n:
                            st(items[k])

            NPQ = NPB * T_S
            assert NPQ <= 512
            nsteps = max(1, (NPB - 1).bit_length())

            def sample_unit(b):
                sl = b % 2
                qa, qb_ = T_P + b * T_S, T_P + (b + 1) * T_S
                for br in range(2):
                    kd, vd = csrc[br]
                    for hh in range(0, PAST, 2048):
                        he = min(PAST, hh + 2048)
                        P.dma("pool", ckT[br][sl][:, hh:he], kd[b, :, hh:he],
                              writes=[("ckT", br, sl)], semkey=("ckT", br, sl))
                    cvf = cvv[br][sl][:].rearrange("p k d -> p (k d)")
                    for hh in range(0, NPB * 128, 2048):
                        he = min(NPB * 128, hh + 2048)
                        P.dma("pool", cvf[:, hh:he], vd[b, :, hh:he],
                              writes=[("cvv", br, sl)], semkey=("cvv", br, sl))
                mark0 = len(P.ops)
                zi = 0
                kk = [("ckT", 0, sl)]
                qk = [("qsT", NT - 1)]

                def mm_z(e, K=ckT[0][sl], qT_=qsT, kT_=ksT, zi=zi):
                    for blk in range(NPB):
                        e.matmul(z_ps[zi][:, blk * T_S:(blk + 1) * T_S], K[:, blk * 128:(blk + 1) * 128],
                                 qT_[:, qa:qb_], start=True, stop=True)
                    return e.matmul(misc_ps[0:T_S, 0:T_S], kT_[:, qa:qb_], qT_[:, qa:qb_], start=True, stop=True)
                P.pe(mm_z, reads=kk + qk + [("ksT", NT - 1)], writes=[("z_ps", zi), "misc_ps"])
                P.act(lambda e: e.activation(e_sb[0][:, 0:NPQ], z_ps[0][:, 0:NPQ], AF.Exp),
                      reads=[("z_ps", 0)], writes=[("e_sb", 0)])
                P.act(lambda e: e.activation(en_s[:], misc_ps[0:T_S, 0:T_S], AF.Exp),
                      reads=["misc_ps"], writes=["en_s"])
                P.dve(lambda e: e.tensor_tensor(en_s[:], en_s[:], cst32[0:T_S, 3, 0:T_S], ALU.mult),
                      reads=["en_s", "cst32"], writes=["en_s"])
                P.act(lambda e: e.activation(sp_sb[0][:, 0:NPQ], e_sb[0][:, 0:NPQ], AF.Ln, bias=1.0),
                      reads=[("e_sb", 0)], writes=[("sp_sb", 0)])
                P.act(lambda e: e.activation(spn_s[:], en_s[:], AF.Ln, bias=1.0), reads=["en_s"], writes=["spn_s"])
                P.dve(lambda e: e.tensor_copy(spnb[:, 0:NPQ].rearrange("p (k q) -> p k q", q=T_S),
                                              spn_s[:].unsqueeze(1).broadcast_to([T_S, NPB, T_S])),
                      reads=["spn_s"], writes=["spnb"])
                src, srck = sp_sb[0], ("sp_sb", 0)
                for stp in range(nsteps):
                    sh = 1 << stp
                    dst, dstk = (S_sb[stp % 2], ("S_sb", stp % 2))
                    w = (NPB - sh) * T_S
                    P.dve(lambda e, src=src, dst=dst, w=w, sh=sh: e.tensor_tensor(
                        dst[:, 0:w], src[:, 0:w], src[:, sh * T_S:sh * T_S + w], ALU.add),
                        reads=[srck], writes=[dstk])
                    P.dve(lambda e, src=src, dst=dst, w=w: e.tensor_copy(dst[:, w:NPQ], src[:, w:NPQ]),
                          reads=[srck, dstk], writes=[dstk])
                    src, srck = dst, dstk
                incl, inclk = src, srck

                def mm_c(e, incl=incl):
                    e.matmul(c_ps[0][:, 0:NPQ], tge_bf[:], sp_sb[0][:, 0:NPQ], start=True, stop=False)
                    e.matmul(c_ps[0][:, 0:NPQ - T_S], ones_bf[:], incl[:, T_S:NPQ], start=False, stop=False)
                    e.matmul(c_ps[0][:, 0:NPQ], ones_bf[0:T_S, :], spnb[:, 0:NPQ], start=False, stop=True)
                    return e.matmul(misc_ps[0:T_S, T_S:2 * T_S], tge_bf[0:T_S, 0:T_S], spn_s[:],
                                    start=True, stop=True)
                P.pe(mm_c, reads=[("sp_sb", 0), inclk, "spnb", "spn_s", "tge_bf", "ones_bf"],
                     writes=[("c_ps", 0), "misc_ps"])
                P.act(lambda e: e.activation(X_sb[0][:, 0:NPQ], c_ps[0][:, 0:NPQ], AF.Exp, scale=-1.0),
                      reads=[("c_ps", 0)], writes=[("X_sb", 0)])
                P.act(lambda e: e.activation(xn_s[:], misc_ps[0:T_S, T_S:2 * T_S], AF.Exp, scale=-1.0),
                      reads=["misc_ps"], writes=["xn_s"])
                P.dve(lambda e: e.tensor_tensor(W_sb[0][:, 0:NPQ], e_sb[0][:, 0:NPQ], X_sb[0][:, 0:NPQ], ALU.mult),
                      reads=[("e_sb", 0), ("X_sb", 0)], writes=[("W_sb", 0)])
                P.dve(lambda e: e.tensor_tensor(wn_s[:], en_s[:], xn_s[:], ALU.mult),
                      reads=["en_s", "xn_s"], writes=["wn_s"])

                def mm_o(e, V=cvv[0][sl]):
                    for blk in range(NPB):
                        e.matmul(o_ps[0][:, 0:T_S], V[:, blk, :], W_sb[0][:, blk * T_S:(blk + 1) * T_S],
                                 start=(blk == 0), stop=False)
                    return e.matmul(o_ps[0][:, 0:T_S], vnb[:, b, 0:128], wn_s[:], start=False, stop=True)
                P.pe(mm_o, reads=[("cvv", 0, sl), ("W_sb", 0), "wn_s", ("vnb", b)], writes=[("o_ps", 0)])
                P.act(lambda e: e.activation(ogsamp[0][:, b * T_S:(b + 1) * T_S], o_ps[0][:, 0:T_S], AF.Copy),
                      reads=[("o_ps", 0)], writes=[("ogsamp", 0, b)])
                mark1 = len(P.ops)
                zi = 1

                def mm_zf(e, K=ckT[1][sl], zi=zi):
                    for blk in range(NPB):
                        e.matmul(z_ps[zi][:, blk * T_S:(blk + 1) * T_S], K[:, blk * 128:(blk + 1) * 128],
                                 qfT[:, qa:qb_], start=True, stop=True)
                    return e.matmul(misc_ps[0:T_S, 2 * T_S:3 * T_S], kfT[:, qa:qb_], qfT[:, qa:qb_],
                                    start=True, stop=True)
                P.pe(mm_zf, reads=[("ckT", 1, sl), ("qfT", NT - 1), ("kfT", NT - 1)],
                     writes=[("z_ps", zi), "misc_ps"])
                P.dve(lambda e: e.tensor_tensor(tz[:, 0:NPQ].rearrange("p (k q) -> p k q", q=T_S),
                                                z_ps[1][:, 0:NPQ].rearrange("p (k q) -> p k q", q=T_S),
                                                biasp[:, b, :].unsqueeze(2).broadcast_to([128, NPB, T_S]),
                                                ALU.add),
                      reads=[("z_ps", 1), "biasp"], writes=["rden"])
                P.act(lambda e: e.activation(P_fx[0][:, 0:NPQ], tz[:, 0:NPQ], AF.Exp), reads=["rden"], writes=[("P_fx", 0)])
                P.act(lambda e: e.activation(pn_s[:], misc_ps[0:T_S, 2 * T_S:3 * T_S], AF.Exp,
                                             bias=biasn[:, b:b + 1]),
                      reads=["misc_ps", "biasn"], writes=["pn_s"])
                P.dve(lambda e: e.tensor_tensor(pn_s[:], pn_s[:], cst32[0:T_S, 1, 0:T_S], ALU.mult),
                      reads=["pn_s", "cst32"], writes=["pn_s"])

                def mm_of(e, V=cvv[1][sl]):
                    for blk in range(NPB):
                        e.matmul(o_ps[1][:, 0:T_S], V[:, blk, :], P_fx[0][:, blk * T_S:(blk + 1) * T_S],
                                 start=(blk == 0), stop=False)
                    e.matmul(o_ps[1][:, 0:T_S], vnb[:, b, 128:256], pn_s[:], start=False, stop=True)
                    for blk in range(NPB):
                        e.matmul(den_ps[:, 0:T_S], ones_bf[:], P_fx[0][:, blk * T_S:(blk + 1) * T_S],
                                 start=(blk == 0), stop=False)
                    return e.matmul(den_ps[:, 0:T_S], ones_bf[0:T_S, :], pn_s[:], start=False, stop=True)
                P.pe(mm_of, reads=[("cvv", 1, sl), ("P_fx", 0), "pn_s", ("vnb", b), "ones_bf"],
                     writes=[("o_ps", 1), "den_ps"])
                P.dve(lambda e: e.reciprocal(rden[:, 0:T_S], den_ps[:, 0:T_S]), reads=["den_ps"], writes=["rden"])
                P.dve(lambda e: e.tensor_tensor(ogsamp[1][:, b * T_S:(b + 1) * T_S], o_ps[1][:, 0:T_S],
                                                rden[:, 0:T_S], ALU.mult),
                      reads=[("o_ps", 1), "rden"], writes=[("ogsamp", 1, b)])
                a_ops, b_ops = P.ops[mark0:mark1], P.ops[mark1:]
                merged = []
                ia = ib = 0
                while ia < len(a_ops) or ib < len(b_ops):
                    if ia < len(a_ops):
                        merged.append(a_ops[ia]); ia += 1
                    if ia < len(a_ops) and len(a_ops) > 2 * len(b_ops) - 2:
                        merged.append(a_ops[ia]); ia += 1
                    if ib < len(b_ops):
                        merged.append(b_ops[ib]); ib += 1
                P.ops[mark0:] = merged

            csrc = [(cskT, csv), (cfkT, cfv)]
            sample_done = 0
            for qt in range(NQT):
                osl = qt % 2
                nkb = 4 * qt + 4
                P.dve(lambda e, osl=osl, nkb=nkb: e.tensor_scalar(
                    biasq[osl][:, 0:nkb], fcum[:, 0:nkb], -1.0, incl[:, nkb - 1:nkb], ALU.mult, ALU.add),
                    reads=["fcum", "incl"], writes=[("biasq", osl)])
                def st1(it):
                    sb_s1(it)
                    fx_s1(it)

                def st2(it):
                    sb_s2(it)
                    fx_s2(it)
                run_pipeline(prompt_items(qt), [st1, st2, sb_s3], "sb")
                tgt = (B_S * (qt + 1)) // NQT
                while sample_done < tgt:
                    sample_unit(sample_done)
                    sample_done += 1
            while sample_done < B_S:
                sample_unit(sample_done)
                sample_done += 1
            for br in range(2):
                P.dma("sp", ogd[br][:, T_P // 2:NTOK // 2], ogsamp[br][:].bitcast(F32),
                      reads=[("ogsamp", br, b) for b in range(B_S)], semkey=("ogsamp", br))

            P.emit()


def build_C(nc, NPC, og_src):
    NPP = NPC - 32
    ntile = (NPC + 511) // 512
    TW = NPC // ntile
    assert TW * ntile == NPC
    NPBK = NPP // 128

    def din(name, shape, dt=F32):
        return nc.dram_tensor(name, shape, dt, kind="ExternalInput").ap()

    xmT = din("xmT", [128, DC, NPC])
    xm = din("xm", [NPC, D])
    nwd = din("nw", [128, DC])
    cstd = din("cst", [128, 4, 128])
    wg = din("wg", [DC, 128, DC, 128])
    wm = din("wm", [2, DC, 128, DC, 128])
    wb = din("wb", [2, DC, 128, 8, 128])
    wo = din("wo", [4, 128, DC, 512])
    o_y = nc.dram_tensor("o_y", [NPC, D], F32, kind="ExternalOutput").ap()

    es = contextlib.ExitStack()
    with es:
        def sb(name, shape, dt):
            return es.enter_context(nc.sbuf_tensor("sC_" + name, shape, dt))

        def ps(name, shape, dt=F32):
            return es.enter_context(nc.psum_tensor("pC_" + name, shape, dt))

        hT = sb("hT", [128, DC, NPC], BF16)
        og = sb("og", [128, DC, NPC], BF16)
        mg = sb("mg", [128, DC, NPC], BF16)
        cst32 = sb("cst32", [128, 4, 128], F32)
        ones_bf = sb("ones_bf", [128, 128], BF16)
        nw = sb("nw", [128, DC], F32)
        nws = sb("nws", [128, DC], F32)
        xt = [sb(f"xt{i}", [128, TW], F32) for i in range(2)]
        rr = sb("rr", [128, TW], F32)
        wgb = [sb(f"wgb{i}", [128, DC, 128], BF16) for i in range(2)]
        wmb = [[sb(f"wmb{br}_{i}", [128, DC, 128], BF16) for i in range(2)] for br in range(2)]
        wbb = [[sb(f"wbb{br}_{i}", [128, 8, 128], BF16) for i in range(2)] for br in range(2)]
        sg = [sb(f"sg{i}", [128, TW], BF16) for i in range(2)]
        sig = [sb(f"sig{i}", [128, TW], F32) for i in range(2)]
        t1 = [sb(f"t1_{i}", [128, TW], F32) for i in range(2)]
        wob = [sb(f"wob{i}", [128, DC, 512], BF16) for i in range(2)]
        xres = [sb(f"xres{i}", [128, 512], F32) for i in range(2)]
        yst = [sb(f"yst{i}", [128, 512], F32) for i in range(2)]
        pA = [ps(f"pA{i}", [128, 512]) for i in range(2)]
        pU = [ps(f"pU{i}", [128, 512]) for i in range(2)]
        pM = [ps(f"pM{i}", [128, 512]) for i in range(2)]
        pY = [ps(f"pY{i}", [128, 512]) for i in range(2)]

        P = Prog(nc, "c")
        P.dma("sp", cst32[:], cstd, writes=["cst32"], semkey="cst32")
        P.dma("sp", nw[:], nwd, writes=["nw"], semkey="nw")
        P.act(lambda e: e.activation(ones_bf[:], cst32[:, 0, :], AF.Copy), reads=["cst32"], writes=["ones_bf"])
        P.dve(lambda e: e.tensor_scalar(nws[:], nw[:], float(D) ** 0.5, None, ALU.mult),
              reads=["nw"], writes=["nws"])
        for ch in range(DC):
            P.dma("pool", og[:, ch, :].bitcast(F32), og_src(ch), writes=[("og", ch)], semkey=("og", ch))
        for ti in range(ntile):
            c0 = ti * TW
            for ch in range(DC):
                s = ch % 2
                P.dma("sp", xt[s][:], xmT[:, ch, c0:c0 + TW], writes=[("xt", s)], semkey=("xt", s))
                P.act(lambda e, s=s, ch=ch, c0=c0: e.activation(mg[:, ch, c0:c0 + TW], xt[s][:], AF.Square),
                      reads=[("xt", s)], writes=[("sq", ch)])
                P.dve(lambda e, s=s, ch=ch, c0=c0: e.tensor_scalar(
                    hT[:, ch, c0:c0 + TW], xt[s][:], nws[:, ch:ch + 1], None, ALU.mult),
                    reads=[("xt", s), "nws"], writes=[("hT", ti)])

            def mm_ss(e, c0=c0):
                for ch in range(DC):
                    ins = e.matmul(pA[0][:, 0:TW], ones_bf[:], mg[:, ch, c0:c0 + TW],
                                   start=(ch == 0), stop=(ch == DC - 1))
                return ins
            P.pe(mm_ss, reads=[("sq", ch) for ch in range(DC)] + ["ones_bf"], writes=[("pA", 0)])
            P.act(lambda e: e.activation(rr[:], pA[0][:, 0:TW], AF.Ln, bias=float(D) * EPS),
                  reads=[("pA", 0)], writes=["rr"])
            P.act(lambda e: e.activation(rr[:], rr[:], AF.Exp, scale=-0.5), reads=["rr"], writes=["rr"])
            P.dve(lambda e, c0=c0: e.tensor_tensor(hT[:, :, c0:c0 + TW], hT[:, :, c0:c0 + TW],
                                                   _bc_mid(rr[:], DC), ALU.mult),
                  reads=[("hT", ti), "rr"], writes=[("hT", ti)])
        HT = [("hT", ti) for ti in range(ntile)]

        wq = {"n": 0}

        def load_w(dst_bf, src_ap, key, nch):
            P.dma("pool", dst_bf[:], src_ap, writes=[key], semkey=key)

        for cb in range(DC):
            s = cb % 2
            load_w(wgb[s], wg[cb], ("wgb", s), DC)
            for ti in range(ntile):
                c0 = ti * TW
                pi = ti % 2

                def mm_g(e, s=s, c0=c0, pi=pi):
                    for ch in range(DC):
                        ins = e.matmul(pA[pi][:, 0:TW], wgb[s][:, ch, :], hT[:, ch, c0:c0 + TW],
                                       start=(ch == 0), stop=(ch == DC - 1))
                    return ins
                P.pe(mm_g, reads=[("wgb", s)] + HT, writes=[("pA", pi)])
                P.act(lambda e, pi=pi: e.activation(sg[pi][:], pA[pi][:, 0:TW], AF.Silu),
                      reads=[("pA", pi)], writes=[("sg", pi)])
                P.dve(lambda e, cb=cb, c0=c0, pi=pi: e.tensor_tensor(
                    og[:, cb, c0:c0 + TW], og[:, cb, c0:c0 + TW], sg[pi][:], ALU.mult),
                    reads=[("og", cb), ("sg", pi)], writes=[("og", cb)])
        OG = [("og", ch) for ch in range(DC)]

        for cb in range(DC):
            s = cb % 2
            for br in range(2):
                load_w(wmb[br][s], wm[br, cb], ("wmb", br, s), DC)
                load_w(wbb[br][s], wb[br, cb], ("wbb", br, s), 8)
            for ti in range(ntile):
                c0 = ti * TW
                for br in range(2):
                    def mm_u(e, s=s, c0=c0, br=br):
                        for ch in range(8):
                            ins = e.matmul(pU[br][:, 0:TW], wbb[br][s][:, ch, :], og[:, br * 8 + ch, c0:c0 + TW],
                                           start=(ch == 0), stop=(ch == 7))
                        return ins
                    P.pe(mm_u, reads=[("wbb", br, s)] + OG, writes=[("pU", br)])

                    def mm_m(e, s=s, c0=c0, br=br):
                        for ch in range(DC):
                            ins = e.matmul(pM[br][:, 0:TW], wmb[br][s][:, ch, :], hT[:, ch, c0:c0 + TW],
                                           start=(ch == 0), stop=(ch == DC - 1))
                        return ins
                    P.pe(mm_m, reads=[("wmb", br, s)] + HT, writes=[("pM", br)])
                    P.act(lambda e, br=br: e.activation(sig[br][:], pM[br][:, 0:TW], AF.Sigmoid),
                          reads=[("pM", br)], writes=[("sig", br)])
                    P.dve(lambda e, br=br: e.tensor_tensor(t1[br][:], pU[br][:, 0:TW], sig[br][:], ALU.mult),
                          reads=[("pU", br), ("sig", br)], writes=[("t1", br)])
                P.dve(lambda e, cb=cb, c0=c0: e.tensor_tensor(mg[:, cb, c0:c0 + TW], t1[0][:], t1[1][:], ALU.add),
                      reads=[("t1", 0), ("t1", 1)], writes=[("mg", cb, ti)])
        MG = [("mg", cb, ti) for cb in range(DC) for ti in range(ntile)]

        tblocks = [(i * 128, 128) for i in range(NPBK)] + [(NPP, 32)]
        for cbk in range(4):
            s = cbk % 2
            for c2 in range(DC // 2):
                P.dma("pool", wob[s][:, 2 * c2:2 * c2 + 2, :], wo[cbk, :, 2 * c2:2 * c2 + 2, :],
                      writes=[("wob", s, c2)], semkey=("wob", s))
            WO = [("wob", s, c2) for c2 in range(DC // 2)]
            for bi, (r0, nr) in enumerate(tblocks):
                pi = bi % 2
                P.dma("sp", xres[pi][0:nr, :], xm[r0:r0 + nr, cbk * 512:(cbk + 1) * 512],
                      writes=[("xres", pi)], semkey=("xres", pi))

                def mm_y(e, s=s, r0=r0, nr=nr, pi=pi):
                    for ch in range(DC):
                        ins = e.matmul(pY[pi][0:nr, :], mg[:, ch, r0:r0 + nr], wob[s][:, ch, :],
                                       start=(ch == 0), stop=(ch == DC - 1))
                    return ins
                P.pe(mm_y, reads=WO + MG, writes=[("pY", pi)])
                P.dve(lambda e, nr=nr, pi=pi: e.tensor_tensor(yst[pi][0:nr, :], pY[pi][0:nr, :],
                                                              xres[pi][0:nr, :], ALU.add),
                      reads=[("pY", pi), ("xres", pi)], writes=[("yst", pi)])
                P.dma("sp", o_y[r0:r0 + nr, cbk * 512:(cbk + 1) * 512], yst[pi][0:nr, :],
                      reads=[("yst", pi)], semkey=("yst", pi))
        P.emit()


def _consts():
    p = np.arange(128)[:, None]
    c = np.arange(128)[None, :]
    cst = np.zeros((128, 4, 128), np.float32)
    cst[:, 0, :] = 1.0
    cst[:, 1, :] = (p <= c)
    cst[:, 2, :] = (p >= c)
    cst[:, 3, :] = (p < c)
    return cst


_WA_BLOCKS = [0, 1, 4, 5]


def kernel(x_prompt, x_sample, cache_sb_k, cache_sb_v, cache_fox_k, cache_fox_v, cache_fox_logf,
           norm_w, w_in, b_forget, q_norm_w, k_norm_w, w_branch_sb, w_branch_fox, w_out):
    f32 = np.float32
    x_prompt = np.asarray(x_prompt, f32)
    x_sample = np.asarray(x_sample, f32)
    T_P = x_prompt.shape[1]
    PAST = cache_sb_k.shape[2]
    NTOK = T_P + NSAMP
    NPB = PAST // 128
    NB = T_P // 128
    w_in = np.asarray(w_in, f32)[0]
    cst = _consts()
    nwl = np.ascontiguousarray(np.asarray(norm_w, f32)[0].reshape(DC, 128).T)

    x_all = np.concatenate([x_prompt[0], x_sample.reshape(NSAMP, D)], 0)
    NT_ = NTOK // 256
    xT = np.ascontiguousarray(
        x_all.reshape(NT_, 256, DC, 128).transpose(0, 3, 2, 1)).reshape(NT_, 128, DC * 256)
    csk = np.asarray(cache_sb_k, f32)[0]
    csv = np.asarray(cache_sb_v, f32)[0]
    cfk = np.asarray(cache_fox_k, f32)[0]
    cfv = np.asarray(cache_fox_v, f32)[0]
    clf = np.asarray(cache_fox_logf, f32)[0]
    in_maps = []
    for c in range(NCORES):
        cols = []
        for blk in _WA_BLOCKS:
            cols.append(w_in[:, blk * 1024 + c * 128: blk * 1024 + (c + 1) * 128])
        cols.append(w_in[:, 2 * 1024 + c * 128: 2 * 1024 + (c + 1) * 128])
        cols.append(w_in[:, 6 * 1024 + c * 128: 6 * 1024 + (c + 1) * 128])
        cols.append(w_in[:, 8 * 1024 + c: 8 * 1024 + c + 1])
        wa = np.concatenate(cols, 1)
        wa = np.ascontiguousarray(wa.reshape(DC, 128, 769).transpose(1, 0, 2))
        vec = np.stack([np.full(128, np.asarray(b_forget, f32)[0, c], f32),
                        np.asarray(q_norm_w, f32)[0], np.asarray(k_norm_w, f32)[0]], 1)
        in_maps.append({
            "xT": xT, "wa": wa, "nw": nwl, "vec": np.ascontiguousarray(vec), "cst": cst,
            "cskT": np.ascontiguousarray(csk[:, :, c, :].transpose(0, 2, 1)),
            "cfkT": np.ascontiguousarray(cfk[:, :, c, :].transpose(0, 2, 1)),
            "csv": np.ascontiguousarray(
                csv[:, :, c, :].reshape(B_S, NPB, 128, 128).transpose(0, 2, 1, 3)).reshape(B_S, 128, NPB * 128),
            "cfv": np.ascontiguousarray(
                cfv[:, :, c, :].reshape(B_S, NPB, 128, 128).transpose(0, 2, 1, 3)).reshape(B_S, 128, NPB * 128),
            "clf": np.ascontiguousarray(clf[:, :, c].reshape(B_S, NPB, 128).transpose(2, 0, 1)),
        })
    nc = bass.Bass("TRN2", target_bir_lowering=False)
    o_og = nc.dram_tensor("o_og", [2, 128, NTOK // 2], F32, kind="ExternalOutput").ap()
    build_A(nc, T_P, PAST, lambda br: o_og[br])
    resA = run_bass_kernel_spmd(nc, in_maps, core_ids=list(range(NCORES))).results

    p_sb_k = np.zeros((1, 1, T_P, 8, 128), f32); s_sb_k = np.zeros((1, B_S, T_S, 8, 128), f32)
    p_sb_v = np.zeros_like(p_sb_k); s_sb_v = np.zeros_like(s_sb_k)
    p_fx_k = np.zeros_like(p_sb_k); s_fx_k = np.zeros_like(s_sb_k)
    p_fx_v = np.zeros_like(p_sb_k); s_fx_v = np.zeros_like(s_sb_k)
    p_lf = np.zeros((1, 1, T_P, 8), f32); s_lf = np.zeros((1, B_S, T_S, 8), f32)
    og_all = np.zeros((2, 8, 128, NTOK), np.uint16)
    for c in range(NCORES):
        r = resA[c]
        kT = np.asarray(r["o_kT"])
        v = np.asarray(r["o_v"])
        p_sb_k[0, 0, :, c, :] = kT[0, :, :T_P].T
        s_sb_k[0, :, :, c, :] = kT[0, :, T_P:].T.reshape(B_S, T_S, 128)
        p_fx_k[0, 0, :, c, :] = kT[1, :, :T_P].T
        s_fx_k[0, :, :, c, :] = kT[1, :, T_P:].T.reshape(B_S, T_S, 128)
        p_sb_v[0, 0, :, c, :] = v[:T_P, 0:128]
        s_sb_v[0, :, :, c, :] = v[T_P:, 0:128].reshape(B_S, T_S, 128)
        p_fx_v[0, 0, :, c, :] = v[:T_P, 128:256]
        s_fx_v[0, :, :, c, :] = v[T_P:, 128:256].reshape(B_S, T_S, 128)
        p_lf[0, 0, :, c] = np.asarray(r["o_lf"]).T.reshape(T_P)
        s_lf[0, :, :, c] = np.asarray(r["o_lfs"]).T
        og_all[:, c] = np.ascontiguousarray(np.asarray(r["o_og"])).view(np.uint16).reshape(2, 128, NTOK)

    NPP = T_P // NCORES
    NPC = NPP + 32
    wg = np.concatenate([w_in[:, 3 * 1024:4 * 1024], w_in[:, 7 * 1024:8 * 1024]], 1)
    wg = np.ascontiguousarray(wg.reshape(DC, 128, DC, 128).transpose(2, 1, 0, 3))
    m0 = 8 * 1024 + 8
    wm = np.stack([w_in[:, m0:m0 + D], w_in[:, m0 + D:m0 + 2 * D]], 0)
    wm = np.ascontiguousarray(wm.reshape(2, DC, 128, DC, 128).transpose(0, 3, 2, 1, 4))
    wb = np.stack([np.asarray(w_branch_sb, f32)[0], np.asarray(w_branch_fox, f32)[0]], 0)
    wb = np.ascontiguousarray(wb.reshape(2, 8, 128, DC, 128).transpose(0, 3, 2, 1, 4))
    wo = np.asarray(w_out, f32)[0]
    wo = np.ascontiguousarray(wo.reshape(DC, 128, 4, 512).transpose(2, 1, 0, 3))
    in_maps = []
    for c in range(NCORES):
        tok = np.concatenate([np.arange(c * NPP, (c + 1) * NPP), T_P + np.arange(c * 32, (c + 1) * 32)])
        xm = np.ascontiguousarray(x_all[tok])
        xmT = np.ascontiguousarray(xm.T.reshape(DC, 128, NPC).transpose(1, 0, 2))
        ogc = np.ascontiguousarray(og_all[:, :, :, tok].reshape(DC, 128, NPC)).view(np.float32)
        in_maps.append({"xmT": xmT, "xm": xm, "nw": nwl, "cst": cst, "wg": wg, "wm": wm, "wb": wb,
                        "wo": wo, "ogin": np.ascontiguousarray(ogc)})
    nc2 = bass.Bass("TRN2", target_bir_lowering=False)
    ogin = nc2.dram_tensor("ogin", [DC, 128, NPC // 2], F32, kind="ExternalInput").ap()
    build_C(nc2, NPC, lambda ch: ogin[ch])
    resC = run_bass_kernel_spmd(nc2, in_maps, core_ids=list(range(NCORES))).results
    y_p = np.zeros((1, T_P, D), f32)
    y_s = np.zeros((NSAMP, D), f32)
    for c in range(NCORES):
        y = np.asarray(resC[c]["o_y"])
        y_p[0, c * NPP:(c + 1) * NPP] = y[:NPP]
        y_s[c * 32:(c + 1) * 32] = y[NPP:]
    y_s = y_s.reshape(B_S, T_S, D)
    return (y_p, y_s, p_sb_k, p_sb_v, p_fx_k, p_fx_v, p_lf, s_sb_k, s_sb_v, s_fx_k, s_fx_v, s_lf)
```

```python
import contextlib
import numpy as np
import concourse.bass as bass
import concourse.mybir as mybir
from concourse.bass_utils import run_bass_kernel_spmd

F32 = mybir.dt.float32
BF16 = mybir.dt.bfloat16
U32 = mybir.dt.uint32
AF = mybir.ActivationFunctionType
ALU = mybir.AluOpType

NCORES = 8
D = 2048
DC = 16
HD = 128
B_S = 16
T_S = 16
NSAMP = B_S * T_S
EPS = 1e-6


class Prog:
    ENGS = ("pe", "act", "dve", "pool", "sp")

    def __init__(self, nc, name="p"):
        self.nc = nc
        self.name = name
        self.ops = []

    def add(self, eng, fn, reads=(), writes=(), semkey=None, sem_inc=16):
        assert eng in self.ENGS
        self.ops.append(
            dict(eng=eng, fn=fn, reads=list(reads), writes=list(writes), semkey=semkey,
                 sem_inc=sem_inc)
        )

    def pe(self, fn, reads=(), writes=()):
        self.add("pe", fn, reads, writes)

    def act(self, fn, reads=(), writes=()):
        self.add("act", fn, reads, writes)

    def dve(self, fn, reads=(), writes=()):
        self.add("dve", fn, reads, writes)

    def pool(self, fn, reads=(), writes=()):
        self.add("pool", fn, reads, writes)

    def dma(self, q, out, in_, reads=(), writes=(), semkey=None, **kw):
        assert semkey is not None
        self.add(
            q,
            lambda e, out=out, in_=in_, kw=kw: e.dma_start(out=out, in_=in_, **kw),
            reads, writes, semkey,
        )

    def emit(self):
        nc = self.nc
        import os as _os
        mx = _os.environ.get("MAXOPS_" + self.name)
        if mx is not None:
            self.ops = self.ops[:int(mx)]
        ops = self.ops
        print("emit", self.name, "nops", len(ops), flush=True)
        if not ops:
            return
        last_writer = {}
        readers = {}
        for i, op in enumerate(ops):
            deps = set()
            for k in op["reads"]:
                if k in last_writer:
                    deps.add(last_writer[k])
                if _is_psum_key(k):
                    for r_ in readers.get(k, ()):
                        if ops[r_]["eng"] != op["eng"]:
                            deps.add(r_)
            for k in op["writes"]:
                if k in last_writer:
                    deps.add(last_writer[k])
                deps.update(readers.get(k, ()))
            deps.discard(i)
            if op["eng"] == "pe" and op["semkey"] is None:
                deps = {d for d in deps
                        if not (ops[d]["eng"] == "pe" and ops[d]["semkey"] is None)}
            op["deps"] = deps
            for k in op["reads"]:
                readers.setdefault(k, []).append(i)
            for k in op["writes"]:
                last_writer[k] = i
                readers[k] = []
        has_dep = [False] * len(ops)
        for op in ops:
            for d in op["deps"]:
                has_dep[d] = True
        eng_cnt = {e: 0 for e in self.ENGS}
        dma_cnt = {}
        for i, op in enumerate(ops):
            if op["semkey"] is not None:
                k = op["semkey"]
                dma_cnt[k] = dma_cnt.get(k, 0) + op["sem_inc"]
                op["sig"] = (("dma", k), dma_cnt[k])
            elif has_dep[i]:
                eng_cnt[op["eng"]] += 1
                op["sig"] = (("eng", op["eng"]), eng_cnt[op["eng"]])
            else:
                op["sig"] = None
        semnames = [("eng", e) for e in self.ENGS] + [("dma", k) for k in dma_cnt]
        with contextlib.ExitStack() as es:
            sems = {}
            for j, sn in enumerate(semnames):
                sems[sn] = es.enter_context(nc.semaphore(f"{self.name}_s{j}"))
            block = es.enter_context(nc.Block())
            per_eng = {e: [op for op in ops if op["eng"] == e] for e in self.ENGS}
            dma_issuer = {}
            for op in ops:
                if op["semkey"] is not None:
                    dma_issuer[op["semkey"]] = op["eng"]

            def run_engine(engname, e):
                sat = {}
                for op in per_eng[engname]:
                    need = {}
                    for d in op["deps"]:
                        sn, c = ops[d]["sig"]
                        if c > need.get(sn, 0):
                            need[sn] = c
                    for sn, c in need.items():
                        if sat.get(sn, 0) < c:
                            e.wait_ge(sems[sn], c)
                            sat[sn] = c
                    ins = op["fn"](e)
                    if op["sig"] is not None:
                        sn, c = op["sig"]
                        if sn[0] == "dma":
                            if op["sem_inc"] == 1:
                                ins.then_inc(sems[sn])
                            else:
                                ins.then_inc(sems[sn], op["sem_inc"])
                        else:
                            ins.then_inc(sems[sn], 1)
                for k, tot in dma_cnt.items():
                    if dma_issuer[k] == engname:
                        sn = ("dma", k)
                        if sat.get(sn, 0) < tot:
                            e.wait_ge(sems[sn], tot)

            @block.tensor
            def _(e):
                run_engine("pe", e)

            @block.scalar
            def _(e):
                run_engine("act", e)

            @block.vector
            def _(e):
                run_engine("dve", e)

            @block.gpsimd
            def _(e):
                run_engine("pool", e)

            @block.sync
            def _(e):
                run_engine("sp", e)


_PSUM_NAMES = {"misc_ps", "ss_ps", "psf", "psv", "ssq_ps", "z_ps", "c_ps", "o_ps", "den_ps", "pA", "pU", "pM", "pY"}


def _is_psum_key(k):
    n = k[0] if isinstance(k, tuple) else k
    return n in _PSUM_NAMES


def _bc_mid(ap, n):
    p, f = ap.shape
    return ap.unsqueeze(1).broadcast_to([p, n, f])


def build_A(nc, T_P, PAST, og_dst):
    NTOK = T_P + NSAMP
    NT = NTOK // 256
    NB = T_P // 128
    NPB = PAST // 128
    NQT = T_P // 512
    SC = float(HD) ** -0.5

    def din(name, shape, dt=F32):
        return nc.dram_tensor(name, shape, dt, kind="ExternalInput").ap()

    def dout(name, shape, dt=F32):
        return nc.dram_tensor(name, shape, dt, kind="ExternalOutput").ap()

    xTp = din("xTp", [T_P // 512, 128, DC * 512])
    xTs = din("xTs", [128, DC * NSAMP])
    wa = din("wa", [128, DC, 769])
    nwd = din("nw", [128, DC])
    vecd = din("vec", [128, 3])
    cstd = din("cst", [128, 4, 128])
    cskT = din("cskT", [B_S, 128, PAST])
    cfkT = din("cfkT", [B_S, 128, PAST])
    csv = din("csv", [B_S, 128, NPB * 128])
    cfv = din("cfv", [B_S, 128, NPB * 128])
    clfd = din("clf", [128, B_S, NPB])

    o_kT = dout("o_kT", [2, 128, NTOK])
    o_v = dout("o_v", [NTOK, 256])
    o_lf = dout("o_lf", [128, NB])
    o_lfs = dout("o_lfs", [T_S, B_S])

    es = contextlib.ExitStack()
    with es:
        def sb(name, shape, dt):
            return es.enter_context(nc.sbuf_tensor("sA_" + name, shape, dt))

        def ps(name, shape, dt=F32):
            return es.enter_context(nc.psum_tensor("pA_" + name, shape, dt))

        qsT = sb("qsT", [128, NTOK], BF16)
        ksT = sb("ksT", [128, NTOK], BF16)
        qfT = sb("qfT", [128, NTOK], BF16)
        kfT = sb("kfT", [128, NTOK], BF16)
        vv = sb("vv", [128, NB, 256], BF16)
        vnb = sb("vnb", [T_S, B_S, 256], BF16)
        cst32 = sb("cst32", [128, 4, 128], F32)
        ones_bf = sb("ones_bf", [128, 128], BF16)
        tge_bf = sb("tge_bf", [128, 128], BF16)
        vec = sb("vec", [128, 3], F32)
        negb = sb("negb", [128, 1], F32)
        kcol = sb("kcol", [128, 1], F32)
        lfcol = sb("lfcol", [128, NB], F32)
        lfs = sb("lfs", [T_S, B_S], F32)
        fcum = sb("fcum", [128, NB], F32)
        incl = sb("incl", [128, NB], F32)
        biasp = sb("biasp", [128, B_S, NPB], F32)
        biasn = sb("biasn", [T_S, B_S], F32)
        ones32 = cst32[:, 0, :]
        mle32 = cst32[:, 1, :]
        mlt32 = cst32[:, 3, :]

        es1 = contextlib.ExitStack()
        with es1:
            def sb1(name, shape, dt):
                return es1.enter_context(nc.sbuf_tensor("sA1_" + name, shape, dt))

            def ps1(name, shape, dt=F32):
                return es1.enter_context(nc.psum_tensor("pA1_" + name, shape, dt))

            wbf = sb1("wbf", [128, DC, 784], BF16)
            wst1 = sb1("wst0", [128, 769], F32)
            wst = [wst1, wst1]
            nw = sb1("nw", [128, DC], F32)
            nws = sb1("nws", [128, DC], F32)
            xt = [sb1(f"xt{i}", [128, DC, 512], BF16) for i in range(2)]
            sqh = [sb1(f"sqh{i}", [128, 4, 512], BF16) for i in range(2)]
            rr = sb1("rr", [128, 512], F32)
            rr2 = sb1("rr2", [128, 512], F32)
            sqq = sb1("sqq", [128, 1024], BF16)
            rq = sb1("rq", [128, 1024], F32)
            kst32 = sb1("kst32", [128, 512], F32)
            kf32 = sb1("kf32", [128, 512], F32)
            rcol = sb1("rcol", [128, 2], F32)
            v32 = [sb1(f"v32_{i}", [128, 256], F32) for i in range(2)]
            vn32 = [sb1(f"vn32_{i}", [T_S, 256], F32) for i in range(2)]
            fraw = sb1("fraw", [128, NB], F32)
            fraws = sb1("fraws", [T_S, B_S], F32)
            ftmp = sb1("ftmp", [128, NB], F32)
            ftmps = sb1("ftmps", [T_S, B_S], F32)
            ss_ps = ps1("ss_ps", [128, 512])
            psf = [ps1(f"psf{i}", [128, 512]) for i in range(4)]
            psv = [ps1(f"psv{i}", [128, 512]) for i in range(2)]
            ssq_ps = ps1("ssq_ps", [128, 512])

            P = Prog(nc, "a1")
            P.dma("sp", cst32[:], cstd, writes=["cst32"], semkey="cst32")
            P.dma("sp", vec[:], vecd, writes=["vec"], semkey="vec")
            P.dma("sp", nw[:], nwd, writes=["nw"], semkey="nw")
            P.act(lambda e: e.activation(ones_bf[:], cst32[:, 0, :], AF.Copy),
                  reads=["cst32"], writes=["ones_bf"])
            P.act(lambda e: e.activation(tge_bf[:], cst32[:, 2, :], AF.Copy),
                  reads=["cst32"], writes=["tge_bf"])
            P.dve(lambda e: e.tensor_scalar(negb[:], vec[:, 0:1], -1.0, None, ALU.mult),
                  reads=["vec"], writes=["negb"])
            P.dve(lambda e: e.tensor_scalar(kcol[:], vec[:, 2:3], float(HD) ** 0.5, None, ALU.mult),
                  reads=["vec"], writes=["kcol"])
            P.dve(lambda e: e.tensor_scalar(nws[:], nw[:], float(D) ** 0.5, None, ALU.mult),
                  reads=["nw"], writes=["nws"])
            for ch in range(DC):
                s = ch % 2
                P.dma("sp", wst[s][:], wa[:, ch, :], writes=[("wst", 0)], semkey=("wst", 0))
                if ch % 2 == 0:
                    P.dve(lambda e, ch=ch, s=s: e.tensor_scalar(
                        wbf[:, ch, 0:769], wst[s][:], nws[:, ch:ch + 1], None, ALU.mult),
                        reads=[("wst", 0), "nws"], writes=[("wbf", ch)])
                else:
                    P.act(lambda e, ch=ch, s=s: e.activation(
                        wbf[:, ch, 0:769], wst[s][:], AF.Copy, scale=nws[:, ch:ch + 1]),
                        reads=[("wst", 0), "nws"], writes=[("wbf", ch)])
            WB = [("wbf", ch) for ch in range(DC)]


            tiles = [(i * 512, 512) for i in range(T_P // 512)] + [(T_P, NSAMP)]
            one1 = cst32[0:1, 0, 0:1]
            for ti, (t0, n) in enumerate(tiles):
                s = ti % 2
                is_samp = t0 == T_P

                def kt(name, t0=t0, n=n):
                    return [(name, t) for t in range(t0 // 256, (t0 + n) // 256)]
                if not is_samp:
                    for g in range(4):
                        P.dma("pool", xt[s][:, 4 * g:4 * g + 4, :].rearrange("p c t -> p (c t)"),
                              xTp[ti, :, g * 2048:(g + 1) * 2048], writes=[("xt", s)], semkey=("xt", s))
                else:
                    for g in range(2):
                        P.dma("pool", xt[s][:, 8 * g:8 * g + 8, 0:n],
                              xTs[:, g * 8 * n:(g + 1) * 8 * n].rearrange("p (c t) -> p c t", t=n),
                              writes=[("xt", s)], semkey=("xt", s))
                for g in range(4):
                    sg = g % 2
                    P.act(lambda e, s=s, g=g, sg=sg, n=n: e.activation(sqh[sg][:, :, 0:n], xt[s][:, 4 * g:4 * g + 4, 0:n],
                                                                     AF.Square),
                          reads=[("xt", s)], writes=[("sqh", sg)])

                    def mm_ss(e, g=g, sg=sg, n=n):
                        for c in range(4):
                            ins = e.matmul(ss_ps[:, 0:n], ones_bf[:], sqh[sg][:, c, 0:n],
                                           start=(g == 0 and c == 0), stop=(g == 3 and c == 3))
                        return ins
                    P.pe(mm_ss, reads=[("sqh", sg), "ones_bf"], writes=["ss_ps"])
                P.act(lambda e, n=n: e.activation(rr[:, 0:n], ss_ps[:, 0:n], AF.Ln, bias=float(D) * EPS),
                      reads=["ss_ps"], writes=["rr"])
                P.act(lambda e, n=n: e.activation(rr[:, 0:n], rr[:, 0:n], AF.Exp, scale=-0.5),
                      reads=["rr"], writes=["rr"])
                P.dve(lambda e, n=n: e.tensor_tensor(rr2[:, 0:n], rr[:, 0:n], rr[:, 0:n], ALU.mult),
                      reads=["rr"], writes=["rr2"])
                for cb in range(4):
                    def mm_f(e, cb=cb, s=s, n=n):
                        for ch in range(DC):
                            ins = e.matmul(psf[cb][:, 0:n], wbf[:, ch, cb * 128:(cb + 1) * 128], xt[s][:, ch, 0:n],
                                           start=(ch == 0), stop=(ch == DC - 1))
                        return ins
                    P.pe(mm_f, reads=[("xt", s)] + WB, writes=[("psf", cb)])
                P.dve(lambda e, t0=t0, n=n: e.scalar_tensor_tensor(
                    qsT[:, t0:t0 + n], psf[0][:, 0:n], SC, rr[:, 0:n], ALU.mult, ALU.mult),
                    reads=[("psf", 0), "rr"], writes=kt("qsT"))
                P.dve(lambda e, n=n: e.tensor_tensor(kst32[:, 0:n], psf[1][:, 0:n], rr[:, 0:n], ALU.mult),
                      reads=[("psf", 1), "rr"], writes=["kst32"])
                P.act(lambda e, t0=t0, n=n: e.activation(ksT[:, t0:t0 + n], kst32[:, 0:n], AF.Copy),
                      reads=["kst32"], writes=kt("ksT"))
                P.dma("sp", o_kT[0, :, t0:t0 + n], kst32[:, 0:n], reads=["kst32"], semkey="kst32")
                P.act(lambda e, n=n: e.activation(sqq[:, 0:n], psf[2][:, 0:n], AF.Square),
                      reads=[("psf", 2)], writes=["sqq"])
                P.act(lambda e, n=n: e.activation(sqq[:, 512:512 + n], psf[3][:, 0:n], AF.Square),
                      reads=[("psf", 3)], writes=["sqq"])

                def mm_q(e, n=n):
                    e.matmul(ss_ps[:, 0:n], ones_bf[:], sqq[:, 0:n], start=True, stop=True)
                    return e.matmul(ssq_ps[:, 0:n], ones_bf[:], sqq[:, 512:512 + n], start=True, stop=True)
                P.pe(mm_q, reads=["sqq", "ones_bf"], writes=["ss_ps", "ssq_ps"])
                P.dve(lambda e, n=n: e.tensor_tensor(rq[:, 0:n], ss_ps[:, 0:n], rr2[:, 0:n], ALU.mult),
                      reads=["ss_ps", "rr2"], writes=["rq"])
                P.dve(lambda e, n=n: e.tensor_tensor(rq[:, 512:512 + n], ssq_ps[:, 0:n], rr2[:, 0:n], ALU.mult),
                      reads=["ssq_ps", "rr2", "rq"], writes=["rq"])
                for h0 in (0, 512):
                    P.act(lambda e, n=n, h0=h0: e.activation(rq[:, h0:h0 + n], rq[:, h0:h0 + n], AF.Ln,
                                                             bias=float(HD) * EPS),
                          reads=["rq"], writes=["rq"])
                    P.act(lambda e, n=n, h0=h0: e.activation(rq[:, h0:h0 + n], rq[:, h0:h0 + n], AF.Exp, scale=-0.5),
                          reads=["rq"], writes=["rq"])
                    P.dve(lambda e, n=n, h0=h0: e.tensor_tensor(rq[:, h0:h0 + n], rq[:, h0:h0 + n], rr[:, 0:n],
                                                                ALU.mult),
                          reads=["rq", "rr"], writes=["rq"])
                P.dve(lambda e, t0=t0, n=n: e.scalar_tensor_tensor(
                    qfT[:, t0:t0 + n], psf[2][:, 0:n], vec[:, 1:2], rq[:, 0:n], ALU.mult, ALU.mult),
                    reads=[("psf", 2), "rq", "vec"], writes=kt("qfT"))
                P.dve(lambda e, n=n: e.scalar_tensor_tensor(
                    kf32[:, 0:n], psf[3][:, 0:n], kcol[:, 0:1], rq[:, 512:512 + n], ALU.mult, ALU.mult),
                    reads=[("psf", 3), "rq", "kcol"], writes=["kf32"])
                P.act(lambda e, t0=t0, n=n: e.activation(kfT[:, t0:t0 + n], kf32[:, 0:n], AF.Copy),
                      reads=["kf32"], writes=kt("kfT"))
                P.dma("sp", o_kT[1, :, t0:t0 + n], kf32[:, 0:n], reads=["kf32"], semkey="kf32")

                if not is_samp:
                    for tb in range(n // 128):
                        pb = tb % 2
                        blk = t0 // 128 + tb

                        def mm_v(e, tb=tb, pb=pb, s=s):
                            e.matmul(psv[pb][:, 300:301], rr[0:1, tb * 128:(tb + 1) * 128], one1,
                                     start=True, stop=True)
                            for ch in range(DC):
                                ins = e.matmul(psv[pb][:, 0:257], xt[s][:, ch, tb * 128:(tb + 1) * 128],
                                               wbf[:, ch, 512:769], start=(ch == 0), stop=(ch == DC - 1))
                            return ins
                        P.pe(mm_v, reads=[("xt", s), "rr", "cst32"] + WB, writes=[("psv", pb)])
                        P.act(lambda e, pb=pb: e.activation(rcol[:, pb:pb + 1], psv[pb][:, 300:301], AF.Copy),
                              reads=[("psv", pb)], writes=[("rcol", pb)])
                        P.act(lambda e, pb=pb, blk=blk: e.activation(vv[:, blk, :], psv[pb][:, 0:256], AF.Copy,
                                                                    scale=rcol[:, pb:pb + 1]),
                              reads=[("psv", pb), ("rcol", pb)], writes=[("vv", blk)])
                        P.dve(lambda e, pb=pb: e.tensor_scalar(v32[pb][:], psv[pb][:, 0:256], rcol[:, pb:pb + 1], None,
                                                               ALU.mult),
                              reads=[("psv", pb), ("rcol", pb)], writes=[("v32", pb)])
                        P.dve(lambda e, pb=pb, blk=blk: e.tensor_scalar(fraw[:, blk:blk + 1], psv[pb][:, 256:257],
                                                                        rcol[:, pb:pb + 1], None, ALU.mult),
                              reads=[("psv", pb), ("rcol", pb)], writes=[("fraw", blk)])
                        P.dma("sp", o_v[t0 + tb * 128:t0 + (tb + 1) * 128, :], v32[pb][:],
                              reads=[("v32", pb)], semkey=("v32", pb))
                else:
                    for b in range(B_S):
                        pb = b % 2

                        def mm_vs(e, b=b, pb=pb, s=s):
                            e.matmul(psv[pb][0:T_S, 300:301], rr[0:1, b * T_S:(b + 1) * T_S], one1,
                                     start=True, stop=True)
                            for ch in range(DC):
                                ins = e.matmul(psv[pb][0:T_S, 0:257], xt[s][:, ch, b * T_S:(b + 1) * T_S],
                                               wbf[:, ch, 512:769], start=(ch == 0), stop=(ch == DC - 1))
                            return ins
                        P.pe(mm_vs, reads=[("xt", s), "rr", "cst32"] + WB, writes=[("psv", pb)])
                        P.act(lambda e, pb=pb: e.activation(rcol[0:T_S, pb:pb + 1], psv[pb][0:T_S, 300:301], AF.Copy),
                              reads=[("psv", pb)], writes=[("rcol", pb)])
                        P.act(lambda e, b=b, pb=pb: e.activation(vnb[:, b, :], psv[pb][0:T_S, 0:256], AF.Copy,
                                                                 scale=rcol[0:T_S, pb:pb + 1]),
                              reads=[("psv", pb), ("rcol", pb)], writes=[("vnb", b)])
                        P.dve(lambda e, pb=pb: e.tensor_scalar(vn32[pb][:], psv[pb][0:T_S, 0:256],
                                                               rcol[0:T_S, pb:pb + 1], None, ALU.mult),
                              reads=[("psv", pb), ("rcol", pb)], writes=[("vn32", pb)])
                        P.dve(lambda e, b=b, pb=pb: e.tensor_scalar(fraws[:, b:b + 1], psv[pb][0:T_S, 256:257],
                                                                    rcol[0:T_S, pb:pb + 1], None, ALU.mult),
                              reads=[("psv", pb), ("rcol", pb)], writes=[("fraws", b)])
                        P.dma("sp", o_v[T_P + b * T_S:T_P + (b + 1) * T_S, :], vn32[pb][:],
                              reads=[("vn32", pb)], semkey=("vn32", pb))


            FR = [("fraw", blk) for blk in range(NB)]
            P.act(lambda e: e.activation(ftmp[:], fraw[:], AF.Exp, bias=negb[:, 0:1], scale=-1.0),
                  reads=FR + ["negb"], writes=["ftmp"])
            P.act(lambda e: e.activation(ftmp[:], ftmp[:], AF.Ln, bias=1.0), reads=["ftmp"], writes=["ftmp"])
            P.dve(lambda e: e.tensor_scalar(lfcol[:], ftmp[:], -1.0, None, ALU.mult),
                  reads=["ftmp"], writes=["lfcol"])
            P.dma("sp", o_lf, lfcol[:], reads=["lfcol"], semkey="o_lf")
            FRS = [("fraws", b) for b in range(B_S)]
            P.act(lambda e: e.activation(ftmps[:], fraws[:], AF.Exp, bias=negb[0:T_S, 0:1], scale=-1.0),
                  reads=FRS + ["negb"], writes=["ftmps"])
            P.act(lambda e: e.activation(ftmps[:], ftmps[:], AF.Ln, bias=1.0), reads=["ftmps"], writes=["ftmps"])
            P.dve(lambda e: e.tensor_scalar(lfs[:], ftmps[:], -1.0, None, ALU.mult),
                  reads=["ftmps"], writes=["lfs"])
            P.dma("sp", o_lfs, lfs[:], reads=["lfs"], semkey="o_lfs")
            P.emit()

        import os as _os
        if _os.environ.get("STOP_A1"):
            return
        es2 = contextlib.ExitStack()
        with es2:
            def sb2(name, shape, dt):
                return es2.enter_context(nc.sbuf_tensor("sA2_" + name, shape, dt))

            def ps2(name, shape, dt=F32):
                return es2.enter_context(nc.psum_tensor("pA2_" + name, shape, dt))

            tot = sb2("tot", [128, NB], F32)
            onesr = sb2("onesr", [128, NB], F32)
            tmpf = sb2("tmpf", [128, NB], F32)
            clf = sb2("clf", [128, B_S, NPB], F32)
            tots = sb2("tots", [128, B_S * NPB], F32)
            segm = sb2("segm", [128, B_S, NPB], F32)
            incls = sb2("incls", [128, B_S * NPB], F32)
            fcums = sb2("fcums", [128, B_S, NPB], F32)
            fref = sb2("fref", [128, B_S], F32)
            cumn = sb2("cumn", [T_S, B_S], F32)
            biasq = [sb2(f"biasq{i}", [128, NB], F32) for i in range(2)]
            e_sb = [sb2(f"e_sb{i}", [128, 512], BF16) for i in range(2)]
            sp_sb = [sb2(f"sp_sb{i}", [128, 512], BF16) for i in range(2)]
            S_sb = [sb2(f"S_sb{i}", [128, 512], BF16) for i in range(2)]
            X_sb = [sb2(f"X_sb{i}", [128, 512], BF16) for i in range(2)]
            W_sb = [sb2(f"W_sb{i}", [128, 512], BF16) for i in range(2)]
            P_fx = [sb2(f"P_fx{i}", [128, 512], BF16) for i in range(2)]
            rden = sb2("rden", [128, 512], F32)
            ogs = [sb2(f"ogs{i}", [128, 512], BF16) for i in range(2)]
            ogf = [sb2(f"ogf{i}", [128, 512], BF16) for i in range(2)]
            ogsamp = [sb2(f"ogsamp{i}", [128, NSAMP], BF16) for i in range(2)]
            ckT = [[sb2(f"ckT{br}_{i}", [128, PAST], BF16) for i in range(2)] for br in range(2)]
            cvv = [[sb2(f"cvv{br}_{i}", [128, NPB, 128], BF16) for i in range(2)] for br in range(2)]
            z_ps = [ps2(f"z_ps{i}", [128, 512]) for i in range(2)]
            c_ps = [ps2(f"c_ps{i}", [128, 512]) for i in range(2)]
            o_ps = [ps2(f"o_ps{i}", [128, 512]) for i in range(2)]
            den_ps = ps2("den_ps", [128, 512])
            misc_ps = ps2("misc_ps", [128, 512])
            en_s = sb2("en_s", [T_S, T_S], BF16)
            spn_s = sb2("spn_s", [T_S, T_S], BF16)
            xn_s = sb2("xn_s", [T_S, T_S], BF16)
            wn_s = sb2("wn_s", [T_S, T_S], BF16)
            pn_s = sb2("pn_s", [T_S, T_S], BF16)
            spnb = sb2("spnb", [T_S, 512], BF16)
            tz = rden

            P = Prog(nc, "a2")
            NBS = B_S * NPB
            P.dve(lambda e: e.memset(onesr[:], 1.0), writes=["onesr"])

            def mm_cum(e):
                e.matmul(c_ps[0][:, 0:NB], mle32, lfcol[:], start=True, stop=True)
                return e.matmul(c_ps[1][:, 0:NB], ones32, lfcol[:], start=True, stop=True)
            P.pe(mm_cum, reads=["lfcol", "cst32"], writes=[("c_ps", 0), ("c_ps", 1)])
            P.dve(lambda e: e.tensor_copy(tot[:], c_ps[1][:, 0:NB]), reads=[("c_ps", 1)], writes=["tot"])
            P.dve(lambda e: e.tensor_tensor_scan(incl[:], onesr[:, 0:NB], tot[:], 0.0, ALU.mult, ALU.add),
                  reads=["onesr", "tot"], writes=["incl"])
            P.dve(lambda e: e.tensor_tensor(tmpf[:], incl[:], tot[:], ALU.subtract),
                  reads=["incl", "tot"], writes=["tmpf"])
            P.dve(lambda e: e.tensor_tensor(fcum[:], c_ps[0][:, 0:NB], tmpf[:], ALU.add),
                  reads=[("c_ps", 0), "tmpf"], writes=["fcum"])
            P.dma("sp", clf[:], clfd, writes=["clf"], semkey="clf")
            P.dve(lambda e: e.memset(segm[:], 1.0), writes=["segm"])
            P.dve(lambda e: e.memset(segm[:, :, 0:1], 0.0), reads=["segm"], writes=["segm"])
            clf2 = clf[:].rearrange("p b k -> p (b k)")

            def mm_cums(e):
                e.matmul(c_ps[0][:, 0:NBS], mle32, clf2, start=True, stop=True)
                return e.matmul(c_ps[1][:, 0:NBS], ones32, clf2, start=True, stop=True)
            P.pe(mm_cums, reads=["clf", "cst32"], writes=[("c_ps", 0), ("c_ps", 1)])
            P.dve(lambda e: e.tensor_copy(tots[:], c_ps[1][:, 0:NBS]), reads=[("c_ps", 1)], writes=["tots"])
            P.dve(lambda e: e.tensor_tensor_scan(incls[:], segm[:].rearrange("p b k -> p (b k)"), tots[:], 0.0,
                                                 ALU.mult, ALU.add),
                  reads=["segm", "tots"], writes=["incls"])
            P.dve(lambda e: e.tensor_tensor(tots[:], incls[:], tots[:], ALU.subtract),
                  reads=["incls", "tots"], writes=["tots"])
            P.dve(lambda e: e.tensor_tensor(fcums[:].rearrange("p b k -> p (b k)"), c_ps[0][:, 0:NBS], tots[:],
                                            ALU.add),
                  reads=[("c_ps", 0), "tots"], writes=["fcums"])
            def mm_new(e):
                e.matmul(z_ps[0][0:T_S, 0:B_S], cst32[0:T_S, 1, 0:T_S], lfs[:], start=True, stop=True)
                return e.matmul(z_ps[1][:, 0:B_S], cst32[0:T_S, 0, :], lfs[:], start=True, stop=True)
            P.pe(mm_new, reads=["lfs", "cst32"], writes=[("z_ps", 0), ("z_ps", 1)])
            P.dve(lambda e: e.tensor_copy(cumn[:], z_ps[0][0:T_S, 0:B_S]), reads=[("z_ps", 0)], writes=["cumn"])
            incls3 = incls[:].rearrange("p (b k) -> p b k", b=B_S)
            P.dve(lambda e: e.tensor_tensor(fref[:], z_ps[1][:, 0:B_S], incls3[:, :, NPB - 1], ALU.add),
                  reads=[("z_ps", 1), "incls"], writes=["fref"])
            P.dve(lambda e: e.tensor_tensor(biasp[:], fref[:].unsqueeze(2).broadcast_to([128, B_S, NPB]),
                                            fcums[:], ALU.subtract),
                  reads=["fref", "fcums"], writes=["biasp"])
            P.dve(lambda e: e.tensor_tensor(biasn[:], z_ps[1][0:T_S, 0:B_S], cumn[:], ALU.subtract),
                  reads=[("z_ps", 1), "cumn"], writes=["biasn"])


            ogd = [og_dst(0), og_dst(1)]

            def tkeys(name, lo, hi):
                return [(name, t) for t in range(lo // 256, (hi + 255) // 256)]

            def prompt_items(qt):
                q0 = qt * 512
                nkb = 4 * qt + 4
                its = []
                for j, kb in enumerate(range(nkb - 1, -1, -1)):
                    i = kb - 4 * qt
                    its.append(dict(qt=qt, q0=q0, kb=kb, k0=kb * 128, c0=(128 * i if i > 0 else 0),
                                    diag=(i >= 0), first=(j == 0), last=(kb == 0), idx=j))
                return its

            gcnt = {"sb": 0, "fx": 0}

            def sb_s1(it):
                si = it["g"] % 2
                c0, q0, k0 = it["c0"], it["q0"], it["k0"]
                P.pe(lambda e: e.matmul(z_ps[0][:, c0:512], ksT[:, k0:k0 + 128], qsT[:, q0 + c0:q0 + 512],
                                        start=True, stop=True),
                     reads=tkeys("ksT", k0, k0 + 128) + tkeys("qsT", q0, q0 + 512), writes=[("z_ps", 0)])
                P.act(lambda e: e.activation(e_sb[si][:, c0:512], z_ps[0][:, c0:512], AF.Exp),
                      reads=[("z_ps", 0)], writes=[("e_sb", si)])
                if it["diag"]:
                    P.dve(lambda e: e.tensor_tensor(e_sb[si][:, c0:c0 + 128], e_sb[si][:, c0:c0 + 128],
                                                    cst32[:, 3, :], ALU.mult),
                          reads=[("e_sb", si), "cst32"], writes=[("e_sb", si)])
                P.act(lambda e: e.activation(sp_sb[si][:, c0:512], e_sb[si][:, c0:512], AF.Ln, bias=1.0),
                      reads=[("e_sb", si)], writes=[("sp_sb", si)])

            def sb_s2(it):
                si = it["g"] % 2
                sprev = 1 - si
                c0 = it["c0"]
                first, last = it["first"], it["last"]

                def mm_c(e):
                    ins = e.matmul(c_ps[si][:, c0:512], tge_bf[:], sp_sb[si][:, c0:512], start=True, stop=first)
                    if not first:
                        ins = e.matmul(c_ps[si][:, c0:512], ones_bf[:], S_sb[sprev][:, c0:512],
                                       start=False, stop=True)
                    return ins
                P.pe(mm_c, reads=[("sp_sb", si), "tge_bf", "ones_bf"] + ([] if first else [("S_sb", sprev)]),
                     writes=[("c_ps", si)])
                P.act(lambda e: e.activation(X_sb[si][:, c0:512], c_ps[si][:, c0:512], AF.Exp, scale=-1.0),
                      reads=[("c_ps", si)], writes=[("X_sb", si)])
                P.dve(lambda e: e.tensor_tensor(W_sb[si][:, c0:512], e_sb[si][:, c0:512], X_sb[si][:, c0:512],
                                                ALU.mult),
                      reads=[("e_sb", si), ("X_sb", si)], writes=[("W_sb", si)])
                if not last:
                    if first:
                        P.dve(lambda e: e.memset(S_sb[si][:, 0:c0], 0.0), writes=[("S_sb", si)])
                        P.dve(lambda e: e.tensor_copy(S_sb[si][:, c0:512], sp_sb[si][:, c0:512]),
                              reads=[("sp_sb", si), ("S_sb", si)], writes=[("S_sb", si)])
                    elif c0 > 0:
                        P.dve(lambda e: e.tensor_copy(S_sb[si][:, 0:c0], S_sb[sprev][:, 0:c0]),
                              reads=[("S_sb", sprev)], writes=[("S_sb", si)])
                        P.dve(lambda e: e.tensor_tensor(S_sb[si][:, c0:512], S_sb[sprev][:, c0:512],
                                                        sp_sb[si][:, c0:512], ALU.add),
                              reads=[("S_sb", sprev), ("sp_sb", si), ("S_sb", si)], writes=[("S_sb", si)])
                    else:
                        P.dve(lambda e: e.tensor_tensor(S_sb[si][:], S_sb[sprev][:], sp_sb[si][:], ALU.add),
                              reads=[("S_sb", sprev), ("sp_sb", si)], writes=[("S_sb", si)])

            def sb_s3(it):
                si = it["g"] % 2
                c0, kb, qt = it["c0"], it["kb"], it["qt"]
                P.pe(lambda e: e.matmul(o_ps[0][:, c0:512], vv[:, kb, 0:128], W_sb[si][:, c0:512],
                                        start=it["first"], stop=it["last"]),
                     reads=[("vv", kb), ("W_sb", si)], writes=[("o_ps", 0)])
                if it["last"]:
                    osl = qt % 2
                    q0 = it["q0"]
                    P.act(lambda e: e.activation(ogs[osl][:], o_ps[0][:], AF.Copy),
                          reads=[("o_ps", 0)], writes=[("ogs", osl)])
                    P.dma("sp", ogd[0][:, q0 // 2:(q0 + 512) // 2], ogs[osl][:].bitcast(F32),
                          reads=[("ogs", osl)], semkey=("ogs", osl))

            def fx_s1(it):
                pi = it["g"] % 2
                zi = 1
                c0, q0, k0, kb = it["c0"], it["q0"], it["k0"], it["kb"]
                osl = it["qt"] % 2
                P.pe(lambda e: e.matmul(z_ps[zi][:, c0:512], kfT[:, k0:k0 + 128], qfT[:, q0 + c0:q0 + 512],
                                        start=True, stop=True),
                     reads=tkeys("kfT", k0, k0 + 128) + tkeys("qfT", q0, q0 + 512), writes=[("z_ps", zi)])
                P.act(lambda e: e.activation(P_fx[pi][:, c0:512], z_ps[zi][:, c0:512], AF.Exp,
                                             bias=biasq[osl][:, kb:kb + 1]),
                      reads=[("z_ps", zi), ("biasq", osl)], writes=[("P_fx", pi)])
                if it["diag"]:
                    P.dve(lambda e: e.tensor_tensor(P_fx[pi][:, c0:c0 + 128], P_fx[pi][:, c0:c0 + 128],
                                                    cst32[:, 1, :], ALU.mult),
                          reads=[("P_fx", pi), "cst32"], writes=[("P_fx", pi)])

            def fx_s2(it):
                pi = it["g"] % 2
                c0, kb, qt = it["c0"], it["kb"], it["qt"]

                def mm_o(e):
                    e.matmul(o_ps[1][:, c0:512], vv[:, kb, 128:256], P_fx[pi][:, c0:512],
                             start=it["first"], stop=it["last"])
                    return e.matmul(den_ps[:, c0:512], ones_bf[:], P_fx[pi][:, c0:512],
                                    start=it["first"], stop=it["last"])
                P.pe(mm_o, reads=[("vv", kb), ("P_fx", pi), "ones_bf"], writes=[("o_ps", 1), "den_ps"])
                if it["last"]:
                    osl = qt % 2
                    q0 = it["q0"]
                    P.dve(lambda e: e.reciprocal(rden[:], den_ps[:]), reads=["den_ps"], writes=["rden"])
                    P.dve(lambda e: e.tensor_tensor(ogf[osl][:], o_ps[1][:], rden[:], ALU.mult),
                          reads=[("o_ps", 1), "rden"], writes=[("ogf", osl)])
                    P.dma("sp", ogd[1][:, q0 // 2:(q0 + 512) // 2], ogf[osl][:].bitcast(F32),
                          reads=[("ogf", osl)], semkey=("ogf", osl))

            def run_pipeline(items, stages, tag):
                for it in items:
                    it["g"] = gcnt[tag]
                    gcnt[tag] += 1
                ns = len(stages)
                n = len(items)
                for step in range(n + ns - 1):
                    for si_, st in enumerate(stages):
                        k = step - si_
                        if 0 <= k < n:
                            st(items[k])

            NPQ = NPB * T_S
            assert NPQ <= 512
            nsteps = max(1, (NPB - 1).bit_length())

            def sample_unit(b):
                sl = b % 2
                qa, qb_ = T_P + b * T_S, T_P + (b + 1) * T_S
                for br in range(2):
                    kd, vd = csrc[br]
                    for hh in range(0, PAST, 2048):
                        he = min(PAST, hh + 2048)
                        P.dma("pool", ckT[br][sl][:, hh:he], kd[b, :, hh:he],
                              writes=[("ckT", br, sl)], semkey=("ckT", br, sl))
                    cvf = cvv[br][sl][:].rearrange("p k d -> p (k d)")
                    for hh in range(0, NPB * 128, 2048):
                        he = min(NPB * 128, hh + 2048)
                        P.dma("pool", cvf[:, hh:he], vd[b, :, hh:he],
                              writes=[("cvv", br, sl)], semkey=("cvv", br, sl))
                mark0 = len(P.ops)
                zi = 0
                kk = [("ckT", 0, sl)]
                qk = [("qsT", NT - 1)]

                def mm_z(e, K=ckT[0][sl], qT_=qsT, kT_=ksT, zi=zi):
                    for blk in range(NPB):
                        e.matmul(z_ps[zi][:, blk * T_S:(blk + 1) * T_S], K[:, blk * 128:(blk + 1) * 128],
                                 qT_[:, qa:qb_], start=True, stop=True)
                    return e.matmul(misc_ps[0:T_S, 0:T_S], kT_[:, qa:qb_], qT_[:, qa:qb_], start=True, stop=True)
                P.pe(mm_z, reads=kk + qk + [("ksT", NT - 1)], writes=[("z_ps", zi), "misc_ps"])
                P.act(lambda e: e.activation(e_sb[0][:, 0:NPQ], z_ps[0][:, 0:NPQ], AF.Exp),
                      reads=[("z_ps", 0)], writes=[("e_sb", 0)])
                P.act(lambda e: e.activation(en_s[:], misc_ps[0:T_S, 0:T_S], AF.Exp),
                      reads=["misc_ps"], writes=["en_s"])
                P.dve(lambda e: e.tensor_tensor(en_s[:], en_s[:], cst32[0:T_S, 3, 0:T_S], ALU.mult),
                      reads=["en_s", "cst32"], writes=["en_s"])
                P.act(lambda e: e.activation(sp_sb[0][:, 0:NPQ], e_sb[0][:, 0:NPQ], AF.Ln, bias=1.0),
                      reads=[("e_sb", 0)], writes=[("sp_sb", 0)])
                P.act(lambda e: e.activation(spn_s[:], en_s[:], AF.Ln, bias=1.0), reads=["en_s"], writes=["spn_s"])
                P.dve(lambda e: e.tensor_copy(spnb[:, 0:NPQ].rearrange("p (k q) -> p k q", q=T_S),
                                              spn_s[:].unsqueeze(1).broadcast_to([T_S, NPB, T_S])),
                      reads=["spn_s"], writes=["spnb"])
                src, srck = sp_sb[0], ("sp_sb", 0)
                for stp in range(nsteps):
                    sh = 1 << stp
                    dst, dstk = (S_sb[stp % 2], ("S_sb", stp % 2))
                    w = (NPB - sh) * T_S
                    P.dve(lambda e, src=src, dst=dst, w=w, sh=sh: e.tensor_tensor(
                        dst[:, 0:w], src[:, 0:w], src[:, sh * T_S:sh * T_S + w], ALU.add),
                        reads=[srck], writes=[dstk])
                    P.dve(lambda e, src=src, dst=dst, w=w: e.tensor_copy(dst[:, w:NPQ], src[:, w:NPQ]),
                          reads=[srck, dstk], writes=[dstk])
                    src, srck = dst, dstk
                incl, inclk = src, srck

                def mm_c(e, incl=incl):
                    e.matmul(c_ps[0][:, 0:NPQ], tge_bf[:], sp_sb[0][:, 0:NPQ], start=True, stop=False)
                    e.matmul(c_ps[0][:, 0:NPQ - T_S], ones_bf[:], incl[:, T_S:NPQ], start=False, stop=False)
                    e.matmul(c_ps[0][:, 0:NPQ], ones_bf[0:T_S, :], spnb[:, 0:NPQ], start=False, stop=True)
                    return e.matmul(misc_ps[0:T_S, T_S:2 * T_S], tge_bf[0:T_S, 0:T_S], spn_s[:],
                                    start=True, stop=True)
                P.pe(mm_c, reads=[("sp_sb", 0), inclk, "spnb", "spn_s", "tge_bf", "ones_bf"],
                     writes=[("c_ps", 0), "misc_ps"])
                P.act(lambda e: e.activation(X_sb[0][:, 0:NPQ], c_ps[0][:, 0:NPQ], AF.Exp, scale=-1.0),
                      reads=[("c_ps", 0)], writes=[("X_sb", 0)])
                P.act(lambda e: e.activation(xn_s[:], misc_ps[0:T_S, T_S:2 * T_S], AF.Exp, scale=-1.0),
                      reads=["misc_ps"], writes=["xn_s"])
                P.dve(lambda e: e.tensor_tensor(W_sb[0][:, 0:NPQ], e_sb[0][:, 0:NPQ], X_sb[0][:, 0:NPQ], ALU.mult),
                      reads=[("e_sb", 0), ("X_sb", 0)], writes=[("W_sb", 0)])
                P.dve(lambda e: e.tensor_tensor(wn_s[:], en_s[:], xn_s[:], ALU.mult),
                      reads=["en_s", "xn_s"], writes=["wn_s"])

                def mm_o(e, V=cvv[0][sl]):
                    for blk in range(NPB):
                        e.matmul(o_ps[0][:, 0:T_S], V[:, blk, :], W_sb[0][:, blk * T_S:(blk + 1) * T_S],
                                 start=(blk == 0), stop=False)
                    return e.matmul(o_ps[0][:, 0:T_S], vnb[:, b, 0:128], wn_s[:], start=False, stop=True)
                P.pe(mm_o, reads=[("cvv", 0, sl), ("W_sb", 0), "wn_s", ("vnb", b)], writes=[("o_ps", 0)])
                P.act(lambda e: e.activation(ogsamp[0][:, b * T_S:(b + 1) * T_S], o_ps[0][:, 0:T_S], AF.Copy),
                      reads=[("o_ps", 0)], writes=[("ogsamp", 0, b)])
                mark1 = len(P.ops)
                zi = 1

                def mm_zf(e, K=ckT[1][sl], zi=zi):
                    for blk in range(NPB):
                        e.matmul(z_ps[zi][:, blk * T_S:(blk + 1) * T_S], K[:, blk * 128:(blk + 1) * 128],
                                 qfT[:, qa:qb_], start=True, stop=True)
                    return e.matmul(misc_ps[0:T_S, 2 * T_S:3 * T_S], kfT[:, qa:qb_], qfT[:, qa:qb_],
                                    start=True, stop=True)
                P.pe(mm_zf, reads=[("ckT", 1, sl), ("qfT", NT - 1), ("kfT", NT - 1)],
                     writes=[("z_ps", zi), "misc_ps"])
                P.dve(lambda e: e.tensor_tensor(tz[:, 0:NPQ].rearrange("p (k q) -> p k q", q=T_S),
                                                z_ps[1][:, 0:NPQ].rearrange("p (k q) -> p k q", q=T_S),
                                                biasp[:, b, :].unsqueeze(2).broadcast_to([128, NPB, T_S]),
                                                ALU.add),
                      reads=[("z_ps", 1), "biasp"], writes=["rden"])
                P.act(lambda e: e.activation(P_fx[0][:, 0:NPQ], tz[:, 0:NPQ], AF.Exp), reads=["rden"], writes=[("P_fx", 0)])
                P.act(lambda e: e.activation(pn_s[:], misc_ps[0:T_S, 2 * T_S:3 * T_S], AF.Exp,
                                             bias=biasn[:, b:b + 1]),
                      reads=["misc_ps", "biasn"], writes=["pn_s"])
                P.dve(lambda e: e.tensor_tensor(pn_s[:], pn_s[:], cst32[0:T_S, 1, 0:T_S], ALU.mult),
                      reads=["pn_s", "cst32"], writes=["pn_s"])

                def mm_of(e, V=cvv[1][sl]):
                    for blk in range(NPB):
                        e.matmul(o_ps[1][:, 0:T_S], V[:, blk, :], P_fx[0][:, blk * T_S:(blk + 1) * T_S],
                                 start=(blk == 0), stop=False)
                    e.matmul(o_ps[1][:, 0:T_S], vnb[:, b, 128:256], pn_s[:], start=False, stop=True)
                    for blk in range(NPB):
                        e.matmul(den_ps[:, 0:T_S], ones_bf[:], P_fx[0][:, blk * T_S:(blk + 1) * T_S],
                                 start=(blk == 0), stop=False)
                    return e.matmul(den_ps[:, 0:T_S], ones_bf[0:T_S, :], pn_s[:], start=False, stop=True)
                P.pe(mm_of, reads=[("cvv", 1, sl), ("P_fx", 0), "pn_s", ("vnb", b), "ones_bf"],
                     writes=[("o_ps", 1), "den_ps"])
                P.dve(lambda e: e.reciprocal(rden[:, 0:T_S], den_ps[:, 0:T_S]), reads=["den_ps"], writes=["rden"])
                P.dve(lambda e: e.tensor_tensor(ogsamp[1][:, b * T_S:(b + 1) * T_S], o_ps[1][:, 0:T_S],
                                                rden[:, 0:T_S], ALU.mult),
                      reads=[("o_ps", 1), "rden"], writes=[("ogsamp", 1, b)])
                a_ops, b_ops = P.ops[mark0:mark1], P.ops[mark1:]
                merged = []
                ia = ib = 0
                while ia < len(a_ops) or ib < len(b_ops):
                    if ia < len(a_ops):
                        merged.append(a_ops[ia]); ia += 1
                    if ia < len(a_ops) and len(a_ops) > 2 * len(b_ops) - 2:
                        merged.append(a_ops[ia]); ia += 1
                    if ib < len(b_ops):
                        merged.append(b_ops[ib]); ib += 1
                P.ops[mark0:] = merged

            csrc = [(cskT, csv), (cfkT, cfv)]
            sample_done = 0
            for qt in range(NQT):
                osl = qt % 2
                nkb = 4 * qt + 4
                P.dve(lambda e, osl=osl, nkb=nkb: e.tensor_scalar(
                    biasq[osl][:, 0:nkb], fcum[:, 0:nkb], -1.0, incl[:, nkb - 1:nkb], ALU.mult, ALU.add),
                    reads=["fcum", "incl"], writes=[("biasq", osl)])
                def st1(it):
                    sb_s1(it)
                    fx_s1(it)

                def st2(it):
                    sb_s2(it)
                    fx_s2(it)
                run_pipeline(prompt_items(qt), [st1, st2, sb_s3], "sb")
                tgt = (B_S * (qt + 1)) // NQT
                while sample_done < tgt:
                    sample_unit(sample_done)
                    sample_done += 1
            while sample_done < B_S:
                sample_unit(sample_done)
                sample_done += 1
            for br in range(2):
                P.dma("sp", ogd[br][:, T_P // 2:NTOK // 2], ogsamp[br][:].bitcast(F32),
                      reads=[("ogsamp", br, b) for b in range(B_S)], semkey=("ogsamp", br))

            P.emit()


def build_C(nc, NPC, og_src):
    NPP = NPC - 32
    ntile = (NPC + 511) // 512
    TW = NPC // ntile
    assert TW * ntile == NPC
    NPBK = NPP // 128

    def din(name, shape, dt=F32):
        return nc.dram_tensor(name, shape, dt, kind="ExternalInput").ap()

    xmT = din("xmT", [128, DC, NPC])
    xm = din("xm", [NPC, D])
    nwd = din("nw", [128, DC])
    cstd = din("cst", [128, 4, 128])
    wg = din("wg", [DC, 128, DC, 128])
    wm = din("wm", [2, DC, 128, DC, 128])
    wb = din("wb", [2, DC, 128, 8, 128])
    wo = din("wo", [4, 128, DC, 512])
    o_y = nc.dram_tensor("o_y", [NPC, D], F32, kind="ExternalOutput").ap()

    es = contextlib.ExitStack()
    with es:
        def sb(name, shape, dt):
            return es.enter_context(nc.sbuf_tensor("sC_" + name, shape, dt))

        def ps(name, shape, dt=F32):
            return es.enter_context(nc.psum_tensor("pC_" + name, shape, dt))

        hT = sb("hT", [128, DC, NPC], BF16)
        og = sb("og", [128, DC, NPC], BF16)
        mg = sb("mg", [128, DC, NPC], BF16)
        cst32 = sb("cst32", [128, 4, 128], F32)
        ones_bf = sb("ones_bf", [128, 128], BF16)
        nw = sb("nw", [128, DC], F32)
        nws = sb("nws", [128, DC], F32)
        xt = [sb(f"xt{i}", [128, TW], F32) for i in range(2)]
        rr = sb("rr", [128, TW], F32)
        wgb = [sb(f"wgb{i}", [128, DC, 128], BF16) for i in range(2)]
        wmb = [[sb(f"wmb{br}_{i}", [128, DC, 128], BF16) for i in range(2)] for br in range(2)]
        wbb = [[sb(f"wbb{br}_{i}", [128, 8, 128], BF16) for i in range(2)] for br in range(2)]
        sg = [sb(f"sg{i}", [128, TW], BF16) for i in range(2)]
        sig = [sb(f"sig{i}", [128, TW], F32) for i in range(2)]
        t1 = [sb(f"t1_{i}", [128, TW], F32) for i in range(2)]
        wob = [sb(f"wob{i}", [128, DC, 512], BF16) for i in range(2)]
        xres = [sb(f"xres{i}", [128, 512], F32) for i in range(2)]
        yst = [sb(f"yst{i}", [128, 512], F32) for i in range(2)]
        pA = [ps(f"pA{i}", [128, 512]) for i in range(2)]
        pU = [ps(f"pU{i}", [128, 512]) for i in range(2)]
        pM = [ps(f"pM{i}", [128, 512]) for i in range(2)]
        pY = [ps(f"pY{i}", [128, 512]) for i in range(2)]

        P = Prog(nc, "c")
        P.dma("sp", cst32[:], cstd, writes=["cst32"], semkey="cst32")
        P.dma("sp", nw[:], nwd, writes=["nw"], semkey="nw")
        P.act(lambda e: e.activation(ones_bf[:], cst32[:, 0, :], AF.Copy), reads=["cst32"], writes=["ones_bf"])
        P.dve(lambda e: e.tensor_scalar(nws[:], nw[:], float(D) ** 0.5, None, ALU.mult),
              reads=["nw"], writes=["nws"])
        for ch in range(DC):
            P.dma("pool", og[:, ch, :].bitcast(F32), og_src(ch), writes=[("og", ch)], semkey=("og", ch))
        for ti in range(ntile):
            c0 = ti * TW
            for ch in range(DC):
                s = ch % 2
                P.dma("sp", xt[s][:], xmT[:, ch, c0:c0 + TW], writes=[("xt", s)], semkey=("xt", s))
                P.act(lambda e, s=s, ch=ch, c0=c0: e.activation(mg[:, ch, c0:c0 + TW], xt[s][:], AF.Square),
                      reads=[("xt", s)], writes=[("sq", ch)])
                P.dve(lambda e, s=s, ch=ch, c0=c0: e.tensor_scalar(
                    hT[:, ch, c0:c0 + TW], xt[s][:], nws[:, ch:ch + 1], None, ALU.mult),
                    reads=[("xt", s), "nws"], writes=[("hT", ti)])

            def mm_ss(e, c0=c0):
                for ch in range(DC):
                    ins = e.matmul(pA[0][:, 0:TW], ones_bf[:], mg[:, ch, c0:c0 + TW],
                                   start=(ch == 0), stop=(ch == DC - 1))
                return ins
            P.pe(mm_ss, reads=[("sq", ch) for ch in range(DC)] + ["ones_bf"], writes=[("pA", 0)])
            P.act(lambda e: e.activation(rr[:], pA[0][:, 0:TW], AF.Ln, bias=float(D) * EPS),
                  reads=[("pA", 0)], writes=["rr"])
            P.act(lambda e: e.activation(rr[:], rr[:], AF.Exp, scale=-0.5), reads=["rr"], writes=["rr"])
            P.dve(lambda e, c0=c0: e.tensor_tensor(hT[:, :, c0:c0 + TW], hT[:, :, c0:c0 + TW],
                                                   _bc_mid(rr[:], DC), ALU.mult),
                  reads=[("hT", ti), "rr"], writes=[("hT", ti)])
        HT = [("hT", ti) for ti in range(ntile)]

        wq = {"n": 0}

        def load_w(dst_bf, src_ap, key, nch):
            P.dma("pool", dst_bf[:], src_ap, writes=[key], semkey=key)

        for cb in range(DC):
            s = cb % 2
            load_w(wgb[s], wg[cb], ("wgb", s), DC)
            for ti in range(ntile):
                c0 = ti * TW
                pi = ti % 2

                def mm_g(e, s=s, c0=c0, pi=pi):
                    for ch in range(DC):
                        ins = e.matmul(pA[pi][:, 0:TW], wgb[s][:, ch, :], hT[:, ch, c0:c0 + TW],
                                       start=(ch == 0), stop=(ch == DC - 1))
                    return ins
                P.pe(mm_g, reads=[("wgb", s)] + HT, writes=[("pA", pi)])
                P.act(lambda e, pi=pi: e.activation(sg[pi][:], pA[pi][:, 0:TW], AF.Silu),
                      reads=[("pA", pi)], writes=[("sg", pi)])
                P.dve(lambda e, cb=cb, c0=c0, pi=pi: e.tensor_tensor(
                    og[:, cb, c0:c0 + TW], og[:, cb, c0:c0 + TW], sg[pi][:], ALU.mult),
                    reads=[("og", cb), ("sg", pi)], writes=[("og", cb)])
        OG = [("og", ch) for ch in range(DC)]

        for cb in range(DC):
            s = cb % 2
            for br in range(2):
                load_w(wmb[br][s], wm[br, cb], ("wmb", br, s), DC)
                load_w(wbb[br][s], wb[br, cb], ("wbb", br, s), 8)
            for ti in range(ntile):
                c0 = ti * TW
                for br in range(2):
                    def mm_u(e, s=s, c0=c0, br=br):
                        for ch in range(8):
                            ins = e.matmul(pU[br][:, 0:TW], wbb[br][s][:, ch, :], og[:, br * 8 + ch, c0:c0 + TW],
                                           start=(ch == 0), stop=(ch == 7))
                        return ins
                    P.pe(mm_u, reads=[("wbb", br, s)] + OG, writes=[("pU", br)])

                    def mm_m(e, s=s, c0=c0, br=br):
                        for ch in range(DC):
                            ins = e.matmul(pM[br][:, 0:TW], wmb[br][s][:, ch, :], hT[:, ch, c0:c0 + TW],
                                           start=(ch == 0), stop=(ch == DC - 1))
                        return ins
                    P.pe(mm_m, reads=[("wmb", br, s)] + HT, writes=[("pM", br)])
                    P.act(lambda e, br=br: e.activation(sig[br][:], pM[br][:, 0:TW], AF.Sigmoid),
                          reads=[("pM", br)], writes=[("sig", br)])
                    P.dve(lambda e, br=br: e.tensor_tensor(t1[br][:], pU[br][:, 0:TW], sig[br][:], ALU.mult),
                          reads=[("pU", br), ("sig", br)], writes=[("t1", br)])
                P.dve(lambda e, cb=cb, c0=c0: e.tensor_tensor(mg[:, cb, c0:c0 + TW], t1[0][:], t1[1][:], ALU.add),
                      reads=[("t1", 0), ("t1", 1)], writes=[("mg", cb, ti)])
        MG = [("mg", cb, ti) for cb in range(DC) for ti in range(ntile)]

        tblocks = [(i * 128, 128) for i in range(NPBK)] + [(NPP, 32)]
        for cbk in range(4):
            s = cbk % 2
            for c2 in range(DC // 2):
                P.dma("pool", wob[s][:, 2 * c2:2 * c2 + 2, :], wo[cbk, :, 2 * c2:2 * c2 + 2, :],
                      writes=[("wob", s, c2)], semkey=("wob", s))
            WO = [("wob", s, c2) for c2 in range(DC // 2)]
            for bi, (r0, nr) in enumerate(tblocks):
                pi = bi % 2
                P.dma("sp", xres[pi][0:nr, :], xm[r0:r0 + nr, cbk * 512:(cbk + 1) * 512],
                      writes=[("xres", pi)], semkey=("xres", pi))

                def mm_y(e, s=s, r0=r0, nr=nr, pi=pi):
                    for ch in range(DC):
                        ins = e.matmul(pY[pi][0:nr, :], mg[:, ch, r0:r0 + nr], wob[s][:, ch, :],
                                       start=(ch == 0), stop=(ch == DC - 1))
                    return ins
                P.pe(mm_y, reads=WO + MG, writes=[("pY", pi)])
                P.dve(lambda e, nr=nr, pi=pi: e.tensor_tensor(yst[pi][0:nr, :], pY[pi][0:nr, :],
                                                              xres[pi][0:nr, :], ALU.add),
                      reads=[("pY", pi), ("xres", pi)], writes=[("yst", pi)])
                P.dma("sp", o_y[r0:r0 + nr, cbk * 512:(cbk + 1) * 512], yst[pi][0:nr, :],
                      reads=[("yst", pi)], semkey=("yst", pi))
        P.emit()


def _consts():
    p = np.arange(128)[:, None]
    c = np.arange(128)[None, :]
    cst = np.zeros((128, 4, 128), np.float32)
    cst[:, 0, :] = 1.0
    cst[:, 1, :] = (p <= c)
    cst[:, 2, :] = (p >= c)
    cst[:, 3, :] = (p < c)
    return cst


_WA_BLOCKS = [0, 1, 4, 5]


def kernel(x_prompt, x_sample, cache_sb_k, cache_sb_v, cache_fox_k, cache_fox_v, cache_fox_logf,
           norm_w, w_in, b_forget, q_norm_w, k_norm_w, w_branch_sb, w_branch_fox, w_out):
    f32 = np.float32
    x_prompt = np.asarray(x_prompt, f32)
    x_sample = np.asarray(x_sample, f32)
    T_P = x_prompt.shape[1]
    PAST = cache_sb_k.shape[2]
    NTOK = T_P + NSAMP
    NPB = PAST // 128
    NB = T_P // 128
    w_in = np.asarray(w_in, f32)[0]
    cst = _consts()
    nwl = np.ascontiguousarray(np.asarray(norm_w, f32)[0].reshape(DC, 128).T)

    x_all = np.concatenate([x_prompt[0], x_sample.reshape(NSAMP, D)], 0)
    xTp = np.ascontiguousarray(
        x_all[:T_P].reshape(T_P // 512, 512, DC, 128).transpose(0, 3, 2, 1)).reshape(T_P // 512, 128, DC * 512)
    xTs = np.ascontiguousarray(x_all[T_P:].reshape(NSAMP, DC, 128).transpose(2, 1, 0)).reshape(128, DC * NSAMP)
    csk = np.asarray(cache_sb_k, f32)[0]
    csv = np.asarray(cache_sb_v, f32)[0]
    cfk = np.asarray(cache_fox_k, f32)[0]
    cfv = np.asarray(cache_fox_v, f32)[0]
    clf = np.asarray(cache_fox_logf, f32)[0]
    in_maps = []
    for c in range(NCORES):
        cols = []
        for blk in _WA_BLOCKS:
            cols.append(w_in[:, blk * 1024 + c * 128: blk * 1024 + (c + 1) * 128])
        cols.append(w_in[:, 2 * 1024 + c * 128: 2 * 1024 + (c + 1) * 128])
        cols.append(w_in[:, 6 * 1024 + c * 128: 6 * 1024 + (c + 1) * 128])
        cols.append(w_in[:, 8 * 1024 + c: 8 * 1024 + c + 1])
        wa = np.concatenate(cols, 1)
        wa = np.ascontiguousarray(wa.reshape(DC, 128, 769).transpose(1, 0, 2))
        vec = np.stack([np.full(128, np.asarray(b_forget, f32)[0, c], f32),
                        np.asarray(q_norm_w, f32)[0], np.asarray(k_norm_w, f32)[0]], 1)
        in_maps.append({
            "xTp": xTp, "xTs": xTs, "wa": wa, "nw": nwl, "vec": np.ascontiguousarray(vec), "cst": cst,
            "cskT": np.ascontiguousarray(csk[:, :, c, :].transpose(0, 2, 1)),
            "cfkT": np.ascontiguousarray(cfk[:, :, c, :].transpose(0, 2, 1)),
            "csv": np.ascontiguousarray(
                csv[:, :, c, :].reshape(B_S, NPB, 128, 128).transpose(0, 2, 1, 3)).reshape(B_S, 128, NPB * 128),
            "cfv": np.ascontiguousarray(
                cfv[:, :, c, :].reshape(B_S, NPB, 128, 128).transpose(0, 2, 1, 3)).reshape(B_S, 128, NPB * 128),
            "clf": np.ascontiguousarray(clf[:, :, c].reshape(B_S, NPB, 128).transpose(2, 0, 1)),
        })
    nc = bass.Bass("TRN2", target_bir_lowering=False)
    o_og = nc.dram_tensor("o_og", [2, 128, NTOK // 2], F32, kind="ExternalOutput").ap()
    build_A(nc, T_P, PAST, lambda br: o_og[br])
    resA = run_bass_kernel_spmd(nc, in_maps, core_ids=list(range(NCORES))).results

    p_sb_k = np.zeros((1, 1, T_P, 8, 128), f32); s_sb_k = np.zeros((1, B_S, T_S, 8, 128), f32)
    p_sb_v = np.zeros_like(p_sb_k); s_sb_v = np.zeros_like(s_sb_k)
    p_fx_k = np.zeros_like(p_sb_k); s_fx_k = np.zeros_like(s_sb_k)
    p_fx_v = np.zeros_like(p_sb_k); s_fx_v = np.zeros_like(s_sb_k)
    p_lf = np.zeros((1, 1, T_P, 8), f32); s_lf = np.zeros((1, B_S, T_S, 8), f32)
    og_all = np.zeros((2, 8, 128, NTOK), np.uint16)
    for c in range(NCORES):
        r = resA[c]
        kT = np.asarray(r["o_kT"])
        v = np.asarray(r["o_v"])
        p_sb_k[0, 0, :, c, :] = kT[0, :, :T_P].T
        s_sb_k[0, :, :, c, :] = kT[0, :, T_P:].T.reshape(B_S, T_S, 128)
        p_fx_k[0, 0, :, c, :] = kT[1, :, :T_P].T
        s_fx_k[0, :, :, c, :] = kT[1, :, T_P:].T.reshape(B_S, T_S, 128)
        p_sb_v[0, 0, :, c, :] = v[:T_P, 0:128]
        s_sb_v[0, :, :, c, :] = v[T_P:, 0:128].reshape(B_S, T_S, 128)
        p_fx_v[0, 0, :, c, :] = v[:T_P, 128:256]
        s_fx_v[0, :, :, c, :] = v[T_P:, 128:256].reshape(B_S, T_S, 128)
        p_lf[0, 0, :, c] = np.asarray(r["o_lf"]).T.reshape(T_P)
        s_lf[0, :, :, c] = np.asarray(r["o_lfs"]).T
        og_all[:, c] = np.ascontiguousarray(np.asarray(r["o_og"])).view(np.uint16).reshape(2, 128, NTOK)

    NPP = T_P // NCORES
    NPC = NPP + 32
    wg = np.concatenate([w_in[:, 3 * 1024:4 * 1024], w_in[:, 7 * 1024:8 * 1024]], 1)
    wg = np.ascontiguousarray(wg.reshape(DC, 128, DC, 128).transpose(2, 1, 0, 3))
    m0 = 8 * 1024 + 8
    wm = np.stack([w_in[:, m0:m0 + D], w_in[:, m0 + D:m0 + 2 * D]], 0)
    wm = np.ascontiguousarray(wm.reshape(2, DC, 128, DC, 128).transpose(0, 3, 2, 1, 4))
    wb = np.stack([np.asarray(w_branch_sb, f32)[0], np.asarray(w_branch_fox, f32)[0]], 0)
    wb = np.ascontiguousarray(wb.reshape(2, 8, 128, DC, 128).transpose(0, 3, 2, 1, 4))
    wo = np.asarray(w_out, f32)[0]
    wo = np.ascontiguousarray(wo.reshape(DC, 128, 4, 512).transpose(2, 1, 0, 3))
    in_maps = []
    for c in range(NCORES):
        tok = np.concatenate([np.arange(c * NPP, (c + 1) * NPP), T_P + np.arange(c * 32, (c + 1) * 32)])
        xm = np.ascontiguousarray(x_all[tok])
        xmT = np.ascontiguousarray(xm.T.reshape(DC, 128, NPC).transpose(1, 0, 2))
        ogc = np.ascontiguousarray(og_all[:, :, :, tok].reshape(DC, 128, NPC)).view(np.float32)
        in_maps.append({"xmT": xmT, "xm": xm, "nw": nwl, "cst": cst, "wg": wg, "wm": wm, "wb": wb,
                        "wo": wo, "ogin": np.ascontiguousarray(ogc)})
    nc2 = bass.Bass("TRN2", target_bir_lowering=False)
    ogin = nc2.dram_tensor("ogin", [DC, 128, NPC // 2], F32, kind="ExternalInput").ap()
    build_C(nc2, NPC, lambda ch: ogin[ch])
    resC = run_bass_kernel_spmd(nc2, in_maps, core_ids=list(range(NCORES))).results
    y_p = np.zeros((1, T_P, D), f32)
    y_s = np.zeros((NSAMP, D), f32)
    for c in range(NCORES):
        y = np.asarray(resC[c]["o_y"])
        y_p[0, c * NPP:(c + 1) * NPP] = y[:NPP]
        y_s[c * 32:(c + 1) * 32] = y[NPP:]
    y_s = y_s.reshape(B_S, T_S, D)
    return (y_p, y_s, p_sb_k, p_sb_v, p_fx_k, p_fx_v, p_lf, s_sb_k, s_sb_v, s_fx_k, s_fx_v, s_lf)
```

```python
import contextlib
import numpy as np
import concourse.bass as bass
import concourse.mybir as mybir
from concourse.bass_utils import run_bass_kernel_spmd

F32 = mybir.dt.float32
BF16 = mybir.dt.bfloat16
U32 = mybir.dt.uint32
AF = mybir.ActivationFunctionType
ALU = mybir.AluOpType

NCORES = 8
D = 2048
DC = 16
HD = 128
B_S = 16
T_S = 16
NSAMP = B_S * T_S
EPS = 1e-6


class Prog:
    ENGS = ("pe", "act", "dve", "pool", "sp")

    def __init__(self, nc, name="p"):
        self.nc = nc
        self.name = name
        self.ops = []

    def add(self, eng, fn, reads=(), writes=(), semkey=None, sem_inc=16):
        assert eng in self.ENGS
        self.ops.append(
            dict(eng=eng, fn=fn, reads=list(reads), writes=list(writes), semkey=semkey,
                 sem_inc=sem_inc)
        )

    def pe(self, fn, reads=(), writes=()):
        self.add("pe", fn, reads, writes)

    def act(self, fn, reads=(), writes=()):
        self.add("act", fn, reads, writes)

    def dve(self, fn, reads=(), writes=()):
        self.add("dve", fn, reads, writes)

    def pool(self, fn, reads=(), writes=()):
        self.add("pool", fn, reads, writes)

    def dma(self, q, out, in_, reads=(), writes=(), semkey=None, **kw):
        assert semkey is not None
        self.add(
            q,
            lambda e, out=out, in_=in_, kw=kw: e.dma_start(out=out, in_=in_, **kw),
            reads, writes, semkey,
        )

    def emit(self):
        nc = self.nc
        import os as _os
        mx = _os.environ.get("MAXOPS_" + self.name)
        if mx is not None:
            self.ops = self.ops[:int(mx)]
        ops = self.ops
        print("emit", self.name, "nops", len(ops), flush=True)
        if not ops:
            return
        last_writer = {}
        readers = {}
        for i, op in enumerate(ops):
            deps = set()
            for k in op["reads"]:
                if k in last_writer:
                    deps.add(last_writer[k])
                if _is_psum_key(k):
                    for r_ in readers.get(k, ()):
                        if ops[r_]["eng"] != op["eng"]:
                            deps.add(r_)
            for k in op["writes"]:
                if k in last_writer:
                    deps.add(last_writer[k])
                deps.update(readers.get(k, ()))
            deps.discard(i)
            if op["eng"] == "pe" and op["semkey"] is None:
                deps = {d for d in deps
                        if not (ops[d]["eng"] == "pe" and ops[d]["semkey"] is None)}
            op["deps"] = deps
            for k in op["reads"]:
                readers.setdefault(k, []).append(i)
            for k in op["writes"]:
                last_writer[k] = i
                readers[k] = []
        has_dep = [False] * len(ops)
        for op in ops:
            for d in op["deps"]:
                has_dep[d] = True
        eng_cnt = {e: 0 for e in self.ENGS}
        dma_cnt = {}
        for i, op in enumerate(ops):
            if op["semkey"] is not None:
                k = op["semkey"]
                dma_cnt[k] = dma_cnt.get(k, 0) + op["sem_inc"]
                op["sig"] = (("dma", k), dma_cnt[k])
            elif has_dep[i]:
                eng_cnt[op["eng"]] += 1
                op["sig"] = (("eng", op["eng"]), eng_cnt[op["eng"]])
            else:
                op["sig"] = None
        semnames = [("eng", e) for e in self.ENGS] + [("dma", k) for k in dma_cnt]
        with contextlib.ExitStack() as es:
            sems = {}
            for j, sn in enumerate(semnames):
                sems[sn] = es.enter_context(nc.semaphore(f"{self.name}_s{j}"))
            block = es.enter_context(nc.Block())
            per_eng = {e: [op for op in ops if op["eng"] == e] for e in self.ENGS}
            dma_issuer = {}
            for op in ops:
                if op["semkey"] is not None:
                    dma_issuer[op["semkey"]] = op["eng"]

            def run_engine(engname, e):
                sat = {}
                for op in per_eng[engname]:
                    need = {}
                    for d in op["deps"]:
                        sn, c = ops[d]["sig"]
                        if c > need.get(sn, 0):
                            need[sn] = c
                    for sn, c in need.items():
                        if sat.get(sn, 0) < c:
                            e.wait_ge(sems[sn], c)
                            sat[sn] = c
                    ins = op["fn"](e)
                    if op["sig"] is not None:
                        sn, c = op["sig"]
                        if sn[0] == "dma":
                            if op["sem_inc"] == 1:
                                ins.then_inc(sems[sn])
                            else:
                                ins.then_inc(sems[sn], op["sem_inc"])
                        else:
                            ins.then_inc(sems[sn], 1)
                for k, tot in dma_cnt.items():
                    if dma_issuer[k] == engname:
                        sn = ("dma", k)
                        if sat.get(sn, 0) < tot:
                            e.wait_ge(sems[sn], tot)

            @block.tensor
            def _(e):
                run_engine("pe", e)

            @block.scalar
            def _(e):
                run_engine("act", e)

            @block.vector
            def _(e):
                run_engine("dve", e)

            @block.gpsimd
            def _(e):
                run_engine("pool", e)

            @block.sync
            def _(e):
                run_engine("sp", e)


_PSUM_NAMES = {"misc_ps", "ss_ps", "psf", "psv", "ssq_ps", "z_ps", "c_ps", "o_ps", "den_ps", "pA", "pU", "pM", "pY"}


def _is_psum_key(k):
    n = k[0] if isinstance(k, tuple) else k
    return n in _PSUM_NAMES


def _bc_mid(ap, n):
    p, f = ap.shape
    return ap.unsqueeze(1).broadcast_to([p, n, f])


def build_A(nc, T_P, PAST, og_dst):
    NTOK = T_P + NSAMP
    NT = NTOK // 256
    NB = T_P // 128
    NPB = PAST // 128
    NQT = T_P // 512
    SC = float(HD) ** -0.5

    def din(name, shape, dt=F32):
        return nc.dram_tensor(name, shape, dt, kind="ExternalInput").ap()

    def dout(name, shape, dt=F32):
        return nc.dram_tensor(name, shape, dt, kind="ExternalOutput").ap()

    xTp = din("xTp", [T_P // 512, 128, DC * 512])
    xTs = din("xTs", [128, DC * NSAMP])
    wa = din("wa", [128, DC, 769])
    nwd = din("nw", [128, DC])
    vecd = din("vec", [128, 3])
    cstd = din("cst", [128, 4, 128])
    cskT = din("cskT", [B_S, 128, PAST])
    cfkT = din("cfkT", [B_S, 128, PAST])
    csv = din("csv", [B_S, 128, NPB * 128])
    cfv = din("cfv", [B_S, 128, NPB * 128])
    clfd = din("clf", [128, B_S, NPB])

    o_kT = dout("o_kT", [2, 128, NTOK])
    o_v = dout("o_v", [NTOK, 256])
    o_lf = dout("o_lf", [128, NB])
    o_lfs = dout("o_lfs", [T_S, B_S])

    es = contextlib.ExitStack()
    with es:
        def sb(name, shape, dt):
            return es.enter_context(nc.sbuf_tensor("sA_" + name, shape, dt))

        def ps(name, shape, dt=F32):
            return es.enter_context(nc.psum_tensor("pA_" + name, shape, dt))

        qsT = sb("qsT", [128, NTOK], BF16)
        ksT = sb("ksT", [128, NTOK], BF16)
        qfT = sb("qfT", [128, NTOK], BF16)
        kfT = sb("kfT", [128, NTOK], BF16)
        vv = sb("vv", [128, NB, 256], BF16)
        vnb = sb("vnb", [T_S, B_S, 256], BF16)
        cst32 = sb("cst32", [128, 4, 128], F32)
        ones_bf = sb("ones_bf", [128, 128], BF16)
        tge_bf = sb("tge_bf", [128, 128], BF16)
        vec = sb("vec", [128, 3], F32)
        negb = sb("negb", [128, 1], F32)
        kcol = sb("kcol", [128, 1], F32)
        lfcol = sb("lfcol", [128, NB], F32)
        lfs = sb("lfs", [T_S, B_S], F32)
        fcum = sb("fcum", [128, NB], F32)
        incl = sb("incl", [128, NB], F32)
        biasp = sb("biasp", [128, B_S, NPB], F32)
        biasn = sb("biasn", [T_S, B_S], F32)
        ones32 = cst32[:, 0, :]
        mle32 = cst32[:, 1, :]
        mlt32 = cst32[:, 3, :]

        es1 = contextlib.ExitStack()
        with es1:
            def sb1(name, shape, dt):
                return es1.enter_context(nc.sbuf_tensor("sA1_" + name, shape, dt))

            def ps1(name, shape, dt=F32):
                return es1.enter_context(nc.psum_tensor("pA1_" + name, shape, dt))

            wbf = sb1("wbf", [128, DC, 784], BF16)
            wst1 = sb1("wst0", [128, 769], F32)
            wst = [wst1, wst1]
            nw = sb1("nw", [128, DC], F32)
            nws = sb1("nws", [128, DC], F32)
            xt = [sb1(f"xt{i}", [128, DC, 512], BF16) for i in range(2)]
            sqh = [sb1(f"sqh{i}", [128, 4, 512], BF16) for i in range(2)]
            rr = sb1("rr", [128, 512], F32)
            rr2 = sb1("rr2", [128, 512], F32)
            sqq = sb1("sqq", [128, 1024], BF16)
            rq = sb1("rq", [128, 1024], F32)
            kst32 = sb1("kst32", [128, 512], F32)
            kf32 = sb1("kf32", [128, 512], F32)
            rcol = sb1("rcol", [128, 2], F32)
            v32 = [sb1(f"v32_{i}", [128, 256], F32) for i in range(2)]
            vn32 = [sb1(f"vn32_{i}", [T_S, 256], F32) for i in range(2)]
            fraw = sb1("fraw", [128, NB], F32)
            fraws = sb1("fraws", [T_S, B_S], F32)
            ftmp = sb1("ftmp", [128, NB], F32)
            ftmps = sb1("ftmps", [T_S, B_S], F32)
            ss_ps = ps1("ss_ps", [128, 512])
            psf = [ps1(f"psf{i}", [128, 512]) for i in range(4)]
            psv = [ps1(f"psv{i}", [128, 512]) for i in range(2)]
            ssq_ps = ps1("ssq_ps", [128, 512])

            P = Prog(nc, "a1")
            P.dma("sp", cst32[:], cstd, writes=["cst32"], semkey="cst32")
            P.dma("sp", vec[:], vecd, writes=["vec"], semkey="vec")
            P.dma("sp", nw[:], nwd, writes=["nw"], semkey="nw")
            P.act(lambda e: e.activation(ones_bf[:], cst32[:, 0, :], AF.Copy),
                  reads=["cst32"], writes=["ones_bf"])
            P.act(lambda e: e.activation(tge_bf[:], cst32[:, 2, :], AF.Copy),
                  reads=["cst32"], writes=["tge_bf"])
            P.dve(lambda e: e.tensor_scalar(negb[:], vec[:, 0:1], -1.0, None, ALU.mult),
                  reads=["vec"], writes=["negb"])
            P.dve(lambda e: e.tensor_scalar(kcol[:], vec[:, 2:3], float(HD) ** 0.5, None, ALU.mult),
                  reads=["vec"], writes=["kcol"])
            P.dve(lambda e: e.tensor_scalar(nws[:], nw[:], float(D) ** 0.5, None, ALU.mult),
                  reads=["nw"], writes=["nws"])
            for ch in range(DC):
                s = ch % 2
                P.dma("sp", wst[s][:], wa[:, ch, :], writes=[("wst", 0)], semkey=("wst", 0))
                if ch % 2 == 0:
                    P.dve(lambda e, ch=ch, s=s: e.tensor_scalar(
                        wbf[:, ch, 0:769], wst[s][:], nws[:, ch:ch + 1], None, ALU.mult),
                        reads=[("wst", 0), "nws"], writes=[("wbf", ch)])
                else:
                    P.act(lambda e, ch=ch, s=s: e.activation(
                        wbf[:, ch, 0:769], wst[s][:], AF.Copy, scale=nws[:, ch:ch + 1]),
                        reads=[("wst", 0), "nws"], writes=[("wbf", ch)])
            WB = [("wbf", ch) for ch in range(DC)]


            tiles = [(i * 512, 512) for i in range(T_P // 512)] + [(T_P, NSAMP)]
            one1 = cst32[0:1, 0, 0:1]
            for ti, (t0, n) in enumerate(tiles):
                s = ti % 2
                is_samp = t0 == T_P

                def kt(name, t0=t0, n=n):
                    return [(name, t) for t in range(t0 // 256, (t0 + n) // 256)]
                if not is_samp:
                    for g in range(4):
                        P.dma("pool", xt[s][:, 4 * g:4 * g + 4, :].rearrange("p c t -> p (c t)"),
                              xTp[ti, :, g * 2048:(g + 1) * 2048], writes=[("xt", s)], semkey=("xt", s))
                else:
                    for g in range(2):
                        P.dma("pool", xt[s][:, 8 * g:8 * g + 8, 0:n],
                              xTs[:, g * 8 * n:(g + 1) * 8 * n].rearrange("p (c t) -> p c t", t=n),
                              writes=[("xt", s)], semkey=("xt", s))
                for g in range(4):
                    sg = g % 2
                    P.act(lambda e, s=s, g=g, sg=sg, n=n: e.activation(sqh[sg][:, :, 0:n], xt[s][:, 4 * g:4 * g + 4, 0:n],
                                                                     AF.Square),
                          reads=[("xt", s)], writes=[("sqh", sg)])

                    def mm_ss(e, g=g, sg=sg, n=n):
                        for c in range(4):
                            ins = e.matmul(ss_ps[:, 0:n], ones_bf[:], sqh[sg][:, c, 0:n],
                                           start=(g == 0 and c == 0), stop=(g == 3 and c == 3))
                        return ins
                    P.pe(mm_ss, reads=[("sqh", sg), "ones_bf"], writes=["ss_ps"])
                P.act(lambda e, n=n: e.activation(rr[:, 0:n], ss_ps[:, 0:n], AF.Ln, bias=float(D) * EPS),
                      reads=["ss_ps"], writes=["rr"])
                P.act(lambda e, n=n: e.activation(rr[:, 0:n], rr[:, 0:n], AF.Exp, scale=-0.5),
                      reads=["rr"], writes=["rr"])
                P.dve(lambda e, n=n: e.tensor_tensor(rr2[:, 0:n], rr[:, 0:n], rr[:, 0:n], ALU.mult),
                      reads=["rr"], writes=["rr2"])
                for cb in range(4):
                    def mm_f(e, cb=cb, s=s, n=n):
                        for ch in range(DC):
                            ins = e.matmul(psf[cb][:, 0:n], wbf[:, ch, cb * 128:(cb + 1) * 128], xt[s][:, ch, 0:n],
                                           start=(ch == 0), stop=(ch == DC - 1))
                        return ins
                    P.pe(mm_f, reads=[("xt", s)] + WB, writes=[("psf", cb)])
                P.dve(lambda e, t0=t0, n=n: e.scalar_tensor_tensor(
                    qsT[:, t0:t0 + n], psf[0][:, 0:n], SC, rr[:, 0:n], ALU.mult, ALU.mult),
                    reads=[("psf", 0), "rr"], writes=kt("qsT"))
                P.dve(lambda e, n=n: e.tensor_tensor(kst32[:, 0:n], psf[1][:, 0:n], rr[:, 0:n], ALU.mult),
                      reads=[("psf", 1), "rr"], writes=["kst32"])
                P.act(lambda e, t0=t0, n=n: e.activation(ksT[:, t0:t0 + n], kst32[:, 0:n], AF.Copy),
                      reads=["kst32"], writes=kt("ksT"))
                P.dma("sp", o_kT[0, :, t0:t0 + n], kst32[:, 0:n], reads=["kst32"], semkey="kst32")
                P.act(lambda e, n=n: e.activation(sqq[:, 0:n], psf[2][:, 0:n], AF.Square),
                      reads=[("psf", 2)], writes=["sqq"])
                P.act(lambda e, n=n: e.activation(sqq[:, 512:512 + n], psf[3][:, 0:n], AF.Square),
                      reads=[("psf", 3)], writes=["sqq"])

                def mm_q(e, n=n):
                    e.matmul(ss_ps[:, 0:n], ones_bf[:], sqq[:, 0:n], start=True, stop=True)
                    return e.matmul(ssq_ps[:, 0:n], ones_bf[:], sqq[:, 512:512 + n], start=True, stop=True)
                P.pe(mm_q, reads=["sqq", "ones_bf"], writes=["ss_ps", "ssq_ps"])
                P.dve(lambda e, n=n: e.tensor_tensor(rq[:, 0:n], ss_ps[:, 0:n], rr2[:, 0:n], ALU.mult),
                      reads=["ss_ps", "rr2"], writes=["rq"])
                P.dve(lambda e, n=n: e.tensor_tensor(rq[:, 512:512 + n], ssq_ps[:, 0:n], rr2[:, 0:n], ALU.mult),
                      reads=["ssq_ps", "rr2", "rq"], writes=["rq"])
                for h0 in (0, 512):
                    P.act(lambda e, n=n, h0=h0: e.activation(rq[:, h0:h0 + n], rq[:, h0:h0 + n], AF.Ln,
                                                             bias=float(HD) * EPS),
                          reads=["rq"], writes=["rq"])
                    P.act(lambda e, n=n, h0=h0: e.activation(rq[:, h0:h0 + n], rq[:, h0:h0 + n], AF.Exp, scale=-0.5),
                          reads=["rq"], writes=["rq"])
                    P.dve(lambda e, n=n, h0=h0: e.tensor_tensor(rq[:, h0:h0 + n], rq[:, h0:h0 + n], rr[:, 0:n],
                                                                ALU.mult),
                          reads=["rq", "rr"], writes=["rq"])
                P.dve(lambda e, t0=t0, n=n: e.scalar_tensor_tensor(
                    qfT[:, t0:t0 + n], psf[2][:, 0:n], vec[:, 1:2], rq[:, 0:n], ALU.mult, ALU.mult),
                    reads=[("psf", 2), "rq", "vec"], writes=kt("qfT"))
                P.dve(lambda e, n=n: e.scalar_tensor_tensor(
                    kf32[:, 0:n], psf[3][:, 0:n], kcol[:, 0:1], rq[:, 512:512 + n], ALU.mult, ALU.mult),
                    reads=[("psf", 3), "rq", "kcol"], writes=["kf32"])
                P.act(lambda e, t0=t0, n=n: e.activation(kfT[:, t0:t0 + n], kf32[:, 0:n], AF.Copy),
                      reads=["kf32"], writes=kt("kfT"))
                P.dma("sp", o_kT[1, :, t0:t0 + n], kf32[:, 0:n], reads=["kf32"], semkey="kf32")

                if not is_samp:
                    for tb in range(n // 128):
                        pb = tb % 2
                        blk = t0 // 128 + tb

                        def mm_v(e, tb=tb, pb=pb, s=s):
                            e.matmul(psv[pb][:, 300:301], rr[0:1, tb * 128:(tb + 1) * 128], one1,
                                     start=True, stop=True)
                            for ch in range(DC):
                                ins = e.matmul(psv[pb][:, 0:257], xt[s][:, ch, tb * 128:(tb + 1) * 128],
                                               wbf[:, ch, 512:769], start=(ch == 0), stop=(ch == DC - 1))
                            return ins
                        P.pe(mm_v, reads=[("xt", s), "rr", "cst32"] + WB, writes=[("psv", pb)])
                        P.act(lambda e, pb=pb: e.activation(rcol[:, pb:pb + 1], psv[pb][:, 300:301], AF.Copy),
                              reads=[("psv", pb)], writes=[("rcol", pb)])
                        P.act(lambda e, pb=pb, blk=blk: e.activation(vv[:, blk, :], psv[pb][:, 0:256], AF.Copy,
                                                                    scale=rcol[:, pb:pb + 1]),
                              reads=[("psv", pb), ("rcol", pb)], writes=[("vv", blk)])
                        P.dve(lambda e, pb=pb: e.tensor_scalar(v32[pb][:], psv[pb][:, 0:256], rcol[:, pb:pb + 1], None,
                                                               ALU.mult),
                              reads=[("psv", pb), ("rcol", pb)], writes=[("v32", pb)])
                        P.dve(lambda e, pb=pb, blk=blk: e.tensor_scalar(fraw[:, blk:blk + 1], psv[pb][:, 256:257],
                                                                        rcol[:, pb:pb + 1], None, ALU.mult),
                              reads=[("psv", pb), ("rcol", pb)], writes=[("fraw", blk)])
                        P.dma("sp", o_v[t0 + tb * 128:t0 + (tb + 1) * 128, :], v32[pb][:],
                              reads=[("v32", pb)], semkey=("v32", pb))
                else:
                    for b in range(B_S):
                        pb = b % 2

                        def mm_vs(e, b=b, pb=pb, s=s):
                            e.matmul(psv[pb][0:T_S, 300:301], rr[0:1, b * T_S:(b + 1) * T_S], one1,
                                     start=True, stop=True)
                            for ch in range(DC):
                                ins = e.matmul(psv[pb][0:T_S, 0:257], xt[s][:, ch, b * T_S:(b + 1) * T_S],
                                               wbf[:, ch, 512:769], start=(ch == 0), stop=(ch == DC - 1))
                            return ins
                        P.pe(mm_vs, reads=[("xt", s), "rr", "cst32"] + WB, writes=[("psv", pb)])
                        P.act(lambda e, pb=pb: e.activation(rcol[0:T_S, pb:pb + 1], psv[pb][0:T_S, 300:301], AF.Copy),
                              reads=[("psv", pb)], writes=[("rcol", pb)])
                        P.act(lambda e, b=b, pb=pb: e.activation(vnb[:, b, :], psv[pb][0:T_S, 0:256], AF.Copy,
                                                                 scale=rcol[0:T_S, pb:pb + 1]),
                              reads=[("psv", pb), ("rcol", pb)], writes=[("vnb", b)])
                        P.dve(lambda e, pb=pb: e.tensor_scalar(vn32[pb][:], psv[pb][0:T_S, 0:256],
                                                               rcol[0:T_S, pb:pb + 1], None, ALU.mult),
                              reads=[("psv", pb), ("rcol", pb)], writes=[("vn32", pb)])
                        P.dve(lambda e, b=b, pb=pb: e.tensor_scalar(fraws[:, b:b + 1], psv[pb][0:T_S, 256:257],
                                                                    rcol[0:T_S, pb:pb + 1], None, ALU.mult),
                              reads=[("psv", pb), ("rcol", pb)], writes=[("fraws", b)])
                        P.dma("sp", o_v[T_P + b * T_S:T_P + (b + 1) * T_S, :], vn32[pb][:],
                              reads=[("vn32", pb)], semkey=("vn32", pb))


            FR = [("fraw", blk) for blk in range(NB)]
            P.act(lambda e: e.activation(ftmp[:], fraw[:], AF.Exp, bias=negb[:, 0:1], scale=-1.0),
                  reads=FR + ["negb"], writes=["ftmp"])
            P.act(lambda e: e.activation(ftmp[:], ftmp[:], AF.Ln, bias=1.0), reads=["ftmp"], writes=["ftmp"])
            P.dve(lambda e: e.tensor_scalar(lfcol[:], ftmp[:], -1.0, None, ALU.mult),
                  reads=["ftmp"], writes=["lfcol"])
            P.dma("sp", o_lf, lfcol[:], reads=["lfcol"], semkey="o_lf")
            FRS = [("fraws", b) for b in range(B_S)]
            P.act(lambda e: e.activation(ftmps[:], fraws[:], AF.Exp, bias=negb[0:T_S, 0:1], scale=-1.0),
                  reads=FRS + ["negb"], writes=["ftmps"])
            P.act(lambda e: e.activation(ftmps[:], ftmps[:], AF.Ln, bias=1.0), reads=["ftmps"], writes=["ftmps"])
            P.dve(lambda e: e.tensor_scalar(lfs[:], ftmps[:], -1.0, None, ALU.mult),
                  reads=["ftmps"], writes=["lfs"])
            P.dma("sp", o_lfs, lfs[:], reads=["lfs"], semkey="o_lfs")
            P.emit()

        import os as _os
        if _os.environ.get("STOP_A1"):
            return
        es2 = contextlib.ExitStack()
        with es2:
            def sb2(name, shape, dt):
                return es2.enter_context(nc.sbuf_tensor("sA2_" + name, shape, dt))

            def ps2(name, shape, dt=F32):
                return es2.enter_context(nc.psum_tensor("pA2_" + name, shape, dt))

            tot = sb2("tot", [128, NB], F32)
            onesr = sb2("onesr", [128, NB], F32)
            tmpf = sb2("tmpf", [128, NB], F32)
            clf = sb2("clf", [128, B_S, NPB], F32)
            tots = sb2("tots", [128, B_S * NPB], F32)
            segm = sb2("segm", [128, B_S, NPB], F32)
            incls = sb2("incls", [128, B_S * NPB], F32)
            fcums = sb2("fcums", [128, B_S, NPB], F32)
            fref = sb2("fref", [128, B_S], F32)
            cumn = sb2("cumn", [T_S, B_S], F32)
            biasq = [sb2(f"biasq{i}", [128, NB], F32) for i in range(2)]
            e_sb = [sb2(f"e_sb{i}", [128, 512], BF16) for i in range(2)]
            sp_sb = [sb2(f"sp_sb{i}", [128, 512], BF16) for i in range(2)]
            S_sb = [sb2(f"S_sb{i}", [128, 512], BF16) for i in range(2)]
            X_sb = [sb2(f"X_sb{i}", [128, 512], BF16) for i in range(2)]
            W_sb = [sb2(f"W_sb{i}", [128, 512], BF16) for i in range(2)]
            P_fx = [sb2(f"P_fx{i}", [128, 512], BF16) for i in range(2)]
            rden = sb2("rden", [128, 512], F32)
            ogs = [sb2(f"ogs{i}", [128, 512], BF16) for i in range(2)]
            ogf = [sb2(f"ogf{i}", [128, 512], BF16) for i in range(2)]
            ogsamp = [sb2(f"ogsamp{i}", [128, NSAMP], BF16) for i in range(2)]
            ckT = [[sb2(f"ckT{br}_{i}", [128, PAST], BF16) for i in range(2)] for br in range(2)]
            cvv = [[sb2(f"cvv{br}_{i}", [128, NPB, 128], BF16) for i in range(2)] for br in range(2)]
            z_ps = [ps2(f"z_ps{i}", [128, 512]) for i in range(2)]
            c_ps = [ps2(f"c_ps{i}", [128, 512]) for i in range(2)]
            o_ps = [ps2(f"o_ps{i}", [128, 512]) for i in range(2)]
            den_ps = ps2("den_ps", [128, 512])
            misc_ps = ps2("misc_ps", [128, 512])
            en_s = sb2("en_s", [T_S, T_S], BF16)
            spn_s = sb2("spn_s", [T_S, T_S], BF16)
            xn_s = sb2("xn_s", [T_S, T_S], BF16)
            wn_s = sb2("wn_s", [T_S, T_S], BF16)
            pn_s = sb2("pn_s", [T_S, T_S], BF16)
            spnb = sb2("spnb", [T_S, 512], BF16)
            tz = rden

            P = Prog(nc, "a2")
            NBS = B_S * NPB
            P.dve(lambda e: e.memset(onesr[:], 1.0), writes=["onesr"])

            def mm_cum(e):
                e.matmul(c_ps[0][:, 0:NB], mle32, lfcol[:], start=True, stop=True)
                return e.matmul(c_ps[1][:, 0:NB], ones32, lfcol[:], start=True, stop=True)
            P.pe(mm_cum, reads=["lfcol", "cst32"], writes=[("c_ps", 0), ("c_ps", 1)])
            P.dve(lambda e: e.tensor_copy(tot[:], c_ps[1][:, 0:NB]), reads=[("c_ps", 1)], writes=["tot"])
            P.dve(lambda e: e.tensor_tensor_scan(incl[:], onesr[:, 0:NB], tot[:], 0.0, ALU.mult, ALU.add),
                  reads=["onesr", "tot"], writes=["incl"])
            P.dve(lambda e: e.tensor_tensor(tmpf[:], incl[:], tot[:], ALU.subtract),
                  reads=["incl", "tot"], writes=["tmpf"])
            P.dve(lambda e: e.tensor_tensor(fcum[:], c_ps[0][:, 0:NB], tmpf[:], ALU.add),
                  reads=[("c_ps", 0), "tmpf"], writes=["fcum"])
            P.dma("sp", clf[:], clfd, writes=["clf"], semkey="clf")
            P.dve(lambda e: e.memset(segm[:], 1.0), writes=["segm"])
            P.dve(lambda e: e.memset(segm[:, :, 0:1], 0.0), reads=["segm"], writes=["segm"])
            clf2 = clf[:].rearrange("p b k -> p (b k)")

            def mm_cums(e):
                e.matmul(c_ps[0][:, 0:NBS], mle32, clf2, start=True, stop=True)
                return e.matmul(c_ps[1][:, 0:NBS], ones32, clf2, start=True, stop=True)
            P.pe(mm_cums, reads=["clf", "cst32"], writes=[("c_ps", 0), ("c_ps", 1)])
            P.dve(lambda e: e.tensor_copy(tots[:], c_ps[1][:, 0:NBS]), reads=[("c_ps", 1)], writes=["tots"])
            P.dve(lambda e: e.tensor_tensor_scan(incls[:], segm[:].rearrange("p b k -> p (b k)"), tots[:], 0.0,
                                                 ALU.mult, ALU.add),
                  reads=["segm", "tots"], writes=["incls"])
            P.dve(lambda e: e.tensor_tensor(tots[:], incls[:], tots[:], ALU.subtract),
                  reads=["incls", "tots"], writes=["tots"])
            P.dve(lambda e: e.tensor_tensor(fcums[:].rearrange("p b k -> p (b k)"), c_ps[0][:, 0:NBS], tots[:],
                                            ALU.add),
                  reads=[("c_ps", 0), "tots"], writes=["fcums"])
            def mm_new(e):
                e.matmul(z_ps[0][0:T_S, 0:B_S], cst32[0:T_S, 1, 0:T_S], lfs[:], start=True, stop=True)
                return e.matmul(z_ps[1][:, 0:B_S], cst32[0:T_S, 0, :], lfs[:], start=True, stop=True)
            P.pe(mm_new, reads=["lfs", "cst32"], writes=[("z_ps", 0), ("z_ps", 1)])
            P.dve(lambda e: e.tensor_copy(cumn[:], z_ps[0][0:T_S, 0:B_S]), reads=[("z_ps", 0)], writes=["cumn"])
            incls3 = incls[:].rearrange("p (b k) -> p b k", b=B_S)
            P.dve(lambda e: e.tensor_tensor(fref[:], z_ps[1][:, 0:B_S], incls3[:, :, NPB - 1], ALU.add),
                  reads=[("z_ps", 1), "incls"], writes=["fref"])
            P.dve(lambda e: e.tensor_tensor(biasp[:], fref[:].unsqueeze(2).broadcast_to([128, B_S, NPB]),
                                            fcums[:], ALU.subtract),
                  reads=["fref", "fcums"], writes=["biasp"])
            P.dve(lambda e: e.tensor_tensor(biasn[:], z_ps[1][0:T_S, 0:B_S], cumn[:], ALU.subtract),
                  reads=[("z_ps", 1), "cumn"], writes=["biasn"])


            ogd = [og_dst(0), og_dst(1)]

            def tkeys(name, lo, hi):
                return [(name, t) for t in range(lo // 256, (hi + 255) // 256)]

            def prompt_items(qt):
                q0 = qt * 512
                nkb = 4 * qt + 4
                its = []
                for j, kb in enumerate(range(nkb - 1, -1, -1)):
                    i = kb - 4 * qt
                    its.append(dict(qt=qt, q0=q0, kb=kb, k0=kb * 128, c0=(128 * i if i > 0 else 0),
                                    diag=(i >= 0), first=(j == 0), last=(kb == 0), idx=j))
                return its

            gcnt = {"sb": 0, "fx": 0}

            def sb_s1a(it):
                si = it["g"] % 2
                c0, q0, k0 = it["c0"], it["q0"], it["k0"]
                P.pe(lambda e: e.matmul(z_ps[0][:, c0:512], ksT[:, k0:k0 + 128], qsT[:, q0 + c0:q0 + 512],
                                        start=True, stop=True),
                     reads=tkeys("ksT", k0, k0 + 128) + tkeys("qsT", q0, q0 + 512), writes=[("z_ps", 0)])

            def sb_s1b(it):
                si = it["g"] % 2
                c0 = it["c0"]
                P.act(lambda e: e.activation(e_sb[si][:, c0:512], z_ps[0][:, c0:512], AF.Exp),
                      reads=[("z_ps", 0)], writes=[("e_sb", si)])
                if it["diag"]:
                    P.dve(lambda e: e.tensor_tensor(e_sb[si][:, c0:c0 + 128], e_sb[si][:, c0:c0 + 128],
                                                    cst32[:, 3, :], ALU.mult),
                          reads=[("e_sb", si), "cst32"], writes=[("e_sb", si)])

            def sb_s1c(it):
                si = it["g"] % 2
                c0 = it["c0"]
                P.act(lambda e: e.activation(sp_sb[si][:, c0:512], e_sb[si][:, c0:512], AF.Ln, bias=1.0),
                      reads=[("e_sb", si)], writes=[("sp_sb", si)])

            def sb_s2(it):
                si = it["g"] % 2
                sprev = 1 - si
                c0 = it["c0"]
                first, last = it["first"], it["last"]

                def mm_c(e):
                    ins = e.matmul(c_ps[si][:, c0:512], tge_bf[:], sp_sb[si][:, c0:512], start=True, stop=first)
                    if not first:
                        ins = e.matmul(c_ps[si][:, c0:512], ones_bf[:], S_sb[sprev][:, c0:512],
                                       start=False, stop=True)
                    return ins
                P.pe(mm_c, reads=[("sp_sb", si), "tge_bf", "ones_bf"] + ([] if first else [("S_sb", sprev)]),
                     writes=[("c_ps", si)])
                P.act(lambda e: e.activation(X_sb[si][:, c0:512], c_ps[si][:, c0:512], AF.Exp, scale=-1.0),
                      reads=[("c_ps", si)], writes=[("X_sb", si)])
                P.dve(lambda e: e.tensor_tensor(W_sb[si][:, c0:512], e_sb[si][:, c0:512], X_sb[si][:, c0:512],
                                                ALU.mult),
                      reads=[("e_sb", si), ("X_sb", si)], writes=[("W_sb", si)])
                if not last:
                    if first:
                        P.dve(lambda e: e.memset(S_sb[si][:, 0:c0], 0.0), writes=[("S_sb", si)])
                        P.dve(lambda e: e.tensor_copy(S_sb[si][:, c0:512], sp_sb[si][:, c0:512]),
                              reads=[("sp_sb", si), ("S_sb", si)], writes=[("S_sb", si)])
                    elif c0 > 0:
                        P.dve(lambda e: e.tensor_copy(S_sb[si][:, 0:c0], S_sb[sprev][:, 0:c0]),
                              reads=[("S_sb", sprev)], writes=[("S_sb", si)])
                        P.dve(lambda e: e.tensor_tensor(S_sb[si][:, c0:512], S_sb[sprev][:, c0:512],
                                                        sp_sb[si][:, c0:512], ALU.add),
                              reads=[("S_sb", sprev), ("sp_sb", si), ("S_sb", si)], writes=[("S_sb", si)])
                    else:
                        P.dve(lambda e: e.tensor_tensor(S_sb[si][:], S_sb[sprev][:], sp_sb[si][:], ALU.add),
                              reads=[("S_sb", sprev), ("sp_sb", si)], writes=[("S_sb", si)])

            def sb_s3(it):
                si = it["g"] % 2
                c0, kb, qt = it["c0"], it["kb"], it["qt"]
                P.pe(lambda e: e.matmul(o_ps[0][:, c0:512], vv[:, kb, 0:128], W_sb[si][:, c0:512],
                                        start=it["first"], stop=it["last"]),
                     reads=[("vv", kb), ("W_sb", si)], writes=[("o_ps", 0)])
                if it["last"]:
                    osl = qt % 2
                    q0 = it["q0"]
                    P.act(lambda e: e.activation(ogs[osl][:], o_ps[0][:], AF.Copy),
                          reads=[("o_ps", 0)], writes=[("ogs", osl)])
                    P.dma("sp", ogd[0][:, q0 // 2:(q0 + 512) // 2], ogs[osl][:].bitcast(F32),
                          reads=[("ogs", osl)], semkey=("ogs", osl))

            def fx_s1a(it):
                c0, q0, k0 = it["c0"], it["q0"], it["k0"]
                P.pe(lambda e: e.matmul(z_ps[1][:, c0:512], kfT[:, k0:k0 + 128], qfT[:, q0 + c0:q0 + 512],
                                        start=True, stop=True),
                     reads=tkeys("kfT", k0, k0 + 128) + tkeys("qfT", q0, q0 + 512), writes=[("z_ps", 1)])

            def fx_s1b(it):
                pi = it["g"] % 2
                c0, kb = it["c0"], it["kb"]
                osl = it["qt"] % 2
                P.act(lambda e: e.activation(P_fx[pi][:, c0:512], z_ps[1][:, c0:512], AF.Exp,
                                             bias=biasq[osl][:, kb:kb + 1]),
                      reads=[("z_ps", 1), ("biasq", osl)], writes=[("P_fx", pi)])
                if it["diag"]:
                    P.dve(lambda e: e.tensor_tensor(P_fx[pi][:, c0:c0 + 128], P_fx[pi][:, c0:c0 + 128],
                                                    cst32[:, 1, :], ALU.mult),
                          reads=[("P_fx", pi), "cst32"], writes=[("P_fx", pi)])

            def fx_s2(it):
                pi = it["g"] % 2
                c0, kb, qt = it["c0"], it["kb"], it["qt"]

                def mm_o(e):
                    e.matmul(o_ps[1][:, c0:512], vv[:, kb, 128:256], P_fx[pi][:, c0:512],
                             start=it["first"], stop=it["last"])
                    return e.matmul(den_ps[:, c0:512], ones_bf[:], P_fx[pi][:, c0:512],
                                    start=it["first"], stop=it["last"])
                P.pe(mm_o, reads=[("vv", kb), ("P_fx", pi), "ones_bf"], writes=[("o_ps", 1), "den_ps"])
                if it["last"]:
                    osl = qt % 2
                    q0 = it["q0"]
                    P.dve(lambda e: e.reciprocal(rden[:], den_ps[:]), reads=["den_ps"], writes=["rden"])
                    P.dve(lambda e: e.tensor_tensor(ogf[osl][:], o_ps[1][:], rden[:], ALU.mult),
                          reads=[("o_ps", 1), "rden"], writes=[("ogf", osl)])
                    P.dma("sp", ogd[1][:, q0 // 2:(q0 + 512) // 2], ogf[osl][:].bitcast(F32),
                          reads=[("ogf", osl)], semkey=("ogf", osl))

            def run_pipeline(items, stages, tag):
                for it in items:
                    it["g"] = gcnt[tag]
                    gcnt[tag] += 1
                ns = len(stages)
                n = len(items)
                for step in range(n + ns - 1):
                    for si_, st in enumerate(stages):
                        k = step - si_
                        if 0 <= k < n:
                            st(items[k])

            NPQ = NPB * T_S
            assert NPQ <= 512
            nsteps = max(1, (NPB - 1).bit_length())

            def sample_unit(b):
                sl = b % 2
                qa, qb_ = T_P + b * T_S, T_P + (b + 1) * T_S
                for br in range(2):
                    kd, vd = csrc[br]
                    for hh in range(0, PAST, 2048):
                        he = min(PAST, hh + 2048)
                        P.dma("pool", ckT[br][sl][:, hh:he], kd[b, :, hh:he],
                              writes=[("ckT", br, sl)], semkey=("ckT", br, sl))
                    cvf = cvv[br][sl][:].rearrange("p k d -> p (k d)")
                    for hh in range(0, NPB * 128, 2048):
                        he = min(NPB * 128, hh + 2048)
                        P.dma("pool", cvf[:, hh:he], vd[b, :, hh:he],
                              writes=[("cvv", br, sl)], semkey=("cvv", br, sl))
                mark0 = len(P.ops)
                zi = 0
                kk = [("ckT", 0, sl)]
                qk = [("qsT", NT - 1)]

                def mm_z(e, K=ckT[0][sl], qT_=qsT, kT_=ksT, zi=zi):
                    for blk in range(NPB):
                        e.matmul(z_ps[zi][:, blk * T_S:(blk + 1) * T_S], K[:, blk * 128:(blk + 1) * 128],
                                 qT_[:, qa:qb_], start=True, stop=True)
                    return e.matmul(misc_ps[0:T_S, 0:T_S], kT_[:, qa:qb_], qT_[:, qa:qb_], start=True, stop=True)
                P.pe(mm_z, reads=kk + qk + [("ksT", NT - 1)], writes=[("z_ps", zi), "misc_ps"])
                P.act(lambda e: e.activation(e_sb[0][:, 0:NPQ], z_ps[0][:, 0:NPQ], AF.Exp),
                      reads=[("z_ps", 0)], writes=[("e_sb", 0)])
                P.act(lambda e: e.activation(en_s[:], misc_ps[0:T_S, 0:T_S], AF.Exp),
                      reads=["misc_ps"], writes=["en_s"])
                P.dve(lambda e: e.tensor_tensor(en_s[:], en_s[:], cst32[0:T_S, 3, 0:T_S], ALU.mult),
                      reads=["en_s", "cst32"], writes=["en_s"])
                P.act(lambda e: e.activation(sp_sb[0][:, 0:NPQ], e_sb[0][:, 0:NPQ], AF.Ln, bias=1.0),
                      reads=[("e_sb", 0)], writes=[("sp_sb", 0)])
                P.act(lambda e: e.activation(spn_s[:], en_s[:], AF.Ln, bias=1.0), reads=["en_s"], writes=["spn_s"])
                P.dve(lambda e: e.tensor_copy(spnb[:, 0:NPQ].rearrange("p (k q) -> p k q", q=T_S),
                                              spn_s[:].unsqueeze(1).broadcast_to([T_S, NPB, T_S])),
                      reads=["spn_s"], writes=["spnb"])
                src, srck = sp_sb[0], ("sp_sb", 0)
                for stp in range(nsteps):
                    sh = 1 << stp
                    dst, dstk = (S_sb[stp % 2], ("S_sb", stp % 2))
                    w = (NPB - sh) * T_S
                    P.dve(lambda e, src=src, dst=dst, w=w, sh=sh: e.tensor_tensor(
                        dst[:, 0:w], src[:, 0:w], src[:, sh * T_S:sh * T_S + w], ALU.add),
                        reads=[srck], writes=[dstk])
                    P.dve(lambda e, src=src, dst=dst, w=w: e.tensor_copy(dst[:, w:NPQ], src[:, w:NPQ]),
                          reads=[srck, dstk], writes=[dstk])
                    src, srck = dst, dstk
                incl, inclk = src, srck

                def mm_c(e, incl=incl):
                    e.matmul(c_ps[0][:, 0:NPQ], tge_bf[:], sp_sb[0][:, 0:NPQ], start=True, stop=False)
                    e.matmul(c_ps[0][:, 0:NPQ - T_S], ones_bf[:], incl[:, T_S:NPQ], start=False, stop=False)
                    e.matmul(c_ps[0][:, 0:NPQ], ones_bf[0:T_S, :], spnb[:, 0:NPQ], start=False, stop=True)
                    return e.matmul(misc_ps[0:T_S, T_S:2 * T_S], tge_bf[0:T_S, 0:T_S], spn_s[:],
                                    start=True, stop=True)
                P.pe(mm_c, reads=[("sp_sb", 0), inclk, "spnb", "spn_s", "tge_bf", "ones_bf"],
                     writes=[("c_ps", 0), "misc_ps"])
                P.act(lambda e: e.activation(X_sb[0][:, 0:NPQ], c_ps[0][:, 0:NPQ], AF.Exp, scale=-1.0),
                      reads=[("c_ps", 0)], writes=[("X_sb", 0)])
                P.act(lambda e: e.activation(xn_s[:], misc_ps[0:T_S, T_S:2 * T_S], AF.Exp, scale=-1.0),
                      reads=["misc_ps"], writes=["xn_s"])
                P.dve(lambda e: e.tensor_tensor(W_sb[0][:, 0:NPQ], e_sb[0][:, 0:NPQ], X_sb[0][:, 0:NPQ], ALU.mult),
                      reads=[("e_sb", 0), ("X_sb", 0)], writes=[("W_sb", 0)])
                P.dve(lambda e: e.tensor_tensor(wn_s[:], en_s[:], xn_s[:], ALU.mult),
                      reads=["en_s", "xn_s"], writes=["wn_s"])

                def mm_o(e, V=cvv[0][sl]):
                    for blk in range(NPB):
                        e.matmul(o_ps[0][:, 0:T_S], V[:, blk, :], W_sb[0][:, blk * T_S:(blk + 1) * T_S],
                                 start=(blk == 0), stop=False)
                    return e.matmul(o_ps[0][:, 0:T_S], vnb[:, b, 0:128], wn_s[:], start=False, stop=True)
                P.pe(mm_o, reads=[("cvv", 0, sl), ("W_sb", 0), "wn_s", ("vnb", b)], writes=[("o_ps", 0)])
                P.act(lambda e: e.activation(ogsamp[0][:, b * T_S:(b + 1) * T_S], o_ps[0][:, 0:T_S], AF.Copy),
                      reads=[("o_ps", 0)], writes=[("ogsamp", 0, b)])
                mark1 = len(P.ops)
                zi = 1

                def mm_zf(e, K=ckT[1][sl], zi=zi):
                    for blk in range(NPB):
                        e.matmul(z_ps[zi][:, blk * T_S:(blk + 1) * T_S], K[:, blk * 128:(blk + 1) * 128],
                                 qfT[:, qa:qb_], start=True, stop=True)
                    return e.matmul(misc_ps[0:T_S, 2 * T_S:3 * T_S], kfT[:, qa:qb_], qfT[:, qa:qb_],
                                    start=True, stop=True)
                P.pe(mm_zf, reads=[("ckT", 1, sl), ("qfT", NT - 1), ("kfT", NT - 1)],
                     writes=[("z_ps", zi), "misc_ps"])
                P.dve(lambda e: e.tensor_tensor(tz[:, 0:NPQ].rearrange("p (k q) -> p k q", q=T_S),
                                                z_ps[1][:, 0:NPQ].rearrange("p (k q) -> p k q", q=T_S),
                                                biasp[:, b, :].unsqueeze(2).broadcast_to([128, NPB, T_S]),
                                                ALU.add),
                      reads=[("z_ps", 1), "biasp"], writes=["rden"])
                P.act(lambda e: e.activation(P_fx[0][:, 0:NPQ], tz[:, 0:NPQ], AF.Exp), reads=["rden"], writes=[("P_fx", 0)])
                P.act(lambda e: e.activation(pn_s[:], misc_ps[0:T_S, 2 * T_S:3 * T_S], AF.Exp,
                                             bias=biasn[:, b:b + 1]),
                      reads=["misc_ps", "biasn"], writes=["pn_s"])
                P.dve(lambda e: e.tensor_tensor(pn_s[:], pn_s[:], cst32[0:T_S, 1, 0:T_S], ALU.mult),
                      reads=["pn_s", "cst32"], writes=["pn_s"])

                def mm_of(e, V=cvv[1][sl]):
                    for blk in range(NPB):
                        e.matmul(o_ps[1][:, 0:T_S], V[:, blk, :], P_fx[0][:, blk * T_S:(blk + 1) * T_S],
                                 start=(blk == 0), stop=False)
                    e.matmul(o_ps[1][:, 0:T_S], vnb[:, b, 128:256], pn_s[:], start=False, stop=True)
                    for blk in range(NPB):
                        e.matmul(den_ps[:, 0:T_S], ones_bf[:], P_fx[0][:, blk * T_S:(blk + 1) * T_S],
                                 start=(blk == 0), stop=False)
                    return e.matmul(den_ps[:, 0:T_S], ones_bf[0:T_S, :], pn_s[:], start=False, stop=True)
                P.pe(mm_of, reads=[("cvv", 1, sl), ("P_fx", 0), "pn_s", ("vnb", b), "ones_bf"],
                     writes=[("o_ps", 1), "den_ps"])
                P.dve(lambda e: e.reciprocal(rden[:, 0:T_S], den_ps[:, 0:T_S]), reads=["den_ps"], writes=["rden"])
                P.dve(lambda e: e.tensor_tensor(ogsamp[1][:, b * T_S:(b + 1) * T_S], o_ps[1][:, 0:T_S],
                                                rden[:, 0:T_S], ALU.mult),
                      reads=[("o_ps", 1), "rden"], writes=[("ogsamp", 1, b)])
                a_ops, b_ops = P.ops[mark0:mark1], P.ops[mark1:]
                merged = []
                ia = ib = 0
                while ia < len(a_ops) or ib < len(b_ops):
                    if ia < len(a_ops):
                        merged.append(a_ops[ia]); ia += 1
                    if ia < len(a_ops) and len(a_ops) > 2 * len(b_ops) - 2:
                        merged.append(a_ops[ia]); ia += 1
                    if ib < len(b_ops):
                        merged.append(b_ops[ib]); ib += 1
                P.ops[mark0:] = merged

            csrc = [(cskT, csv), (cfkT, cfv)]
            sample_done = 0
            for qt in range(NQT):
                osl = qt % 2
                nkb = 4 * qt + 4
                P.dve(lambda e, osl=osl, nkb=nkb: e.tensor_scalar(
                    biasq[osl][:, 0:nkb], fcum[:, 0:nkb], -1.0, incl[:, nkb - 1:nkb], ALU.mult, ALU.add),
                    reads=["fcum", "incl"], writes=[("biasq", osl)])
                def st1(it):
                    sb_s1a(it)
                    fx_s1a(it)
                    sb_s1b(it)
                    fx_s1b(it)
                    sb_s1c(it)

                def st2(it):
                    sb_s2(it)
                    fx_s2(it)
                run_pipeline(prompt_items(qt), [st1, st2, sb_s3], "sb")
                tgt = (B_S * (qt + 1)) // NQT
                while sample_done < tgt:
                    sample_unit(sample_done)
                    sample_done += 1
            while sample_done < B_S:
                sample_unit(sample_done)
                sample_done += 1
            for br in range(2):
                P.dma("sp", ogd[br][:, T_P // 2:NTOK // 2], ogsamp[br][:].bitcast(F32),
                      reads=[("ogsamp", br, b) for b in range(B_S)], semkey=("ogsamp", br))

            P.emit()


def build_C(nc, NPC, og_src):
    NPP = NPC - 32
    ntile = (NPC + 511) // 512
    TW = NPC // ntile
    assert TW * ntile == NPC
    NPBK = NPP // 128

    def din(name, shape, dt=F32):
        return nc.dram_tensor(name, shape, dt, kind="ExternalInput").ap()

    xmT = din("xmT", [128, DC, NPC])
    xm = din("xm", [NPC, D])
    nwd = din("nw", [128, DC])
    cstd = din("cst", [128, 4, 128])
    wg = din("wg", [DC, 128, DC, 128])
    wm = din("wm", [2, DC, 128, DC, 128])
    wb = din("wb", [2, DC, 128, 8, 128])
    wo = din("wo", [4, 128, DC, 512])
    o_y = nc.dram_tensor("o_y", [NPC, D], F32, kind="ExternalOutput").ap()

    es = contextlib.ExitStack()
    with es:
        def sb(name, shape, dt):
            return es.enter_context(nc.sbuf_tensor("sC_" + name, shape, dt))

        def ps(name, shape, dt=F32):
            return es.enter_context(nc.psum_tensor("pC_" + name, shape, dt))

        hT = sb("hT", [128, DC, NPC], BF16)
        og = sb("og", [128, DC, NPC], BF16)
        mg = sb("mg", [128, DC, NPC], BF16)
        cst32 = sb("cst32", [128, 4, 128], F32)
        ones_bf = sb("ones_bf", [128, 128], BF16)
        nw = sb("nw", [128, DC], F32)
        nws = sb("nws", [128, DC], F32)
        xt = [sb(f"xt{i}", [128, TW], F32) for i in range(2)]
        rr = sb("rr", [128, TW], F32)
        wgb = [sb(f"wgb{i}", [128, DC, 128], BF16) for i in range(2)]
        wmb = [[sb(f"wmb{br}_{i}", [128, DC, 128], BF16) for i in range(2)] for br in range(2)]
        wbb = [[sb(f"wbb{br}_{i}", [128, 8, 128], BF16) for i in range(2)] for br in range(2)]
        sg = [sb(f"sg{i}", [128, TW], BF16) for i in range(2)]
        sig = [sb(f"sig{i}", [128, TW], F32) for i in range(2)]
        t1 = [sb(f"t1_{i}", [128, TW], F32) for i in range(2)]
        wob = [sb(f"wob{i}", [128, DC, 512], BF16) for i in range(2)]
        xres = [sb(f"xres{i}", [128, 512], F32) for i in range(2)]
        yst = [sb(f"yst{i}", [128, 512], F32) for i in range(2)]
        pA = [ps(f"pA{i}", [128, 512]) for i in range(2)]
        pU = [ps(f"pU{i}", [128, 512]) for i in range(2)]
        pM = [ps(f"pM{i}", [128, 512]) for i in range(2)]
        pY = [ps(f"pY{i}", [128, 512]) for i in range(2)]

        P = Prog(nc, "c")
        P.dma("sp", cst32[:], cstd, writes=["cst32"], semkey="cst32")
        P.dma("sp", nw[:], nwd, writes=["nw"], semkey="nw")
        P.act(lambda e: e.activation(ones_bf[:], cst32[:, 0, :], AF.Copy), reads=["cst32"], writes=["ones_bf"])
        P.dve(lambda e: e.tensor_scalar(nws[:], nw[:], float(D) ** 0.5, None, ALU.mult),
              reads=["nw"], writes=["nws"])
        for ch in range(DC):
            P.dma("pool", og[:, ch, :].bitcast(F32), og_src(ch), writes=[("og", ch)], semkey=("og", ch))
        for ti in range(ntile):
            c0 = ti * TW
            for ch in range(DC):
                s = ch % 2
                P.dma("sp", xt[s][:], xmT[:, ch, c0:c0 + TW], writes=[("xt", s)], semkey=("xt", s))
                P.act(lambda e, s=s, ch=ch, c0=c0: e.activation(mg[:, ch, c0:c0 + TW], xt[s][:], AF.Square),
                      reads=[("xt", s)], writes=[("sq", ch)])
                P.dve(lambda e, s=s, ch=ch, c0=c0: e.tensor_scalar(
                    hT[:, ch, c0:c0 + TW], xt[s][:], nws[:, ch:ch + 1], None, ALU.mult),
                    reads=[("xt", s), "nws"], writes=[("hT", ti)])

            def mm_ss(e, c0=c0):
                for ch in range(DC):
                    ins = e.matmul(pA[0][:, 0:TW], ones_bf[:], mg[:, ch, c0:c0 + TW],
                                   start=(ch == 0), stop=(ch == DC - 1))
                return ins
            P.pe(mm_ss, reads=[("sq", ch) for ch in range(DC)] + ["ones_bf"], writes=[("pA", 0)])
            P.act(lambda e: e.activation(rr[:], pA[0][:, 0:TW], AF.Ln, bias=float(D) * EPS),
                  reads=[("pA", 0)], writes=["rr"])
            P.act(lambda e: e.activation(rr[:], rr[:], AF.Exp, scale=-0.5), reads=["rr"], writes=["rr"])
            P.dve(lambda e, c0=c0: e.tensor_tensor(hT[:, :, c0:c0 + TW], hT[:, :, c0:c0 + TW],
                                                   _bc_mid(rr[:], DC), ALU.mult),
                  reads=[("hT", ti), "rr"], writes=[("hT", ti)])
        HT = [("hT", ti) for ti in range(ntile)]

        wq = {"n": 0}

        def load_w(dst_bf, src_ap, key, nch):
            P.dma("pool", dst_bf[:], src_ap, writes=[key], semkey=key)

        for cb in range(DC):
            s = cb % 2
            load_w(wgb[s], wg[cb], ("wgb", s), DC)
            for ti in range(ntile):
                c0 = ti * TW
                pi = ti % 2

                def mm_g(e, s=s, c0=c0, pi=pi):
                    for ch in range(DC):
                        ins = e.matmul(pA[pi][:, 0:TW], wgb[s][:, ch, :], hT[:, ch, c0:c0 + TW],
                                       start=(ch == 0), stop=(ch == DC - 1))
                    return ins
                P.pe(mm_g, reads=[("wgb", s)] + HT, writes=[("pA", pi)])
                P.act(lambda e, pi=pi: e.activation(sg[pi][:], pA[pi][:, 0:TW], AF.Silu),
                      reads=[("pA", pi)], writes=[("sg", pi)])
                P.dve(lambda e, cb=cb, c0=c0, pi=pi: e.tensor_tensor(
                    og[:, cb, c0:c0 + TW], og[:, cb, c0:c0 + TW], sg[pi][:], ALU.mult),
                    reads=[("og", cb), ("sg", pi)], writes=[("og", cb)])
        OG = [("og", ch) for ch in range(DC)]

        for cb in range(DC):
            s = cb % 2
            for br in range(2):
                load_w(wmb[br][s], wm[br, cb], ("wmb", br, s), DC)
                load_w(wbb[br][s], wb[br, cb], ("wbb", br, s), 8)
            for ti in range(ntile):
                c0 = ti * TW
                for br in range(2):
                    def mm_u(e, s=s, c0=c0, br=br):
                        for ch in range(8):
                            ins = e.matmul(pU[br][:, 0:TW], wbb[br][s][:, ch, :], og[:, br * 8 + ch, c0:c0 + TW],
                                           start=(ch == 0), stop=(ch == 7))
                        return ins
                    P.pe(mm_u, reads=[("wbb", br, s)] + OG, writes=[("pU", br)])

                    def mm_m(e, s=s, c0=c0, br=br):
                        for ch in range(DC):
                            ins = e.matmul(pM[br][:, 0:TW], wmb[br][s][:, ch, :], hT[:, ch, c0:c0 + TW],
                                           start=(ch == 0), stop=(ch == DC - 1))
                        return ins
                    P.pe(mm_m, reads=[("wmb", br, s)] + HT, writes=[("pM", br)])
                    P.act(lambda e, br=br: e.activation(sig[br][:], pM[br][:, 0:TW], AF.Sigmoid),
                          reads=[("pM", br)], writes=[("sig", br)])
                    P.dve(lambda e, br=br: e.tensor_tensor(t1[br][:], pU[br][:, 0:TW], sig[br][:], ALU.mult),
                          reads=[("pU", br), ("sig", br)], writes=[("t1", br)])
                P.dve(lambda e, cb=cb, c0=c0: e.tensor_tensor(mg[:, cb, c0:c0 + TW], t1[0][:], t1[1][:], ALU.add),
                      reads=[("t1", 0), ("t1", 1)], writes=[("mg", cb, ti)])
        MG = [("mg", cb, ti) for cb in range(DC) for ti in range(ntile)]

        tblocks = [(i * 128, 128) for i in range(NPBK)] + [(NPP, 32)]
        for cbk in range(4):
            s = cbk % 2
            for c2 in range(DC // 2):
                P.dma("pool", wob[s][:, 2 * c2:2 * c2 + 2, :], wo[cbk, :, 2 * c2:2 * c2 + 2, :],
                      writes=[("wob", s, c2)], semkey=("wob", s))
            WO = [("wob", s, c2) for c2 in range(DC // 2)]
            for bi, (r0, nr) in enumerate(tblocks):
                pi = bi % 2
                P.dma("sp", xres[pi][0:nr, :], xm[r0:r0 + nr, cbk * 512:(cbk + 1) * 512],
                      writes=[("xres", pi)], semkey=("xres", pi))

                def mm_y(e, s=s, r0=r0, nr=nr, pi=pi):
                    for ch in range(DC):
                        ins = e.matmul(pY[pi][0:nr, :], mg[:, ch, r0:r0 + nr], wob[s][:, ch, :],
                                       start=(ch == 0), stop=(ch == DC - 1))
                    return ins
                P.pe(mm_y, reads=WO + MG, writes=[("pY", pi)])
                P.dve(lambda e, nr=nr, pi=pi: e.tensor_tensor(yst[pi][0:nr, :], pY[pi][0:nr, :],
                                                              xres[pi][0:nr, :], ALU.add),
                      reads=[("pY", pi), ("xres", pi)], writes=[("yst", pi)])
                P.dma("sp", o_y[r0:r0 + nr, cbk * 512:(cbk + 1) * 512], yst[pi][0:nr, :],
                      reads=[("yst", pi)], semkey=("yst", pi))
        P.emit()


def _consts():
    p = np.arange(128)[:, None]
    c = np.arange(128)[None, :]
    cst = np.zeros((128, 4, 128), np.float32)
    cst[:, 0, :] = 1.0
    cst[:, 1, :] = (p <= c)
    cst[:, 2, :] = (p >= c)
    cst[:, 3, :] = (p < c)
    return cst


_WA_BLOCKS = [0, 1, 4, 5]


def kernel(x_prompt, x_sample, cache_sb_k, cache_sb_v, cache_fox_k, cache_fox_v, cache_fox_logf,
           norm_w, w_in, b_forget, q_norm_w, k_norm_w, w_branch_sb, w_branch_fox, w_out):
    f32 = np.float32
    x_prompt = np.asarray(x_prompt, f32)
    x_sample = np.asarray(x_sample, f32)
    T_P = x_prompt.shape[1]
    PAST = cache_sb_k.shape[2]
    NTOK = T_P + NSAMP
    NPB = PAST // 128
    NB = T_P // 128
    w_in = np.asarray(w_in, f32)[0]
    cst = _consts()
    nwl = np.ascontiguousarray(np.asarray(norm_w, f32)[0].reshape(DC, 128).T)

    x_all = np.concatenate([x_prompt[0], x_sample.reshape(NSAMP, D)], 0)
    xTp = np.ascontiguousarray(
        x_all[:T_P].reshape(T_P // 512, 512, DC, 128).transpose(0, 3, 2, 1)).reshape(T_P // 512, 128, DC * 512)
    xTs = np.ascontiguousarray(x_all[T_P:].reshape(NSAMP, DC, 128).transpose(2, 1, 0)).reshape(128, DC * NSAMP)
    csk = np.asarray(cache_sb_k, f32)[0]
    csv = np.asarray(cache_sb_v, f32)[0]
    cfk = np.asarray(cache_fox_k, f32)[0]
    cfv = np.asarray(cache_fox_v, f32)[0]
    clf = np.asarray(cache_fox_logf, f32)[0]
    in_maps = []
    for c in range(NCORES):
        cols = []
        for blk in _WA_BLOCKS:
            cols.append(w_in[:, blk * 1024 + c * 128: blk * 1024 + (c + 1) * 128])
        cols.append(w_in[:, 2 * 1024 + c * 128: 2 * 1024 + (c + 1) * 128])
        cols.append(w_in[:, 6 * 1024 + c * 128: 6 * 1024 + (c + 1) * 128])
        cols.append(w_in[:, 8 * 1024 + c: 8 * 1024 + c + 1])
        wa = np.concatenate(cols, 1)
        wa = np.ascontiguousarray(wa.reshape(DC, 128, 769).transpose(1, 0, 2))
        vec = np.stack([np.full(128, np.asarray(b_forget, f32)[0, c], f32),
                        np.asarray(q_norm_w, f32)[0], np.asarray(k_norm_w, f32)[0]], 1)
        in_maps.append({
            "xTp": xTp, "xTs": xTs, "wa": wa, "nw": nwl, "vec": np.ascontiguousarray(vec), "cst": cst,
            "cskT": np.ascontiguousarray(csk[:, :, c, :].transpose(0, 2, 1)),
            "cfkT": np.ascontiguousarray(cfk[:, :, c, :].transpose(0, 2, 1)),
            "csv": np.ascontiguousarray(
                csv[:, :, c, :].reshape(B_S, NPB, 128, 128).transpose(0, 2, 1, 3)).reshape(B_S, 128, NPB * 128),
            "cfv": np.ascontiguousarray(
                cfv[:, :, c, :].reshape(B_S, NPB, 128, 128).transpose(0, 2, 1, 3)).reshape(B_S, 128, NPB * 128),
            "clf": np.ascontiguousarray(clf[:, :, c].reshape(B_S, NPB, 128).transpose(2, 0, 1)),
        })
    nc = bass.Bass("TRN2", target_bir_lowering=False)
    o_og = nc.dram_tensor("o_og", [2, 128, NTOK // 2], F32, kind="ExternalOutput").ap()
    build_A(nc, T_P, PAST, lambda br: o_og[br])
    resA = run_bass_kernel_spmd(nc, in_maps, core_ids=list(range(NCORES))).results

    p_sb_k = np.zeros((1, 1, T_P, 8, 128), f32); s_sb_k = np.zeros((1, B_S, T_S, 8, 128), f32)
    p_sb_v = np.zeros_like(p_sb_k); s_sb_v = np.zeros_like(s_sb_k)
    p_fx_k = np.zeros_like(p_sb_k); s_fx_k = np.zeros_like(s_sb_k)
    p_fx_v = np.zeros_like(p_sb_k); s_fx_v = np.zeros_like(s_sb_k)
    p_lf = np.zeros((1, 1, T_P, 8), f32); s_lf = np.zeros((1, B_S, T_S, 8), f32)
    og_all = np.zeros((2, 8, 128, NTOK), np.uint16)
    for c in range(NCORES):
        r = resA[c]
        kT = np.asarray(r["o_kT"])
        v = np.asarray(r["o_v"])
        p_sb_k[0, 0, :, c, :] = kT[0, :, :T_P].T
        s_sb_k[0, :, :, c, :] = kT[0, :, T_P:].T.reshape(B_S, T_S, 128)
        p_fx_k[0, 0, :, c, :] = kT[1, :, :T_P].T
        s_fx_k[0, :, :, c, :] = kT[1, :, T_P:].T.reshape(B_S, T_S, 128)
        p_sb_v[0, 0, :, c, :] = v[:T_P, 0:128]
        s_sb_v[0, :, :, c, :] = v[T_P:, 0:128].reshape(B_S, T_S, 128)
        p_fx_v[0, 0, :, c, :] = v[:T_P, 128:256]
        s_fx_v[0, :, :, c, :] = v[T_P:, 128:256].reshape(B_S, T_S, 128)
        p_lf[0, 0, :, c] = np.asarray(r["o_lf"]).T.reshape(T_P)
        s_lf[0, :, :, c] = np.asarray(r["o_lfs"]).T
        og_all[:, c] = np.ascontiguousarray(np.asarray(r["o_og"])).view(np.uint16).reshape(2, 128, NTOK)

    NPP = T_P // NCORES
    NPC = NPP + 32
    wg = np.concatenate([w_in[:, 3 * 1024:4 * 1024], w_in[:, 7 * 1024:8 * 1024]], 1)
    wg = np.ascontiguousarray(wg.reshape(DC, 128, DC, 128).transpose(2, 1, 0, 3))
    m0 = 8 * 1024 + 8
    wm = np.stack([w_in[:, m0:m0 + D], w_in[:, m0 + D:m0 + 2 * D]], 0)
    wm = np.ascontiguousarray(wm.reshape(2, DC, 128, DC, 128).transpose(0, 3, 2, 1, 4))
    wb = np.stack([np.asarray(w_branch_sb, f32)[0], np.asarray(w_branch_fox, f32)[0]], 0)
    wb = np.ascontiguousarray(wb.reshape(2, 8, 128, DC, 128).transpose(0, 3, 2, 1, 4))
    wo = np.asarray(w_out, f32)[0]
    wo = np.ascontiguousarray(wo.reshape(DC, 128, 4, 512).transpose(2, 1, 0, 3))
    in_maps = []
    for c in range(NCORES):
        tok = np.concatenate([np.arange(c * NPP, (c + 1) * NPP), T_P + np.arange(c * 32, (c + 1) * 32)])
        xm = np.ascontiguousarray(x_all[tok])
        xmT = np.ascontiguousarray(xm.T.reshape(DC, 128, NPC).transpose(1, 0, 2))
        ogc = np.ascontiguousarray(og_all[:, :, :, tok].reshape(DC, 128, NPC)).view(np.float32)
        in_maps.append({"xmT": xmT, "xm": xm, "nw": nwl, "cst": cst, "wg": wg, "wm": wm, "wb": wb,
                        "wo": wo, "ogin": np.ascontiguousarray(ogc)})
    nc2 = bass.Bass("TRN2", target_bir_lowering=False)
    ogin = nc2.dram_tensor("ogin", [DC, 128, NPC // 2], F32, kind="ExternalInput").ap()
    build_C(nc2, NPC, lambda ch: ogin[ch])
    resC = run_bass_kernel_spmd(nc2, in_maps, core_ids=list(range(NCORES))).results
    y_p = np.zeros((1, T_P, D), f32)
    y_s = np.zeros((NSAMP, D), f32)
    for c in range(NCORES):
        y = np.asarray(resC[c]["o_y"])
        y_p[0, c * NPP:(c + 1) * NPP] = y[:NPP]
        y_s[c * 32:(c + 1) * 32] = y[NPP:]
    y_s = y_s.reshape(B_S, T_S, D)
    return (y_p, y_s, p_sb_k, p_sb_v, p_fx_k, p_fx_v, p_lf, s_sb_k, s_sb_v, s_fx_k, s_fx_v, s_lf)
```
